# Optimizing a Trainium2 kernel written in Bass

```python
import math
import jax
import jax.numpy as jnp
from jax import lax
import numpy as np

D_MODEL = 2048
BATCH = 16
SEQ = 256
DEPTH = 4
DEC_BATCH = 4
DEC_SEQ = 4096
PAST_LEN = 256

F32 = jnp.float32
GRID_W = 64
N_MIXERS = 4
D_FF = 5632
NORM_EPS = 1e-6
S5_WIDTH = D_MODEL // 2
S5_GROUP = 16
S5_GROUPS = S5_WIDTH // S5_GROUP
S5_STATE = 64
RG_WIDTH = D_MODEL
RG_BLOCKS = 16
RG_BLOCK = RG_WIDTH // RG_BLOCKS
RG_CONV = 4
RG_C = 8.0
DA_HEADS = 8
DA_DK = D_MODEL // (2 * DA_HEADS)
DA_DV = 2 * DA_DK
Q_BLOCK = 128
ROPE_BASE = 10000.0
ML_WIDTH = 2 * D_MODEL
ML_HEADS = 8
ML_DH = ML_WIDTH // ML_HEADS
ML_CONV = 4
ML_CHUNK = 64

kernel_name = 'hybrid_diffusion_s5_rglru_diffattn_mlstm_step'


def _rms(x, g):
    x32 = x.astype(F32)
    y = x32 * lax.rsqrt(jnp.mean(x32 * x32, axis=-1, keepdims=True) + NORM_EPS)
    return (y * g.astype(F32)).astype(x.dtype)


def _modulate(x, g, shift, scale):
    return _rms(x, g) * (1.0 + scale) + shift


def _swiglu(h, w1, w3, w2):
    return (jax.nn.silu(h @ w1) * (h @ w3)) @ w2


def _ffn_half(x, g, shift, scale, gate, w1, w3, w2):
    return x + 0.5 * gate * _swiglu(_modulate(x, g, shift, scale), w1, w3, w2)


def _conv_centred(x, w, b):
    width = w.shape[0]
    pl = (width - 1) // 2
    L = x.shape[1]
    xp = jnp.pad(x, ((0, 0), (pl, width - 1 - pl), (0, 0)))
    y = b
    for j in range(width):
        y = y + xp[:, j:j + L] * w[j]
    return y


def _real_scan(a, b):
    def comb(e1, e2):
        a1, b1 = e1
        a2, b2 = e2
        return a1 * a2, a2 * b1 + b2
    return lax.associative_scan(comb, (a, b), axis=1)[1]


def _complex_scan(ar, ai, br, bi):
    def comb(e1, e2):
        a1r, a1i, b1r, b1i = e1
        a2r, a2i, b2r, b2i = e2
        return (a1r * a2r - a1i * a2i, a1r * a2i + a1i * a2r,
                a2r * b1r - a2i * b1i + b2r, a2r * b1i + a2i * b1r + b2i)
    _, _, hr, hi = lax.associative_scan(comb, (ar, ai, br, bi), axis=1)
    return hr, hi


def _s5_direction(u, h0r, h0i, a_re, a_im, log_dt, b_re, b_im, c_re, c_im, reverse):
    a_re = a_re.astype(F32)
    a_im = a_im.astype(F32)
    b_re = b_re.astype(F32)
    b_im = b_im.astype(F32)
    dt = jnp.exp(log_dt.astype(F32))[:, None]
    mag = jnp.exp(a_re * dt)
    lr = mag * jnp.cos(a_im * dt)
    li = mag * jnp.sin(a_im * dt)
    den = a_re * a_re + a_im * a_im
    cr = ((lr - 1.0) * a_re + li * a_im) / den
    ci = (li * a_re - (lr - 1.0) * a_im) / den
    bbr = cr[..., None] * b_re - ci[..., None] * b_im
    bbi = cr[..., None] * b_im + ci[..., None] * b_re
    if reverse:
        u = jnp.flip(u, axis=1)
    bur = jnp.einsum('gpc,blgc->blgp', bbr, u)
    bui = jnp.einsum('gpc,blgc->blgp', bbi, u)
    bur = bur.at[:, 0].add(lr * h0r - li * h0i)
    bui = bui.at[:, 0].add(lr * h0i + li * h0r)
    shape = (1, u.shape[1]) + lr.shape
    hr, hi = _complex_scan(jnp.broadcast_to(lr, shape), jnp.broadcast_to(li, shape), bur, bui)
    y = (jnp.einsum('gcp,blgp->blgc', c_re.astype(F32), hr)
         - jnp.einsum('gcp,blgp->blgc', c_im.astype(F32), hi))
    if reverse:
        y = jnp.flip(y, axis=1)
    return y, hr[:, -1], hi[:, -1]


def _s5_mixer(h, state, p):
    (w_in, a_re, a_im, log_dt, b_re, b_im, c_re, c_im, d_skip, w_glu, b_glu, w_out) = p
    B, L, _ = h.shape
    u = (h @ w_in).astype(F32)
    ug = u.reshape(B, L, S5_GROUPS, S5_GROUP)
    y = d_skip.astype(F32) * u
    finals = []
    for d in range(2):
        yd, fr, fi = _s5_direction(ug, state[:, d, 0].astype(F32), state[:, d, 1].astype(F32),
                                   a_re[d], a_im[d], log_dt[d], b_re[d], b_im[d],
                                   c_re[d], c_im[d], reverse=(d == 1))
        y = y + yd.reshape(B, L, S5_WIDTH)
        finals.append(jnp.stack([fr, fi], axis=1))
    z = jax.nn.gelu(y).astype(h.dtype)
    z = z * jax.nn.sigmoid(z @ w_glu + b_glu)
    out = z @ w_out
    return out, jnp.stack(finals, axis=1).astype(h.dtype)


def _rglru_direction(xc, h0, wa, ba, wx, bx, lam, reverse):
    B, L, _ = xc.shape
    if reverse:
        xc = jnp.flip(xc, axis=1)
    xb = xc.reshape(B, L, RG_BLOCKS, RG_BLOCK)
    r = jax.nn.sigmoid(jnp.einsum('blnc,ncd->blnd', xb, wa).reshape(B, L, RG_WIDTH) + ba)
    ig = jax.nn.sigmoid(jnp.einsum('blnc,ncd->blnd', xb, wx).reshape(B, L, RG_WIDTH) + bx)
    log_a = -RG_C * r * jax.nn.softplus(-lam)
    a = jnp.exp(log_a)
    bv = jnp.sqrt(-jnp.expm1(2.0 * log_a)) * (ig * xc)
    bv = bv.at[:, 0].add(a[:, 0] * h0)
    hs = _real_scan(a, bv)
    final = hs[:, -1]
    if reverse:
        hs = jnp.flip(hs, axis=1)
    return hs, final


def _rglru_mixer(h, state, p):
    (w_in, w_gate, conv_w, conv_b, wa, ba, wx, bx, lam, w_out) = p
    xc = _conv_centred(h @ w_in, conv_w, conv_b).astype(F32)
    gate = jax.nn.gelu(h @ w_gate)
    y = jnp.zeros_like(xc)
    finals = []
    for d in range(2):
        hs, f = _rglru_direction(xc, state[:, d].astype(F32), wa[d].astype(F32), ba[d].astype(F32),
                                 wx[d].astype(F32), bx[d].astype(F32), lam[d].astype(F32),
                                 reverse=(d == 1))
        y = y + hs
        finals.append(f)
    out = (y.astype(h.dtype) * gate) @ w_out
    return out, jnp.stack(finals, axis=1).astype(h.dtype)


def _rope_1d(x, pos):
    n = x.shape[-1]
    inv = ROPE_BASE ** (-jnp.arange(0, n, 2, dtype=F32) / n)
    ang = pos[:, None] * inv
    cos = jnp.cos(ang)[:, None, None, :]
    sin = jnp.sin(ang)[:, None, None, :]
    x1, x2 = jnp.split(x.astype(F32), 2, axis=-1)
    return jnp.concatenate([x1 * cos - x2 * sin, x2 * cos + x1 * sin], axis=-1)


def _axial_rope(x):
    L = x.shape[1]
    rows = L // GRID_W
    row = jnp.repeat(jnp.arange(rows, dtype=F32), GRID_W)
    col = jnp.tile(jnp.arange(GRID_W, dtype=F32), rows)
    half = x.shape[-1] // 2
    y = jnp.concatenate([_rope_1d(x[..., :half], row), _rope_1d(x[..., half:], col)], axis=-1)
    return y.astype(x.dtype)


def _attend_blocks(q, k, v):
    B, Lq, H, _, dk = q.shape
    nb = Lq // Q_BLOCK
    scale = 1.0 / math.sqrt(dk)
    qb = jnp.moveaxis(q.reshape(B, nb, Q_BLOCK, H, 2, dk), 1, 0)

    def one(qblk):
        s = jnp.einsum('bqhcd,bkhcd->bhcqk', qblk, k).astype(F32) * scale
        pr = jax.nn.softmax(s, axis=-1).astype(v.dtype)
        return jnp.einsum('bhcqk,bkhe->bqhce', pr, v)

    o = lax.map(one, qb)
    return jnp.moveaxis(o, 0, 1).reshape(B, Lq, H, 2, v.shape[-1])


def _dattn_project(h, wq, wk, wv):
    B, L, _ = h.shape
    q = (h @ wq).reshape(B, L, DA_HEADS, 2, DA_DK)
    k = (h @ wk).reshape(B, L, DA_HEADS, 2, DA_DK)
    v = (h @ wv).reshape(B, L, DA_HEADS, DA_DV)
    return q, k, v


def _dattn_output(o, wo, lam_p, subln_g, lam_init):
    lp = lam_p.astype(F32)
    lam = jnp.exp(jnp.sum(lp[0] * lp[1])) - jnp.exp(jnp.sum(lp[2] * lp[3])) + lam_init
    o32 = o.astype(F32)
    d = o32[..., 0, :] - lam * o32[..., 1, :]
    d = d * lax.rsqrt(jnp.mean(d * d, axis=-1, keepdims=True) + NORM_EPS)
    d = d * subln_g.astype(F32) * (1.0 - lam_init)
    B, L = d.shape[:2]
    return d.reshape(B, L, DA_HEADS * DA_DV).astype(o.dtype) @ wo


def _mlstm_chunk_step(carry, xs):
    C, n, m = carry
    q, k, v, ig, lf = xs
    T = q.shape[2]
    b = jnp.cumsum(lf, axis=-1)
    causal = jnp.tril(jnp.ones((T, T), dtype=bool))
    dlog = jnp.where(causal, b[..., :, None] - b[..., None, :] + ig[..., None, :], -jnp.inf)
    inter = b + m[..., None]
    m_t = jnp.maximum(inter, jnp.max(dlog, axis=-1))
    w_intra = jnp.exp(dlog - m_t[..., None])
    w_inter = jnp.exp(inter - m_t)
    s = jnp.einsum('bhtd,bhsd->bhts', q, k) * w_intra
    num = (jnp.einsum('bhts,bhsd->bhtd', s, v)
           + w_inter[..., None] * jnp.einsum('bhvk,bhtk->bhtv', C, q))
    den = jnp.sum(s, axis=-1) + w_inter * jnp.einsum('bhk,bhtk->bht', n, q)
    h = num / jnp.maximum(jnp.abs(den), jnp.exp(-m_t))[..., None]
    m_new = m_t[..., -1]
    w_old = jnp.exp(b[..., -1] + m - m_new)
    w_in = jnp.exp(b[..., -1:] - b + ig - m_new[..., None])
    C_new = w_old[..., None, None] * C + jnp.einsum('bhsv,bhsk->bhvk', v * w_in[..., None], k)
    n_new = w_old[..., None] * n + jnp.einsum('bhs,bhsk->bhk', w_in, k)
    return (C_new, n_new, m_new), h


def _to_chunks(t):
    B, L = t.shape[:2]
    t = t.reshape((B, L // ML_CHUNK, ML_CHUNK) + t.shape[2:])
    return jnp.swapaxes(jnp.moveaxis(t, 1, 0), 2, 3)


def _mlstm_direction(q, k, v, ig, fg, C0, n0, m0, reverse):
    B, L, H, DH = q.shape
    if reverse:
        q, k, v, ig, fg = (jnp.flip(t, axis=1) for t in (q, k, v, ig, fg))
    lf = jax.nn.log_sigmoid(fg)
    xs = tuple(_to_chunks(t) for t in (q, k, v, ig, lf))
    (C, n, m), hc = lax.scan(_mlstm_chunk_step, (C0, n0, m0), xs)
    h = jnp.moveaxis(jnp.swapaxes(hc, 2, 3), 0, 1).reshape(B, L, H, DH)
    if reverse:
        h = jnp.flip(h, axis=1)
    return h, C, n, m


def _mlstm_mixer(h, C0, n0, m0, p):
    (w_up, conv_w, conv_b, wq, wk, wv, w_gate, b_gate, w_o, b_o, gn_g, skip, w_down) = p
    B, L, _ = h.shape
    xi = h @ w_up
    xc = jax.nn.silu(_conv_centred(xi, conv_w, conv_b))
    q = jnp.einsum('blhd,hde->blhe', xc.reshape(B, L, ML_HEADS, ML_DH), wq)
    k = jnp.einsum('blhd,hde->blhe', xc.reshape(B, L, ML_HEADS, ML_DH), wk) / math.sqrt(ML_DH)
    v = jnp.einsum('blhd,hde->blhe', xi.reshape(B, L, ML_HEADS, ML_DH), wv)
    g = (q.reshape(B, L, ML_WIDTH) @ w_gate[0] + k.reshape(B, L, ML_WIDTH) @ w_gate[1]
         + v.reshape(B, L, ML_WIDTH) @ w_gate[2] + b_gate)
    g = g.astype(F32).reshape(B, L, 2, 2, ML_HEADS)
    q32, k32, v32 = q.astype(F32), k.astype(F32), v.astype(F32)
    cell = jnp.zeros_like(q32)
    Cs, ns, ms = [], [], []
    for d in range(2):
        hd, Cd, nd, md = _mlstm_direction(q32, k32, v32, g[:, :, d, 0], g[:, :, d, 1],
                                          C0[:, d].astype(F32), n0[:, d].astype(F32),
                                          m0[:, d].astype(F32), reverse=(d == 1))
        cell = cell + hd
        Cs.append(Cd)
        ns.append(nd)
        ms.append(md)
    o = jax.nn.sigmoid(h @ w_o + b_o).astype(F32).reshape(B, L, ML_HEADS, ML_DH)
    hc = o * cell
    hn = hc * lax.rsqrt(jnp.mean(hc * hc, axis=-1, keepdims=True) + NORM_EPS)
    y = hn.reshape(B, L, ML_WIDTH).astype(h.dtype) * gn_g + skip * xc
    out = y @ w_down
    return (out, jnp.stack(Cs, axis=1).astype(h.dtype), jnp.stack(ns, axis=1).astype(h.dtype),
            jnp.stack(ms, axis=1).astype(h.dtype))


def setup_inputs(seed: int = 0) -> dict:
    key = jax.random.key(seed)
    ks = iter(jax.random.split(key, 72))

    def nrm(shape, s=1.0):
        return jax.random.normal(next(ks), shape, F32) * s

    D = D_MODEL
    NL = DEPTH
    G, P = S5_GROUPS, S5_STATE
    inp = {}
    inp['x_prompt'] = nrm((BATCH, SEQ, D))
    inp['x_sample'] = nrm((DEC_BATCH, DEC_SEQ, D))
    inp['state_s5'] = nrm((DEC_BATCH, 2, 2, G, P), 0.5)
    inp['state_rglru'] = nrm((DEC_BATCH, 2, RG_WIDTH), 0.5)
    inp['cache_dattn_k'] = nrm((DEC_BATCH, PAST_LEN, DA_HEADS, 2, DA_DK))
    inp['cache_dattn_v'] = nrm((DEC_BATCH, PAST_LEN, DA_HEADS, DA_DV))
    inp['state_mlstm_C'] = nrm((DEC_BATCH, 2, ML_HEADS, ML_DH, ML_DH), 0.05)
    inp['state_mlstm_n'] = nrm((DEC_BATCH, 2, ML_HEADS, ML_DH), 0.05)
    inp['state_mlstm_m'] = nrm((DEC_BATCH, 2, ML_HEADS), 0.5)
    inp['c'] = nrm((DEC_BATCH, D))
    inp['c_ctx'] = nrm((D,))
    inp['ada_w'] = nrm((NL, D, 9 * D), D ** -0.5)
    inp['ada_b'] = nrm((NL, 9 * D), 0.02)
    inp['norm_g'] = 1.0 + nrm((NL, 3, D), 0.02)
    inp['ffn1_w1'] = nrm((NL, D, D_FF), D ** -0.5)
    inp['ffn1_w3'] = nrm((NL, D, D_FF), D ** -0.5)
    inp['ffn1_w2'] = nrm((NL, D_FF, D), D_FF ** -0.5)
    inp['ffn2_w1'] = nrm((NL, D, D_FF), D ** -0.5)
    inp['ffn2_w3'] = nrm((NL, D, D_FF), D ** -0.5)
    inp['ffn2_w2'] = nrm((NL, D_FF, D), D_FF ** -0.5)
    inp['final_norm_g'] = 1.0 + nrm((D,), 0.02)
    inp['s5_w_in'] = nrm((D, S5_WIDTH), D ** -0.5)
    inp['s5_a_re'] = -0.5 + nrm((2, G, P), 0.01)
    inp['s5_a_im'] = math.pi * jnp.arange(P, dtype=F32) + nrm((2, G, P), 0.01)
    inp['s5_log_dt'] = jax.random.uniform(next(ks), (2, G), F32, math.log(1e-3), math.log(1e-1))
    inp['s5_b_re'] = nrm((2, G, P, S5_GROUP), (2 * S5_GROUP) ** -0.5)
    inp['s5_b_im'] = nrm((2, G, P, S5_GROUP), (2 * S5_GROUP) ** -0.5)
    inp['s5_c_re'] = nrm((2, G, S5_GROUP, P), 0.5)
    inp['s5_c_im'] = nrm((2, G, S5_GROUP, P), 0.5)
    inp['s5_d'] = nrm((S5_WIDTH,))
    inp['s5_w_glu'] = nrm((S5_WIDTH, S5_WIDTH), S5_WIDTH ** -0.5)
    inp['s5_b_glu'] = nrm((S5_WIDTH,), 0.02)
    inp['s5_w_out'] = nrm((S5_WIDTH, D), S5_WIDTH ** -0.5)
    inp['rg_w_in'] = nrm((D, RG_WIDTH), D ** -0.5)
    inp['rg_w_gate'] = nrm((D, RG_WIDTH), D ** -0.5)
    inp['rg_conv_w'] = nrm((RG_CONV, RG_WIDTH), RG_CONV ** -0.5)
    inp['rg_conv_b'] = nrm((RG_WIDTH,), 0.02)
    inp['rg_wa'] = nrm((2, RG_BLOCKS, RG_BLOCK, RG_BLOCK), RG_BLOCK ** -0.5)
    inp['rg_ba'] = nrm((2, RG_WIDTH), 0.02)
    inp['rg_wx'] = nrm((2, RG_BLOCKS, RG_BLOCK, RG_BLOCK), RG_BLOCK ** -0.5)
    inp['rg_bx'] = nrm((2, RG_WIDTH), 0.02)
    a_pow = jax.random.uniform(next(ks), (2, RG_WIDTH), F32, 0.9, 0.999)
    s_lam = a_pow ** (1.0 / RG_C)
    inp['rg_lam'] = jnp.log(s_lam) - jnp.log1p(-s_lam)
    inp['rg_w_out'] = nrm((RG_WIDTH, D), RG_WIDTH ** -0.5)
    inp['da_wq'] = nrm((D, 2 * DA_HEADS * DA_DK), D ** -0.5)
    inp['da_wk'] = nrm((D, 2 * DA_HEADS * DA_DK), D ** -0.5)
    inp['da_wv'] = nrm((D, DA_HEADS * DA_DV), D ** -0.5)
    inp['da_wo'] = nrm((DA_HEADS * DA_DV, D), (DA_HEADS * DA_DV) ** -0.5)
    inp['da_lam'] = nrm((4, DA_DK), 0.1)
    inp['da_subln_g'] = 1.0 + nrm((DA_DV,), 0.02)
    inp['ml_w_up'] = nrm((D, ML_WIDTH), D ** -0.5)
    inp['ml_conv_w'] = nrm((ML_CONV, ML_WIDTH), ML_CONV ** -0.5)
    inp['ml_conv_b'] = nrm((ML_WIDTH,), 0.02)
    inp['ml_wq'] = nrm((ML_HEADS, ML_DH, ML_DH), ML_DH ** -0.5)
    inp['ml_wk'] = nrm((ML_HEADS, ML_DH, ML_DH), ML_DH ** -0.5)
    inp['ml_wv'] = nrm((ML_HEADS, ML_DH, ML_DH), ML_DH ** -0.5)
    inp['ml_w_gate'] = nrm((3, ML_WIDTH, 4 * ML_HEADS), 0.5 * (3 * ML_WIDTH) ** -0.5)
    ig_b = nrm((2, 1, ML_HEADS), 0.1)
    fg_b = jnp.linspace(3.0, 6.0, ML_HEADS, dtype=F32)[None, None, :] + nrm((2, 1, ML_HEADS), 0.1)
    inp['ml_b_gate'] = jnp.concatenate([ig_b, fg_b], axis=1).reshape(4 * ML_HEADS)
    inp['ml_w_o'] = nrm((D, ML_WIDTH), D ** -0.5)
    inp['ml_b_o'] = nrm((ML_WIDTH,), 0.02)
    inp['ml_gn'] = 1.0 + nrm((ML_WIDTH,), 0.02)
    inp['ml_skip'] = 1.0 + nrm((ML_WIDTH,), 0.02)
    inp['ml_w_down'] = nrm((ML_WIDTH, D), ML_WIDTH ** -0.5)
    return inp


def reference(x_prompt, x_sample, state_s5, state_rglru, cache_dattn_k, cache_dattn_v,
              state_mlstm_C, state_mlstm_n, state_mlstm_m, c, c_ctx,
              ada_w, ada_b, norm_g, ffn1_w1, ffn1_w3, ffn1_w2, ffn2_w1, ffn2_w3, ffn2_w2,
              final_norm_g,
              s5_w_in, s5_a_re, s5_a_im, s5_log_dt, s5_b_re, s5_b_im, s5_c_re, s5_c_im,
              s5_d, s5_w_glu, s5_b_glu, s5_w_out,
              rg_w_in, rg_w_gate, rg_conv_w, rg_conv_b, rg_wa, rg_ba, rg_wx, rg_bx, rg_lam,
              rg_w_out,
              da_wq, da_wk, da_wv, da_wo, da_lam, da_subln_g,
              ml_w_up, ml_conv_w, ml_conv_b, ml_wq, ml_wk, ml_wv, ml_w_gate, ml_b_gate,
              ml_w_o, ml_b_o, ml_gn, ml_skip, ml_w_down):
    s5_p = (s5_w_in, s5_a_re, s5_a_im, s5_log_dt, s5_b_re, s5_b_im, s5_c_re, s5_c_im,
            s5_d, s5_w_glu, s5_b_glu, s5_w_out)
    rg_p = (rg_w_in, rg_w_gate, rg_conv_w, rg_conv_b, rg_wa, rg_ba, rg_wx, rg_bx, rg_lam, rg_w_out)
    ml_p = (ml_w_up, ml_conv_w, ml_conv_b, ml_wq, ml_wk, ml_wv, ml_w_gate, ml_b_gate,
            ml_w_o, ml_b_o, ml_gn, ml_skip, ml_w_down)
    ctx, lat = x_prompt, x_sample
    Bc = ctx.shape[0]
    sc = jax.nn.silu(c_ctx)[None, :]
    sl = jax.nn.silu(c)
    for i in range(DEPTH):
        mc = jnp.split((sc @ ada_w[i] + ada_b[i])[:, None, :], 9, axis=-1)
        mlt = jnp.split((sl @ ada_w[i] + ada_b[i])[:, None, :], 9, axis=-1)
        ctx = _ffn_half(ctx, norm_g[i, 0], mc[0], mc[1], mc[2], ffn1_w1[i], ffn1_w3[i], ffn1_w2[i])
        lat = _ffn_half(lat, norm_g[i, 0], mlt[0], mlt[1], mlt[2], ffn1_w1[i], ffn1_w3[i], ffn1_w2[i])
        hc = _modulate(ctx, norm_g[i, 1], mc[3], mc[4])
        hl = _modulate(lat, norm_g[i, 1], mlt[3], mlt[4])
        kind = i % N_MIXERS
        if kind == 0:
            zero_s5 = jnp.zeros((Bc, 2, 2, S5_GROUPS, S5_STATE), hc.dtype)
            oc, new_s5 = _s5_mixer(hc, zero_s5, s5_p)
            ol, _ = _s5_mixer(hl, state_s5, s5_p)
        elif kind == 1:
            zero_rg = jnp.zeros((Bc, 2, RG_WIDTH), hc.dtype)
            oc, new_rg = _rglru_mixer(hc, zero_rg, rg_p)
            ol, _ = _rglru_mixer(hl, state_rglru, rg_p)
        elif kind == 2:
            lam_init = 0.8 - 0.6 * math.exp(-0.3 * i)
            qc, kc, vc = _dattn_project(hc, da_wq, da_wk, da_wv)
            oc = _dattn_output(_attend_blocks(qc, kc, vc), da_wo, da_lam, da_subln_g, lam_init)
            new_k, new_v = kc, vc
            ql, kl, vl = _dattn_project(hl, da_wq, da_wk, da_wv)
            ql = _axial_rope(ql)
            kl = _axial_rope(kl)
            k_all = jnp.concatenate([cache_dattn_k.astype(kl.dtype), kl], axis=1)
            v_all = jnp.concatenate([cache_dattn_v.astype(vl.dtype), vl], axis=1)
            ol = _dattn_output(_attend_blocks(ql, k_all, v_all), da_wo, da_lam, da_subln_g, lam_init)
        else:
            C0 = jnp.zeros((Bc, 2, ML_HEADS, ML_DH, ML_DH), hc.dtype)
            n0 = jnp.zeros((Bc, 2, ML_HEADS, ML_DH), hc.dtype)
            m0 = jnp.zeros((Bc, 2, ML_HEADS), hc.dtype)
            oc, new_C, new_n, new_m = _mlstm_mixer(hc, C0, n0, m0, ml_p)
            ol, _, _, _ = _mlstm_mixer(hl, state_mlstm_C, state_mlstm_n, state_mlstm_m, ml_p)
        ctx = ctx + mc[5] * oc
        lat = lat + mlt[5] * ol
        ctx = _ffn_half(ctx, norm_g[i, 2], mc[6], mc[7], mc[8], ffn2_w1[i], ffn2_w3[i], ffn2_w2[i])
        lat = _ffn_half(lat, norm_g[i, 2], mlt[6], mlt[7], mlt[8], ffn2_w1[i], ffn2_w3[i], ffn2_w2[i])
    y_prompt = _rms(ctx, final_norm_g)
    y_sample = _rms(lat, final_norm_g)
    return (y_prompt, y_sample, new_s5, new_rg, new_k, new_v, new_C, new_n, new_m)
```

```python
import numpy as np
import concourse.bass as bass
import concourse.mybir as mybir
from concourse.bass_utils import run_bass_kernel_spmd
from contextlib import ExitStack

F32 = mybir.dt.float32
BF16 = mybir.dt.bfloat16
I32 = mybir.dt.int32
AF = mybir.ActivationFunctionType
ALU = mybir.AluOpType
AX = mybir.AxisListType
AP = bass.AP

D = 2048
KC = 16
DFF = 5632
FC = 44
TT = 512
EPS = 1e-6
EPOCH = 12000
NDMASEM = 12


class Op:
    __slots__ = ("eng", "dma", "sem", "val", "sig")


class Prog:
    def __init__(self, nc, es):
        self.nc = nc
        self.es = es
        self.engs = {"pe": nc.tensor, "act": nc.scalar, "dve": nc.vector,
                     "pool": nc.gpsimd, "sp": nc.sync}
        self.count = {e: 0 for e in self.engs}
        self.nops = {e: 0 for e in self.engs}
        self.csems = {e: [] for e in self.engs}
        self.known = {e: {} for e in self.engs}
        self.dsems = {}
        self.dnext = {e: 0 for e in self.engs}
        self.last_w = {}
        self.readers = {}
        self.pending = {e: [] for e in self.engs}
        self.nwaits = 0

    def _csem(self, eng, epoch):
        lst = self.csems[eng]
        while len(lst) <= epoch:
            lst.append(self.es.enter_context(self.nc.semaphore(f"c_{eng}_{len(lst)}")))
        return lst[epoch]

    def _dsem(self, eng):
        if eng not in self.dsems:
            self.dsems[eng] = [[self.es.enter_context(self.nc.semaphore(f"d_{eng}_{i}")), 0]
                               for i in range(NDMASEM)]
        i = self.dnext[eng] % NDMASEM
        self.dnext[eng] += 1
        return self.dsems[eng][i]

    def _wait(self, eng, sem, val):
        k = self.known[eng]
        key = id(sem)
        if k.get(key, -1) >= val:
            return
        k[key] = val
        self.engs[eng].wait_ge(sem, val)
        self.nwaits += 1

    def op(self, eng, fn, reads=(), writes=(), dma=False, sig=True):
        o = Op()
        o.eng = eng
        o.dma = dma
        o.sig = sig or dma
        o.sem = None
        o.val = None
        deps = {}
        for r in reads:
            d = self.last_w.get(r)
            if d is not None:
                deps[id(d)] = d
        for w in writes:
            d = self.last_w.get(w)
            if d is not None:
                deps[id(d)] = d
            for d in self.readers.get(w, ()):
                deps[id(d)] = d
        for d in deps.values():
            if d.dma or dma or d.eng != eng or eng != "pe":
                if d.sem is None:
                    raise RuntimeError("dependency on op that never signals")
                self._wait(eng, d.sem, d.val)
        if dma:
            slot = self._dsem(eng)
            if slot[1] > 0:
                self._wait(eng, slot[0], slot[1])
            slot[1] += 16
            o.sem, o.val = slot[0], slot[1]
            fn(self.engs[eng]).then_inc(o.sem, 16)
        else:
            ins = fn(self.engs[eng])
            if o.sig:
                self.count[eng] += 1
                c = self.count[eng]
                ep, v = (c - 1) // EPOCH, (c - 1) % EPOCH + 1
                o.sem, o.val = self._csem(eng, ep), v
                ins.then_inc(o.sem, 1)
                for p in self.pending[eng]:
                    p.sem, p.val = o.sem, o.val
                self.pending[eng] = []
            else:
                self.pending[eng].append(o)
        self.nops[eng] += 1
        for r in reads:
            self.readers.setdefault(r, []).append(o)
        for w in writes:
            self.last_w[w] = o
            self.readers[w] = []
        return o

    def barrier(self):
        for e in self.engs:
            assert not self.pending[e]
        for e in self.engs:
            for e2 in self.engs:
                c = self.count[e2]
                if c > 0:
                    ep, v = (c - 1) // EPOCH, (c - 1) % EPOCH + 1
                    self._wait(e, self.csems[e2][ep], v)
            for slots in self.dsems.values():
                for sem, total in slots:
                    if total > 0:
                        self._wait(e, sem, total)
        self.last_w = {}
        self.readers = {}

    def finish(self):
        self.barrier()


class Ctx:
    pass


def build_nc(cfg):
    NTS = cfg.get("ns_tiles", 8)
    NTP = cfg.get("np_tiles", 1)
    NT = NTS + NTP
    NTOK = NT * TT
    LAYERS = cfg.get("layers", 4)
    nc = bass.Bass("TRN2", target_bir_lowering=False)
    dr = lambda name, shape, dt=F32, kind="ExternalInput": nc.dram_tensor(name, shape, dt, kind=kind).ap()
    x_in = dr("x_in", [NTOK, D])
    cvec = dr("cvec", [2, D])
    ada_w = dr("ada_w", [4, D, 9 * D])
    ada_b = dr("ada_b", [4, 9 * D])
    norm_g = dr("norm_g", [4, 3, D])
    fw = {}
    for f in ("ffn1", "ffn2"):
        fw[f] = (dr(f + "_w1", [4, D, DFF]), dr(f + "_w3", [4, D, DFF]), dr(f + "_w2", [4, DFF, D]))
    final_g = dr("final_norm_g", [D])
    y_out = dr("y_out", [NTOK, D], kind="ExternalOutput")
    xT = dr("xT_scr", [D, NTOK], kind="Internal")

    es = ExitStack()
    with es:
        P = Prog(nc, es)
        sbuf = lambda es_, name, shape, dt: es_.enter_context(nc.sbuf_tensor(name, shape, dt))
        ident = sbuf(es, "ident", [128, 128], F32)
        ones = sbuf(es, "ones", [128, 128], F32)
        modc = sbuf(es, "modc", [128, 4 * 2 * 9 * KC], F32)
        gcol = sbuf(es, "gcol", [128, 4 * 3 * KC], F32)
        fgcol = sbuf(es, "fgcol", [128, KC], F32)
        GS = sbuf(es, "GS", [128, 2 * 2 * 3 * KC], F32)
        banks = [es.enter_context(nc.psum_tensor(f"bank{i}", [128, 512], F32)) for i in range(8)]
        BK = [("bank", i) for i in range(8)]

        def mcol(l, c, v, k):
            i = ((l * 2 + c) * 9 + v) * KC + k
            return modc[:, i:i + 1]

        P.op("dve", lambda e: e.memset(ident[:], 0.0), writes=["ident"])
        P.op("pool", lambda e: e.affine_select(out=ident[:], in_=ident[:], pattern=[[-1, 128]],
                                               compare_op=ALU.not_equal, fill=1.0, base=0,
                                               channel_multiplier=1), reads=["ident"], writes=["ident"])
        P.op("dve", lambda e: e.memset(ones[:], 1.0), writes=["ones"])
        for lj in range(12):
            P.op("sp", lambda e, lj=lj: e.dma_start(out=gcol[:, lj * KC:(lj + 1) * KC],
                                                    in_=norm_g[lj // 3, lj % 3].rearrange("(k p) -> p k", p=128),
                                                    allow_slow_non_contiguous=True), writes=["gcol"], dma=True)
        P.op("sp", lambda e: e.dma_start(out=fgcol[:], in_=final_g.rearrange("(k p) -> p k", p=128),
                                         allow_slow_non_contiguous=True), writes=["fgcol"], dma=True)

        with ExitStack() as ph:
            scT = sbuf(ph, "scT", [128, 2, KC], F32)
            sig0 = sbuf(ph, "sig0", [128, 2, KC], F32)
            abT = sbuf(ph, "abT", [128, 4 * 9 * KC], F32)
            astg = [sbuf(ph, f"astg{i}", [128, KC, 512], F32) for i in range(2)]
            for c in range(2):
                P.op("sp", lambda e, c=c: e.dma_start(out=scT[:, c, :], in_=cvec[c].rearrange("(k p) -> p k", p=128),
                                                      allow_slow_non_contiguous=True), writes=["scT"], dma=True)
            for l in range(4):
                P.op("sp", lambda e, l=l: e.dma_start(out=abT[:, l * 144:(l + 1) * 144],
                                                      in_=ada_b[l].rearrange("(j p) -> p j", p=128),
                                                      allow_slow_non_contiguous=True), writes=["abT"], dma=True)
            P.op("act", lambda e: e.activation(out=sig0[:], in_=scT[:], func=AF.Sigmoid), reads=["scT"], writes=["sig0"])
            P.op("dve", lambda e: e.tensor_tensor(out=scT[:], in0=scT[:], in1=sig0[:], op=ALU.mult),
                 reads=["scT", "sig0"], writes=["scT"])
            blk = 0
            for l in range(LAYERS):
                for jb in range(36):
                    st = astg[blk % 2]
                    skey = ("astg", blk % 2)
                    P.op("sp", lambda e, st=st, l=l, jb=jb: e.dma_start(
                        out=st[:], in_=ada_w[l, :, jb * 512:(jb + 1) * 512].rearrange("(k p) c -> p k c", p=128)),
                        writes=[skey], dma=True)
                    bk = banks[blk % 2]
                    for jj in range(4):
                        for k in range(KC):
                            P.op("pe", lambda e, st=st, bk=bk, jj=jj, k=k: e.matmul(
                                bk[:, jj * 2:jj * 2 + 2], lhsT=st[:, k, jj * 128:(jj + 1) * 128], rhs=scT[:, :, k],
                                start=(k == 0), stop=(k == KC - 1)),
                                reads=[skey, "scT"], writes=[BK[blk % 2]], sig=(k == KC - 1))
                    for c in range(2):
                        j0 = jb * 4
                        v, k0 = j0 // KC, j0 % KC
                        i0 = ((l * 2 + c) * 9 + v) * KC + k0
                        b0 = l * 9 * KC + j0
                        P.op("dve", lambda e, bk=bk, c=c, i0=i0, b0=b0: e.tensor_tensor(
                            out=modc[:, i0:i0 + 4], in0=bk[:, c:8:2], in1=abT[:, b0:b0 + 4], op=ALU.add),
                            reads=[BK[blk % 2], "abT"], writes=["modc"])
                    blk += 1
            P.barrier()

        MIX = cfg.get("mixers", ["s5", "rg", "da", "ml"])
        LS = NTS * TT
        seqs = [(0, LS, -1)] + [(LS + i * 256, 256, i) for i in range(NTP * 2)]
        scopeid = [0]

        def xT_tile_ap(t):
            return xT[:, t * TT:(t + 1) * TT].rearrange("(k p) t -> p k t", p=128)

        def colload(dst, src1d, n):
            P.op("sp", lambda e: e.dma_start(out=dst, in_=src1d.rearrange("(k p) -> p k", p=128),
                                             allow_slow_non_contiguous=True), writes=["cols"], dma=True)

        with ExitStack() as ph:
            xin = [sbuf(ph, f"xin{i}", [128, D], F32) for i in range(4)]
            xo = sbuf(ph, "xo", [128, KC, TT], F32)
            for t in range(NT):
                for tb in range(4):
                    r0 = t * TT + tb * 128
                    P.op("sp", lambda e, tb=tb, r0=r0: e.dma_start(out=xin[tb][:], in_=x_in[r0:r0 + 128, :]),
                         writes=[("xin", tb)], dma=True)
                for k in range(KC):
                    bk = banks[k % 4]
                    for tb in range(4):
                        P.op("pe", lambda e, bk=bk, tb=tb, k=k: e.transpose(
                            bk[:, tb * 128:(tb + 1) * 128], xin[tb][:, k * 128:(k + 1) * 128], ident[:]),
                            reads=[("xin", tb), "ident"], writes=[BK[k % 4]], sig=(tb == 3))
                    if k % 2 == 0:
                        P.op("act", lambda e, bk=bk, k=k: e.copy(out=xo[:, k, :], in_=bk[:]),
                             reads=[BK[k % 4]], writes=[("xo", k)])
                    else:
                        P.op("dve", lambda e, bk=bk, k=k: e.tensor_copy(out=xo[:, k, :], in_=bk[:]),
                             reads=[BK[k % 4]], writes=[("xo", k)])
                P.op("pool", lambda e, t=t: e.dma_start(out=xT_tile_ap(t), in_=xo[:]),
                     reads=[("xo", k) for k in range(KC)], writes=[("xT", t, k) for k in range(KC)], dma=True)
            P.barrier()

        class NS:
            pass

        cnt = {"stg": 0, "wb": 0, "xc": 0, "it": 0, "ev": 0}
        NSTG, NWB = 2, 4

        def tok_scope(ph):
            B = NS()
            sid = scopeid[0]
            scopeid[0] += 1
            sb_ = lambda name, shape, dt: sbuf(ph, f"{name}_{sid}", shape, dt)
            B.xt = sb_("xt", [128, KC, TT], F32)
            B.hT = sb_("hT", [128, KC, TT], BF16)
            B.hh = sb_("hh", [128, FC, TT], BF16)
            B.stg = [sb_(f"stg{i}", [128, 4096], F32) for i in range(NSTG)]
            B.wbs = [sb_(f"wb{i}", [128, 4096], BF16) for i in range(NWB)]
            B.sq = [sb_(f"sq{i}", [128, TT], F32) for i in range(2)]
            B.rstd = sb_("rstd", [128, TT], F32)
            B.tt = [sb_(f"tt{i}", [128, TT], F32) for i in range(2)]
            B.ssb = [sb_(f"ssb{i}", [128, TT], F32) for i in range(2)]
            B.xc = [sb_(f"xc{i}", [128, TT], F32) for i in range(3)]
            B.ost = [sb_(f"ost{i}", [128, TT], F32) for i in range(3)]
            B.obf = [sb_(f"obf{i}", [128, TT], BF16) for i in range(3)]
            B.epsc = sb_("epsc", [128, 1], F32)
            P.op("dve", lambda e: e.memset(B.epsc[:], EPS), writes=["epsc"])
            return B

        def wload(B, wap, k0, nk, c0, ncol):
            si = cnt["stg"] % NSTG
            cnt["stg"] += 1
            bi = cnt["wb"] % NWB
            cnt["wb"] += 1
            sv = B.stg[si][:, :nk * ncol].rearrange("p (k c) -> p k c", k=nk)
            bv = B.wbs[bi][:, :nk * ncol].rearrange("p (k c) -> p k c", k=nk)
            P.op("sp", lambda e: e.dma_start(
                out=sv, in_=wap[k0 * 128:(k0 + nk) * 128, c0:c0 + ncol].rearrange("(k p) c -> p k c", p=128)),
                writes=[("stg", si)], dma=True)
            P.op("pool", lambda e: e.tensor_copy(out=bv, in_=sv), reads=[("stg", si)], writes=[("wb", bi)])
            return bv, ("wb", bi)

        def load_x(B, t):
            P.op("sp", lambda e: e.dma_start(out=B.xt[:], in_=xT_tile_ap(t)),
                 reads=[("xT", t, k) for k in range(KC)], writes=[("xt", k) for k in range(KC)], dma=True)

        def rms_rstd(B, src, skey, nk, inv_n):
            for k in range(nk):
                P.op("act", lambda e, k=k: e.activation(out=B.sq[k % 2][:], in_=src(k), func=AF.Square),
                     reads=[skey(k)], writes=[("sq", k % 2)])
                P.op("pe", lambda e, k=k: e.matmul(banks[4][:], lhsT=ones[:], rhs=B.sq[k % 2][:],
                                                   start=(k == 0), stop=(k == nk - 1)),
                     reads=["ones", ("sq", k % 2)], writes=[BK[4]], sig=True)
            P.op("act", lambda e: e.activation(out=B.rstd[:], in_=banks[4][:], func=AF.Sqrt,
                                               scale=inv_n, bias=B.epsc[:]),
                 reads=[BK[4], "epsc"], writes=["rstd"])
            P.op("dve", lambda e: e.reciprocal(out=B.rstd[:], in_=B.rstd[:]), reads=["rstd"], writes=["rstd"])

        def prep_GS(l, j, vsc, vg, gmul):
            for c in range(2):
                i0 = ((l * 2 + c) * 9 + vsc) * KC
                g0 = (l * 3 + j) * KC
                o0 = (c * 3 + 0) * KC
                P.op("dve", lambda e, i0=i0, g0=g0, o0=o0: e.scalar_tensor_tensor(
                    out=GS[:, o0:o0 + KC], in0=modc[:, i0:i0 + KC], scalar=1.0, in1=gcol[:, g0:g0 + KC],
                    op0=ALU.add, op1=ALU.mult), reads=["modc", "gcol"], writes=["GS"])
                i1 = ((l * 2 + c) * 9 + vg) * KC
                o1 = (c * 3 + 1) * KC
                P.op("dve", lambda e, i1=i1, o1=o1: e.tensor_scalar(
                    out=GS[:, o1:o1 + KC], in0=modc[:, i1:i1 + KC], scalar1=gmul, scalar2=None, op0=ALU.mult),
                    reads=["modc"], writes=["GS"])

        def norm_mod(B, l, cnd, vsh):
            rms_rstd(B, lambda k: B.xt[:, k, :], lambda k: ("xt", k), KC, 1.0 / D)
            for k in range(KC):
                P.op("dve", lambda e, k=k: e.tensor_tensor(out=B.tt[k % 2][:], in0=B.xt[:, k, :], in1=B.rstd[:], op=ALU.mult),
                     reads=[("xt", k), "rstd"], writes=[("tt", k % 2)])
                gi = (cnd * 3) * KC + k
                P.op("act", lambda e, k=k, gi=gi: e.activation(out=B.hT[:, k, :], in_=B.tt[k % 2][:], func=AF.Identity,
                                                               scale=GS[:, gi:gi + 1], bias=mcol(l, cnd, vsh, k)),
                     reads=[("tt", k % 2), "GS", "modc"], writes=[("hT", k)])

        def proj(B, wap, nk, src, skey, c0, nchunks, consume, pair=None):
            NG = (nchunks + 1) // 2
            pre = {}

            def issue(g):
                nc_ = min(2, nchunks - g * 2)
                a = wload(B, wap, 0, nk, c0 + g * 256, nc_ * 128)
                b = wload(B, pair[0], 0, nk, c0 + g * 256, nc_ * 128) if pair else None
                pre[g] = (a, b, nc_)
            issue(0)
            for g in range(NG):
                if g + 1 < NG:
                    issue(g + 1)
                (b1, k1), bb, nc_ = pre.pop(g)
                for fc in range(nc_):
                    f = g * 2 + fc
                    it = cnt["it"]
                    cnt["it"] += 1
                    pa, pb = banks[(it % 2) * 2], banks[(it % 2) * 2 + 1]
                    ka, kb = BK[(it % 2) * 2], BK[(it % 2) * 2 + 1]
                    for k in range(nk):
                        P.op("pe", lambda e, pa=pa, b1=b1, fc=fc, k=k: e.matmul(
                            pa[:], lhsT=b1[:, k, fc * 128:(fc + 1) * 128], rhs=src(k),
                            start=(k == 0), stop=(k == nk - 1)),
                            reads=[k1, skey(k)], writes=[ka], sig=(k == nk - 1))
                    if pair:
                        b3, k3 = bb
                        for k in range(nk):
                            P.op("pe", lambda e, pb=pb, b3=b3, fc=fc, k=k: e.matmul(
                                pb[:], lhsT=b3[:, k, fc * 128:(fc + 1) * 128], rhs=src(k),
                                start=(k == 0), stop=(k == nk - 1)),
                                reads=[k3, skey(k)], writes=[kb], sig=(k == nk - 1))
                        pair[1](f, pa, ka, pb, kb)
                    else:
                        consume(f, pa, ka)

        def down_res(B, t, cnd, src, skey, nf, wap):
            pieces = [(f0, min(8, nf - f0)) for f0 in range(0, nf, 8)]
            for dg in range(4):
                pre2 = {}

                def issue2(pi, dg=dg):
                    f0, n_ = pieces[pi]
                    pre2[pi] = wload(B, wap, f0, n_, dg * 512, 512)
                issue2(0)
                for pi, (f0, n_) in enumerate(pieces):
                    if pi + 1 < len(pieces):
                        issue2(pi + 1)
                    bw, kw = pre2.pop(pi)
                    for dc in range(4):
                        for fl in range(n_):
                            f = f0 + fl
                            P.op("pe", lambda e, dc=dc, fl=fl, f=f, bw=bw: e.matmul(
                                banks[4 + dc][:], lhsT=bw[:, fl, dc * 128:(dc + 1) * 128], rhs=src(f),
                                start=(f == 0), stop=(f == nf - 1)),
                                reads=[kw, skey(f)], writes=[BK[4 + dc]], sig=(fl == n_ - 1))
                for dc in range(4):
                    kk = dg * 4 + dc
                    xi = cnt["xc"] % 3
                    cnt["xc"] += 1
                    P.op("sp", lambda e, xi=xi, kk=kk: e.dma_start(
                        out=B.xc[xi][:], in_=xT[kk * 128:(kk + 1) * 128, t * TT:(t + 1) * TT]),
                        reads=[("xT", t, kk)], writes=[("xc", xi)], dma=True)
                    gc = (cnd * 3 + 1) * KC + kk
                    P.op("dve", lambda e, xi=xi, dc=dc, gc=gc: e.scalar_tensor_tensor(
                        out=B.ost[xi][:], in0=banks[4 + dc][:], scalar=GS[:, gc:gc + 1], in1=B.xc[xi][:],
                        op0=ALU.mult, op1=ALU.add),
                        reads=[BK[4 + dc], "GS", ("xc", xi)], writes=[("ost", xi)])
                    P.op("pool", lambda e, xi=xi, kk=kk: e.dma_start(
                        out=xT[kk * 128:(kk + 1) * 128, t * TT:(t + 1) * TT], in_=B.ost[xi][:]),
                        reads=[("ost", xi)], writes=[("xT", t, kk)], dma=True)

        def ffn_half(B, l, j, w1, w3, w2, vsh, vsc, vg):
            if cfg.get("no_ffn"):
                return
            prep_GS(l, j, vsc, vg, 0.5)
            for t in range(NT):
                cnd = 0 if t < NTS else 1
                load_x(B, t)
                norm_mod(B, l, cnd, vsh)

                def gate(f, pa, ka, pb, kb):
                    i = cnt["ev"]
                    cnt["ev"] += 1
                    P.op("act", lambda e: e.activation(out=B.ssb[i % 2][:], in_=pa[:], func=AF.Silu),
                         reads=[ka], writes=[("ssb", i % 2)])
                    P.op("dve", lambda e: e.tensor_tensor(out=B.hh[:, f, :], in0=B.ssb[i % 2][:], in1=pb[:], op=ALU.mult),
                         reads=[("ssb", i % 2), kb], writes=[("hh", f)])
                proj(B, w1[l], KC, lambda k: B.hT[:, k, :], lambda k: ("hT", k), 0, FC, None, pair=(w3[l], gate))
                down_res(B, t, cnd, lambda f: B.hh[:, f, :], lambda f: ("hh", f), FC, w2[l])

        def store_fm(B, scr, f, t, bank, bkey, func, dt, bias=None, ncols=TT):
            i = cnt["ev"]
            cnt["ev"] += 1
            tile_ = (B.ost if dt == F32 else B.obf)[i % 3]
            key = ("ost" if dt == F32 else "obf", i % 3)
            kw = {} if bias is None else {"bias": bias}
            P.op("act", lambda e: e.activation(out=tile_[:, :ncols], in_=bank[:, :ncols], func=func, **kw),
                 reads=[bkey, "cols"], writes=[key])
            P.op("pool", lambda e: e.dma_start(out=scr[f * 128:(f + 1) * 128, t * TT:t * TT + ncols], in_=tile_[:, :ncols]),
                 reads=[key], writes=[(scr.name, f, t)], dma=True)

        if "rg" in MIX:
            rg_w_in = dr("rg_w_in", [D, D]); rg_w_gate = dr("rg_w_gate", [D, D]); rg_w_out = dr("rg_w_out", [D, D])
            rg_conv_w = dr("rg_conv_w", [4, D]); rg_conv_b = dr("rg_conv_b", [D])
            rg_wa = dr("rg_wa", [2, 16, 128, 128]); rg_wx = dr("rg_wx", [2, 16, 128, 128])
            rg_ba = dr("rg_ba", [2, D]); rg_bx = dr("rg_bx", [2, D]); rg_lam = dr("rg_lam", [2, D])
            st_rg = dr("st_rg", [2, D])
            o_rg = dr("o_rg", [NTP * 2, 2, D], kind="ExternalOutput")
            XI = dr("rg_XI", [D, NTOK], kind="Internal")
            GT = dr("rg_GT", [D, NTOK], BF16, kind="Internal")
            YT = dr("rg_YT", [D, NTOK], BF16, kind="Internal")

        def mixer_rg_p1(B, l):
            prep_GS(l, 1, 4, 5, 1.0)
            for t in range(NT):
                cnd = 0 if t < NTS else 1
                load_x(B, t)
                norm_mod(B, l, cnd, 3)
                proj(B, rg_w_in, KC, lambda k: B.hT[:, k, :], lambda k: ("hT", k), 0, KC,
                     lambda f, pa, ka, t=t: store_fm(B, XI, f, t, pa, ka, AF.Identity, F32))
                proj(B, rg_w_gate, KC, lambda k: B.hT[:, k, :], lambda k: ("hT", k), 0, KC,
                     lambda f, pa, ka, t=t: store_fm(B, GT, f, t, pa, ka, AF.Gelu_apprx_tanh, BF16))

        def mixer_rg_p2():
            LM = LS
            with ExitStack() as ph:
                sb_ = lambda name, shape, dt: sbuf(ph, "rg2_" + name, shape, dt)
                xi = sb_("xi", [128, LM + 3], F32)
                xcv = sb_("xcv", [128, LM], F32)
                r_ = sb_("r", [128, LM], F32)
                ig = sb_("ig", [128, LM], F32)
                a_ = sb_("a", [128, LM], F32)
                a2 = sb_("a2", [128, LM], F32)
                hs = [sb_(f"hs{i}", [128, LM], F32) for i in range(2)]
                gt = sb_("gt", [128, LM], BF16)
                yg = sb_("yg", [128, LM], BF16)
                wab = sb_("wab", [128, 2, 2, 128], F32)
                cols = sb_("cols", [128, 16 * 16], F32)
                sp8 = sb_("sp8", [128, 2 * KC], F32)
                zc = sb_("zc", [128, 1], F32)
                CW, CB, BA, BX, LAMC, H0 = 0, 4, 5, 7, 9, 11
                for j in range(4):
                    colload(cols[:, (CW + j) * KC:(CW + j + 1) * KC], rg_conv_w[j], KC)
                colload(cols[:, CB * KC:(CB + 1) * KC], rg_conv_b, KC)
                for d in range(2):
                    colload(cols[:, (BA + d) * KC:(BA + d + 1) * KC], rg_ba[d], KC)
                    colload(cols[:, (BX + d) * KC:(BX + d + 1) * KC], rg_bx[d], KC)
                    colload(cols[:, (LAMC + d) * KC:(LAMC + d + 1) * KC], rg_lam[d], KC)
                    colload(cols[:, (H0 + d) * KC:(H0 + d + 1) * KC], st_rg[d], KC)
                P.op("dve", lambda e: e.memset(zc[:], 0.0), writes=["zc"])
                P.op("act", lambda e: e.activation(out=sp8[:], in_=cols[:, LAMC * KC:(LAMC + 2) * KC], func=AF.Exp, scale=-1.0),
                     reads=["cols"], writes=["sp8"])
                P.op("act", lambda e: e.activation(out=sp8[:], in_=sp8[:], func=AF.Ln, bias=1.0), reads=["sp8"], writes=["sp8"])
                P.op("dve", lambda e: e.tensor_scalar(out=sp8[:], in0=sp8[:], scalar1=-8.0, scalar2=None, op0=ALU.mult),
                     reads=["sp8"], writes=["sp8"])
                col = lambda base, n: cols[:, base * KC + n:base * KC + n + 1]
                bki = 0
                for n in range(KC):
                    for d in range(2):
                        P.op("sp", lambda e, n=n, d=d: e.dma_start(out=wab[:, 0, d, :], in_=rg_wa[d, n]), writes=["wab"], dma=True)
                        P.op("sp", lambda e, n=n, d=d: e.dma_start(out=wab[:, 1, d, :], in_=rg_wx[d, n]), writes=["wab"], dma=True)
                    for (tok0, L, pi) in seqs:
                        P.op("dve", lambda e: e.memset(xi[:, 0:1], 0.0), writes=["xi"])
                        P.op("dve", lambda e, L=L: e.memset(xi[:, L + 1:L + 3], 0.0), writes=["xi"])
                        P.op("sp", lambda e, n=n, tok0=tok0, L=L: e.dma_start(
                            out=xi[:, 1:L + 1], in_=XI[n * 128:(n + 1) * 128, tok0:tok0 + L]), writes=["xi"], dma=True)
                        P.op("sp", lambda e, n=n, tok0=tok0, L=L: e.dma_start(
                            out=gt[:, :L], in_=GT[n * 128:(n + 1) * 128, tok0:tok0 + L]), writes=["gt"], dma=True)
                        P.op("dve", lambda e, n=n, L=L: e.tensor_scalar(
                            out=xcv[:, :L], in0=xi[:, 0:L], scalar1=col(CW, n), scalar2=col(CB, n), op0=ALU.mult, op1=ALU.add),
                            reads=["xi", "cols"], writes=["xcv"])
                        for j in range(1, 4):
                            P.op("dve", lambda e, n=n, L=L, j=j: e.scalar_tensor_tensor(
                                out=xcv[:, :L], in0=xi[:, j:j + L], scalar=col(CW + j, n), in1=xcv[:, :L],
                                op0=ALU.mult, op1=ALU.add), reads=["xi", "cols", "xcv"], writes=["xcv"])
                        for d in range(2):
                            for (gi, dst, bcol, dk) in ((0, r_, BA, "r"), (1, ig, BX, "ig")):
                                for c0 in range(0, L, TT):
                                    w = min(TT, L - c0)
                                    bk, bkk = banks[bki % 4], BK[bki % 4]
                                    bki += 1
                                    P.op("pe", lambda e, bk=bk, gi=gi, d=d, c0=c0, w=w: e.matmul(
                                        bk[:, :w], lhsT=wab[:, gi, d, :], rhs=xcv[:, c0:c0 + w], start=True, stop=True),
                                        reads=["wab", "xcv"], writes=[bkk])
                                    P.op("act", lambda e, bk=bk, dst=dst, c0=c0, w=w, bcol=bcol, d=d, n=n: e.activation(
                                        out=dst[:, c0:c0 + w], in_=bk[:, :w], func=AF.Sigmoid, bias=col(bcol + d, n)),
                                        reads=[bkk, "cols"], writes=[dk])
                            P.op("act", lambda e, L=L, d=d, n=n: e.activation(
                                out=a_[:, :L], in_=r_[:, :L], func=AF.Exp, scale=sp8[:, d * KC + n:d * KC + n + 1]),
                                reads=["r", "sp8"], writes=["a"])
                            P.op("dve", lambda e, L=L: e.tensor_tensor(out=a2[:, :L], in0=a_[:, :L], in1=a_[:, :L], op=ALU.mult),
                                 reads=["a"], writes=["a2"])
                            P.op("act", lambda e, L=L: e.activation(out=a2[:, :L], in_=a2[:, :L], func=AF.Sqrt, scale=-1.0, bias=1.0),
                                 reads=["a2"], writes=["a2"])
                            P.op("dve", lambda e, L=L: e.tensor_tensor(out=ig[:, :L], in0=ig[:, :L], in1=xcv[:, :L], op=ALU.mult),
                                 reads=["ig", "xcv"], writes=["ig"])
                            P.op("dve", lambda e, L=L: e.tensor_tensor(out=a2[:, :L], in0=a2[:, :L], in1=ig[:, :L], op=ALU.mult),
                                 reads=["ig", "a2"], writes=["a2"])
                            h0 = col(H0 + d, n) if pi < 0 else zc[:]
                            if d == 0:
                                o_, x0, x1 = hs[0][:, :L], a_[:, :L], a2[:, :L]
                            else:
                                rv = lambda ap_, L=L: AP(ap_.tensor, ap_.offset + L - 1, [list(ap_.ap[0]), [-1, L]])
                                o_, x0, x1 = rv(hs[1][:, :L]), rv(a_[:, :L]), rv(a2[:, :L])
                            P.op("dve", lambda e, o_=o_, x0=x0, x1=x1, h0=h0: e.tensor_tensor_scan(
                                out=o_, data0=x0, data1=x1, initial=h0, op0=ALU.mult, op1=ALU.add),
                                reads=["a", "a2", "cols", "zc"], writes=[("hs", d)])
                            if pi >= 0:
                                fc_ = L - 1 if d == 0 else 0
                                P.op("pool", lambda e, d=d, n=n, pi=pi, fc_=fc_: e.dma_start(
                                    out=o_rg[pi, d, n * 128:(n + 1) * 128].rearrange("(p o) -> p o", o=1),
                                    in_=hs[d][:, fc_:fc_ + 1]), reads=[("hs", d)], dma=True)
                        P.op("dve", lambda e, L=L: e.tensor_tensor(out=hs[0][:, :L], in0=hs[0][:, :L], in1=hs[1][:, :L], op=ALU.add),
                             reads=[("hs", 0), ("hs", 1)], writes=[("hs", 0)])
                        P.op("dve", lambda e, L=L: e.tensor_tensor(out=yg[:, :L], in0=hs[0][:, :L], in1=gt[:, :L], op=ALU.mult),
                             reads=[("hs", 0), "gt"], writes=["yg"])
                        P.op("pool", lambda e, n=n, tok0=tok0, L=L: e.dma_start(
                            out=YT[n * 128:(n + 1) * 128, tok0:tok0 + L], in_=yg[:, :L]), reads=["yg"], dma=True)
                P.barrier()

        if "s5" in MIX:
            s5_w_in = dr("s5_w_in", [D, 1024]); s5_w_glu = dr("s5_w_glu", [1024, 1024]); s5_w_out = dr("s5_w_out", [1024, D])
            s5_a_re = dr("s5_a_re", [2, 64, 64]); s5_a_im = dr("s5_a_im", [2, 64, 64]); s5_log_dt = dr("s5_log_dt", [2, 64])
            s5_b_re = dr("s5_b_re", [2, 64, 64, 16]); s5_b_im = dr("s5_b_im", [2, 64, 64, 16])
            s5_c_re = dr("s5_c_re", [2, 64, 16, 64]); s5_c_im = dr("s5_c_im", [2, 64, 16, 64])
            s5_d = dr("s5_d", [1024]); s5_b_glu = dr("s5_b_glu", [1024])
            st_s5 = dr("st_s5", [2, 2, 64, 64])
            o_s5 = dr("o_s5", [NTP * 2, 2, 2, 64, 64], kind="ExternalOutput")
            UT = dr("s5_UT", [1024, NTOK], kind="Internal")
            YS = dr("s5_YS", [1024, NTOK], kind="Internal")
            ZT = dr("s5_ZT", [1024, NTOK], BF16, kind="Internal")
            Z2 = dr("s5_Z2", [1024, NTOK], BF16, kind="Internal")

        def mixer_s5_p1(B, l):
            prep_GS(l, 1, 4, 5, 1.0)
            for t in range(NT):
                cnd = 0 if t < NTS else 1
                load_x(B, t)
                norm_mod(B, l, cnd, 3)
                proj(B, s5_w_in, KC, lambda k: B.hT[:, k, :], lambda k: ("hT", k), 0, 8,
                     lambda f, pa, ka, t=t: store_fm(B, UT, f, t, pa, ka, AF.Identity, F32))

        def mixer_s5_p2():
            TWO_PI = 6.283185307179586
            with ExitStack() as ph:
                sb_ = lambda name, shape, dt: sbuf(ph, "s5_" + name, shape, dt)
                W = {n: sb_(n, [128, 32, 128], F32) for n in ("WPr", "WPi", "WMr", "WMi")}
                MB = sb_("MB", [128, 8, 4, 2, 128], BF16)
                MC = sb_("MC", [128, 32, 2, 128], BF16)
                Ct = [sb_(f"Ct{i}", [128, 32, 128], F32) for i in range(2)]
                Bt = Ct
                Ht = [sb_(f"Ht{i}", [128, 32, 128], BF16) for i in range(2)]
                T1, T2 = Ct[0], Ct[1]
                TG = [sb_(f"TG{i}", [128, 8, 128], F32) for i in range(2)]
                onesr = sb_("onesr", [128, 128], F32)
                uf = sb_("uf", [128, 8, 128], F32)
                ub = sb_("ub", [128, 8, 128], BF16)
                yf = sb_("yf", [128, 8, 128], F32)
                yo_ = sb_("yo", [128, 8, 128], F32)
                y2 = sb_("y2", [128, 8, 128], F32)
                zb = sb_("zb", [128, 8, 128], BF16)
                dcol = sb_("dcol", [128, 8], F32)
                sm = {n: sb_("sm_" + n, [128, 32], F32) for n in
                      ("are", "aim", "ldt", "dt", "mag", "ang", "sn", "cs", "lr", "li", "den", "lm1", "cr", "ci",
                       "ir", "ii", "t1", "t2", "t3", "pr", "pi", "qr", "qi", "kr", "ki", "hr", "hi", "h0r", "h0i")}
                smi = sb_("smi", [128, 32], I32)
                BB = {n: sb_(n, [128, 32, 16], F32) for n in ("bre", "bim", "bbr", "bbi")}
                XB = Ct[0]
                CN = sb_("CN", [128, 8, 128], F32)
                P.op("dve", lambda e: e.memset(onesr[:], 1.0), writes=["onesr"])
                colload(dcol[:], s5_d, 8)

                def tt(out, a, b, op, eng="dve", r=("tab",), w=("tab",)):
                    P.op(eng, lambda e: e.tensor_tensor(out=out, in0=a, in1=b, op=op), reads=list(r), writes=list(w))

                def ts(out, a, s1, s2, op0, op1=None, r=("tab",), w=("tab",)):
                    if op1 is None:
                        P.op("dve", lambda e: e.tensor_scalar(out=out, in0=a, scalar1=s1, scalar2=None, op0=op0), reads=list(r), writes=list(w))
                    else:
                        P.op("dve", lambda e: e.tensor_scalar(out=out, in0=a, scalar1=s1, scalar2=s2, op0=op0, op1=op1), reads=list(r), writes=list(w))

                def act(out, a, func, r=("tab",), w=("tab",), **kw):
                    P.op("act", lambda e: e.activation(out=out, in_=a, func=func, **kw), reads=list(r), writes=list(w))

                def cmul(outr, outi, ar, ai, br, bi, t1, t2, eng2="dve", r=("tab",), w=("tab",)):
                    tt(t1, ar, br, ALU.mult, r=r, w=w); tt(t2, ai, bi, ALU.mult, r=r, w=w)
                    tt(outr, t1, t2, ALU.subtract, eng=eng2, r=r, w=w)
                    tt(t1, ar, bi, ALU.mult, r=r, w=w); tt(t2, ai, br, ALU.mult, r=r, w=w)
                    tt(outi, t1, t2, ALU.add, eng=eng2, r=r, w=w)

                def sinturn(out, turns):
                    P.op("dve", lambda e: e.tensor_copy(out=smi[:], in_=turns), reads=["tab"], writes=["tab"])
                    P.op("dve", lambda e: e.tensor_copy(out=sm["t2"][:], in_=smi[:]), reads=["tab"], writes=["tab"])
                    tt(sm["t2"][:], turns, sm["t2"][:], ALU.subtract)
                    ts(sm["t2"][:], sm["t2"][:], 0.4999999, -0.4999999, ALU.min, ALU.max)
                    act(out, sm["t2"][:], AF.Sin, scale=TWO_PI)

                def bc(ap2, n):
                    return AP(ap2.tensor, ap2.offset, [list(ap2.ap[0]), list(ap2.ap[1]), [0, n]])

                def sm_load(dst, src2d):
                    P.op("sp", lambda e: e.dma_start(out=dst, in_=src2d.rearrange("g p -> (g p)").rearrange("(s q) -> q s", q=128),
                                                     allow_slow_non_contiguous=True), writes=["tab"], dma=True)

                for d in range(cfg.get("s5_ndir", 2)):
                    sm_load(sm["are"][:], s5_a_re[d]); sm_load(sm["aim"][:], s5_a_im[d])
                    for h in range(2):
                        src = s5_log_dt[d]
                        bsrc = AP(src.tensor, src.offset + h, [[0, 64], [2, 32]])
                        P.op("sp", lambda e, h=h, bsrc=bsrc: e.dma_start(out=sm["ldt"][h * 64:(h + 1) * 64, :], in_=bsrc,
                                                                         allow_slow_non_contiguous=True), writes=["tab"], dma=True)
                    act(sm["dt"][:], sm["ldt"][:], AF.Exp)
                    tt(sm["t1"][:], sm["are"][:], sm["dt"][:], ALU.mult)
                    act(sm["mag"][:], sm["t1"][:], AF.Exp)
                    tt(sm["ang"][:], sm["aim"][:], sm["dt"][:], ALU.mult)
                    ts(sm["ang"][:], sm["ang"][:], 1.0 / TWO_PI, None, ALU.mult)
                    sinturn(sm["sn"][:], sm["ang"][:])
                    ts(sm["t1"][:], sm["ang"][:], 0.25, None, ALU.add)
                    sinturn(sm["cs"][:], sm["t1"][:])
                    tt(sm["lr"][:], sm["mag"][:], sm["cs"][:], ALU.mult)
                    tt(sm["li"][:], sm["mag"][:], sm["sn"][:], ALU.mult)
                    tt(sm["t1"][:], sm["are"][:], sm["are"][:], ALU.mult); tt(sm["t2"][:], sm["aim"][:], sm["aim"][:], ALU.mult)
                    tt(sm["den"][:], sm["t1"][:], sm["t2"][:], ALU.add)
                    P.op("dve", lambda e: e.reciprocal(out=sm["den"][:], in_=sm["den"][:]), reads=["tab"], writes=["tab"])
                    ts(sm["lm1"][:], sm["lr"][:], -1.0, None, ALU.add)
                    tt(sm["t1"][:], sm["lm1"][:], sm["are"][:], ALU.mult); tt(sm["t2"][:], sm["li"][:], sm["aim"][:], ALU.mult)
                    tt(sm["t1"][:], sm["t1"][:], sm["t2"][:], ALU.add); tt(sm["cr"][:], sm["t1"][:], sm["den"][:], ALU.mult)
                    tt(sm["t1"][:], sm["li"][:], sm["are"][:], ALU.mult); tt(sm["t2"][:], sm["lm1"][:], sm["aim"][:], ALU.mult)
                    tt(sm["t1"][:], sm["t1"][:], sm["t2"][:], ALU.subtract); tt(sm["ci"][:], sm["t1"][:], sm["den"][:], ALU.mult)
                    tt(sm["t1"][:], sm["lr"][:], sm["lr"][:], ALU.mult); tt(sm["t2"][:], sm["li"][:], sm["li"][:], ALU.mult)
                    tt(sm["t1"][:], sm["t1"][:], sm["t2"][:], ALU.add)
                    P.op("dve", lambda e: e.reciprocal(out=sm["t1"][:], in_=sm["t1"][:]), reads=["tab"], writes=["tab"])
                    tt(sm["ir"][:], sm["lr"][:], sm["t1"][:], ALU.mult)
                    tt(sm["ii"][:], sm["li"][:], sm["t1"][:], ALU.mult)
                    ts(sm["ii"][:], sm["ii"][:], -1.0, None, ALU.mult)
                    for (Tr, Ti, mr, mi) in ((W["WPr"], W["WPi"], sm["lr"], sm["li"]), (W["WMr"], W["WMi"], sm["ir"], sm["ii"])):
                        P.op("dve", lambda e, Tr=Tr: e.memset(Tr[:, :, 0:1], 1.0), writes=["tab"])
                        P.op("dve", lambda e, Ti=Ti: e.memset(Ti[:, :, 0:1], 0.0), writes=["tab"])
                        P.op("dve", lambda e, mr=mr: e.tensor_copy(out=sm["pr"][:], in_=mr[:]), reads=["tab"], writes=["tab"])
                        P.op("dve", lambda e, mi=mi: e.tensor_copy(out=sm["pi"][:], in_=mi[:]), reads=["tab"], writes=["tab"])
                        for k in range(7):
                            n = 1 << k
                            cmul(Tr[:, :, n:2 * n], Ti[:, :, n:2 * n], Tr[:, :, 0:n], Ti[:, :, 0:n],
                                 bc(sm["pr"][:], n), bc(sm["pi"][:], n), T1[:, :, 0:n], T2[:, :, 0:n])
                            if k < 6:
                                cmul(sm["qr"][:], sm["qi"][:], sm["pr"][:], sm["pi"][:], sm["pr"][:], sm["pi"][:], sm["t1"][:], sm["t2"][:])
                                P.op("dve", lambda e: e.tensor_copy(out=sm["pr"][:], in_=sm["qr"][:]), reads=["tab"], writes=["tab"])
                                P.op("dve", lambda e: e.tensor_copy(out=sm["pi"][:], in_=sm["qi"][:]), reads=["tab"], writes=["tab"])
                    for (nm, src) in (("bre", s5_b_re), ("bim", s5_b_im)):
                        P.op("sp", lambda e, nm=nm, src=src: e.dma_start(
                            out=BB[nm][:], in_=src[d].rearrange("g p c -> (g p) c").rearrange("(s q) c -> q s c", q=128)),
                            writes=["tab"], dma=True)
                    cmul(BB["bbr"][:], BB["bbi"][:], bc(sm["cr"][:], 16), bc(sm["ci"][:], 16), BB["bre"][:], BB["bim"][:],
                         T1[:, :, 0:16], T2[:, :, 0:16])
                    for ri, nm in enumerate(("bbr", "bbi")):
                        P.op("dve", lambda e: e.memset(XB[:], 0.0), reads=["tab"], writes=["tab"])
                        for r_ in range(4):
                            for h in range(2):
                                off = 16 * ((2 * r_ + h) % 8)
                                P.op("dve", lambda e, r_=r_, h=h, off=off, nm=nm: e.tensor_copy(
                                    out=XB[h * 64:(h + 1) * 64, r_::4, off:off + 16], in_=BB[nm][h * 64:(h + 1) * 64, r_::4, :]),
                                    reads=["tab"], writes=["tab"])
                        for sbi in range(32):
                            bk = banks[sbi % 4]
                            P.op("pe", lambda e, bk=bk, sbi=sbi: e.transpose(bk[:, 0:128], XB[:, sbi, :], ident[:]),
                                 reads=["tab", "ident"], writes=[BK[sbi % 4]])
                            P.op("act", lambda e, bk=bk, sbi=sbi, ri=ri: e.copy(out=MB[:, sbi // 4, sbi % 4, ri, :], in_=bk[:, 0:128]),
                                 reads=[BK[sbi % 4]], writes=["MB"])
                    P.op("dve", lambda e: e.memset(MC[:], 0.0), reads=["MC"], writes=["MC"])
                    for ri, src in enumerate((s5_c_re, s5_c_im)):
                        for hh_ in range(2):
                            P.op("sp", lambda e, src=src, hh_=hh_: e.dma_start(
                                out=CN[:, :, hh_ * 64:(hh_ + 1) * 64],
                                in_=src[d].rearrange("g c p -> (g c) p").rearrange("(k r) p -> r k p", r=128)),
                                writes=["CN"], dma=True)
                        for cc in range(8):
                            bk = banks[cc % 4]
                            P.op("pe", lambda e, bk=bk, cc=cc: e.transpose(bk[:, 0:128], CN[:, cc, :], ident[:]),
                                 reads=["CN", "ident"], writes=[BK[cc % 4]])
                            for sbl in range(4):
                                for h in range(2):
                                    gl = 2 * sbl + h
                                    P.op("act", lambda e, bk=bk, cc=cc, sbl=sbl, h=h, gl=gl, ri=ri: e.activation(
                                        out=MC[h * 64:(h + 1) * 64, cc * 4 + sbl, ri, 16 * gl:16 * gl + 16],
                                        in_=bk[h * 64:(h + 1) * 64, 16 * gl:16 * gl + 16], func=AF.Identity,
                                        scale=(1.0 if ri == 0 else -1.0)),
                                        reads=[BK[cc % 4]], writes=["MC"])
                    P.barrier()
                    for (tok0, L, pi) in (seqs if cfg.get("s5_stop", 9) > 1 else []):
                        nch = L // 128
                        if pi < 0:
                            sm_load(sm["h0r"][:], st_s5[d, 0]); sm_load(sm["h0i"][:], st_s5[d, 1])
                            cmul(sm["kr"][:], sm["ki"][:], sm["lr"][:], sm["li"][:], sm["h0r"][:], sm["h0i"][:], sm["t1"][:], sm["t2"][:],
                                 r=("tab", "K"), w=("tab", "K"))
                        else:
                            P.op("dve", lambda e: e.memset(sm["kr"][:], 0.0), reads=["K"], writes=["K"])
                            P.op("dve", lambda e: e.memset(sm["ki"][:], 0.0), reads=["K"], writes=["K"])
                        order = range(nch) if d == 0 else range(nch - 1, -1, -1)
                        for ci_, ch in enumerate(order):
                            t0 = tok0 + ch * 128
                            P.op("sp", lambda e, t0=t0: e.dma_start(out=uf[:], in_=UT[:, t0:t0 + 128].rearrange("(c p) t -> p c t", p=128)),
                                 writes=["uf"], dma=True)
                            P.op("pool", lambda e: e.tensor_copy(out=ub[:], in_=uf[:]), reads=["uf"], writes=["ub"])
                            if d == 1:
                                P.op("sp", lambda e, t0=t0: e.dma_start(out=yf[:], in_=YS[:, t0:t0 + 128].rearrange("(c p) t -> p c t", p=128)),
                                     writes=["yf"], dma=True)

                            def tv(tile_, g0, ng, rev):
                                a = tile_[:, g0:g0 + ng, :]
                                if not rev:
                                    return a
                                return AP(a.tensor, a.offset + 127, [list(a.ap[0]), list(a.ap[1]), [-1, 128]])
                            rev = (d == 1)
                            for grp in range(8):
                                pr_, pi_ = banks[(grp % 3) * 2], banks[(grp % 3) * 2 + 1]
                                kr_, ki_ = BK[(grp % 3) * 2], BK[(grp % 3) * 2 + 1]
                                for sbl in range(4):
                                    P.op("pe", lambda e, pr_=pr_, grp=grp, sbl=sbl: e.matmul(
                                        pr_[:, sbl * 128:(sbl + 1) * 128], lhsT=MB[:, grp, sbl, 0, :], rhs=ub[:, grp, :], start=True, stop=True),
                                        reads=["MB", "ub"], writes=[kr_], sig=(sbl == 3))
                                for sbl in range(4):
                                    P.op("pe", lambda e, pi_=pi_, grp=grp, sbl=sbl: e.matmul(
                                        pi_[:, sbl * 128:(sbl + 1) * 128], lhsT=MB[:, grp, sbl, 1, :], rhs=ub[:, grp, :], start=True, stop=True),
                                        reads=["MB", "ub"], writes=[ki_], sig=(sbl == 3))
                                g0 = grp * 4
                                tg1 = TG[0][:, (grp % 2) * 4:(grp % 2) * 4 + 4, :]
                                tg2 = TG[1][:, (grp % 2) * 4:(grp % 2) * 4 + 4, :]
                                pr3 = pr_[:].rearrange("p (s t) -> p s t", s=4)
                                pi3 = pi_[:].rearrange("p (s t) -> p s t", s=4)
                                wr, wi = tv(W["WMr"], g0, 4, rev), tv(W["WMi"], g0, 4, rev)
                                kk = ("g", grp % 2)
                                tt(tg1, pr3, wr, ALU.mult, r=(kr_, "tab", kk), w=(kk,))
                                tt(tg2, pi3, wi, ALU.mult, r=(ki_, "tab", kk), w=(kk,))
                                tt(Bt[0][:, g0:g0 + 4, :], tg1, tg2, ALU.subtract, eng="pool", r=(kk, ("C", grp)), w=(kk, ("B", grp)))
                                tt(tg1, pi3, wr, ALU.mult, r=(ki_, "tab", kk), w=(kk,))
                                tt(tg2, pr3, wi, ALU.mult, r=(kr_, "tab", kk), w=(kk,))
                                tt(Bt[1][:, g0:g0 + 4, :], tg1, tg2, ALU.add, eng="pool", r=(kk,), w=(kk, ("B", grp)))
                                for sbi in range(g0, g0 + 4):
                                    for ri in range(2):
                                        o_ = Ct[ri][:, sbi, :]; x_ = Bt[ri][:, sbi, :]; on_ = onesr[:]
                                        if rev:
                                            o_ = AP(o_.tensor, o_.offset + 127, [list(o_.ap[0]), [-1, 128]])
                                            x_ = AP(x_.tensor, x_.offset + 127, [list(x_.ap[0]), [-1, 128]])
                                        kcol = (sm["kr"] if ri == 0 else sm["ki"])[:, sbi:sbi + 1]
                                        P.op("dve", lambda e, o_=o_, x_=x_, on_=on_, kcol=kcol: e.tensor_tensor_scan(
                                            out=o_, data0=on_, data1=x_, initial=kcol, op0=ALU.mult, op1=ALU.add),
                                            reads=[("B", grp), "onesr", "K"], writes=[("C", grp)])
                                wr, wi = tv(W["WPr"], g0, 4, rev), tv(W["WPi"], g0, 4, rev)
                                cr3, ci3 = Ct[0][:, g0:g0 + 4, :], Ct[1][:, g0:g0 + 4, :]
                                tt(tg1, cr3, wr, ALU.mult, r=(("C", grp), "tab", kk), w=(kk,))
                                tt(tg2, ci3, wi, ALU.mult, r=(("C", grp), "tab", kk), w=(kk,))
                                tt(Ht[0][:, g0:g0 + 4, :], tg1, tg2, ALU.subtract, eng="pool", r=(kk, ("H", grp)), w=(kk, ("H", grp)))
                                tt(tg1, ci3, wr, ALU.mult, r=(("C", grp), "tab", kk), w=(kk,))
                                tt(tg2, cr3, wi, ALU.mult, r=(("C", grp), "tab", kk), w=(kk,))
                                tt(Ht[1][:, g0:g0 + 4, :], tg1, tg2, ALU.add, eng="pool", r=(kk,), w=(kk, ("H", grp)))
                            lc = 0 if rev else 127
                            wl = 127
                            Cg = [("C", g) for g in range(8)]
                            cmul(sm["hr"][:], sm["hi"][:], W["WPr"][:, :, wl], W["WPi"][:, :, wl], Ct[0][:, :, lc], Ct[1][:, :, lc],
                                 sm["t1"][:], sm["t2"][:], r=["tab", "K"] + Cg, w=("tab", "K"))
                            cmul(sm["kr"][:], sm["ki"][:], sm["lr"][:], sm["li"][:], sm["hr"][:], sm["hi"][:], sm["t1"][:], sm["t2"][:],
                                 r=("tab", "K"), w=("tab", "K"))
                            if pi >= 0 and ci_ == nch - 1:
                                for ri, nm in enumerate(("hr", "hi")):
                                    P.op("pool", lambda e, ri=ri, nm=nm, pi=pi: e.dma_start(
                                        out=o_s5[pi, d, ri].rearrange("g p -> (g p)").rearrange("(s q) -> q s", q=128), in_=sm[nm][:],
                                        allow_slow_non_contiguous=True), reads=["tab", "K"], dma=True)
                            for cc in range(8):
                                bk, bkk = banks[6 + cc // 4], BK[6 + cc // 4]
                                i_ = 0
                                for sbl in range(4):
                                    for ri in range(2):
                                        sbi = cc * 4 + sbl
                                        P.op("pe", lambda e, bk=bk, cc=cc, sbi=sbi, ri=ri, i_=i_: e.matmul(
                                            bk[:, (cc % 4) * 128:(cc % 4 + 1) * 128], lhsT=MC[:, sbi, ri, :], rhs=Ht[ri][:, sbi, :],
                                            start=(i_ == 0), stop=(i_ == 7)),
                                            reads=["MC", ("H", cc)], writes=[bkk], sig=(i_ == 7))
                                        i_ += 1
                            for hb in range(2):
                                bk, bkk = banks[6 + hb], BK[6 + hb]
                                ysl = (yo_ if d == 0 else y2)[:, hb * 4:(hb + 1) * 4, :]
                                if d == 0:
                                    P.op("act", lambda e, bk=bk, ysl=ysl: e.copy(out=ysl, in_=bk[:].rearrange("p (c t) -> p c t", c=4)),
                                         reads=[bkk], writes=[("yo", hb)])
                                else:
                                    P.op("dve", lambda e, bk=bk, ysl=ysl, hb=hb: e.tensor_tensor(
                                        out=ysl, in0=bk[:].rearrange("p (c t) -> p c t", c=4), in1=yf[:, hb * 4:(hb + 1) * 4, :], op=ALU.add),
                                        reads=[bkk, "yf"], writes=[("y2", hb)])
                            if d == 0:
                                P.op("pool", lambda e, t0=t0: e.dma_start(out=YS[:, t0:t0 + 128].rearrange("(c p) t -> p c t", p=128), in_=yo_[:]),
                                     reads=[("yo", 0), ("yo", 1)], dma=True)
                            else:
                                for cc in range(8):
                                    P.op("dve", lambda e, cc=cc: e.scalar_tensor_tensor(
                                        out=y2[:, cc, :], in0=uf[:, cc, :], scalar=dcol[:, cc:cc + 1], in1=y2[:, cc, :], op0=ALU.mult, op1=ALU.add),
                                        reads=["uf", "cols", ("y2", cc // 4)], writes=[("y2", cc // 4)])
                                P.op("act", lambda e: e.activation(out=zb[:], in_=y2[:], func=AF.Gelu_apprx_tanh),
                                     reads=[("y2", 0), ("y2", 1)], writes=["zb"])
                                P.op("pool", lambda e, t0=t0: e.dma_start(out=ZT[:, t0:t0 + 128].rearrange("(c p) t -> p c t", p=128), in_=zb[:]),
                                     reads=["zb"], dma=True)
                P.barrier()

        def mixer_s5_p3(B, l):
            with ExitStack() as ph2:
                bg = sbuf(ph2, f"bglu_{scopeid[0]}", [128, 8], F32)
                colload(bg[:], s5_b_glu, 8)
                for t in range(NT):
                    P.op("sp", lambda e, t=t: e.dma_start(
                        out=B.hT[:, :8, :], in_=ZT[:, t * TT:(t + 1) * TT].rearrange("(k p) t -> p k t", p=128)),
                        writes=[("hT", k) for k in range(8)], dma=True)

                    def glu(f, pa, ka, t=t):
                        i = cnt["ev"]
                        cnt["ev"] += 1
                        P.op("act", lambda e: e.activation(out=B.ssb[i % 2][:], in_=pa[:], func=AF.Sigmoid, bias=bg[:, f:f + 1]),
                             reads=[ka, "cols"], writes=[("ssb", i % 2)])
                        P.op("dve", lambda e: e.tensor_tensor(out=B.obf[i % 3][:], in0=B.ssb[i % 2][:], in1=B.hT[:, f, :], op=ALU.mult),
                             reads=[("ssb", i % 2), ("hT", f)], writes=[("obf", i % 3)])
                        P.op("pool", lambda e: e.dma_start(out=Z2[f * 128:(f + 1) * 128, t * TT:(t + 1) * TT], in_=B.obf[i % 3][:]),
                             reads=[("obf", i % 3)], dma=True)
                    proj(B, s5_w_glu, 8, lambda k: B.hT[:, k, :], lambda k: ("hT", k), 0, 8, glu)
                P.barrier()

        if "da" in MIX:
            da_wq = dr("da_wq", [D, D]); da_wk = dr("da_wk", [D, D]); da_wv = dr("da_wv", [D, D]); da_wo = dr("da_wo", [D, D])
            da_lam = dr("da_lam", [4, 128]); da_subln_g = dr("da_subln_g", [256])
            kcache = dr("kcache", [256, D]); vcache = dr("vcache", [256, D])
            rope_cos = dr("rope_cos", [128, 4096]); rope_sin = dr("rope_sin", [128, 4096]); rope_perm = dr("rope_perm", [128, 128])
            o_dk = dr("o_dk", [NTP * 512, D], kind="ExternalOutput")
            o_dv = dr("o_dv", [NTP * 512, D], kind="ExternalOutput")
            QT = dr("da_QT", [D, NTOK], BF16, kind="Internal")
            KT = dr("da_KT", [D, 256 + NTOK], BF16, kind="Internal")
            VT = dr("da_VT", [256 + NTOK, D], BF16, kind="Internal")
            AT = dr("da_AT", [D, NTOK], BF16, kind="Internal")
            KO = dr("da_KO", [NTP * 512, D], kind="Internal")
            VO = dr("da_VO", [NTP * 512, D], kind="Internal")
            LAM_INIT = 0.8 - 0.6 * float(np.exp(-0.3 * 2))

        def mixer_da_p1(B, l):
            prep_GS(l, 1, 4, 5, 1.0)
            with ExitStack() as ph2:
                sid = scopeid[0]
                perm = sbuf(ph2, f"perm_{sid}", [128, 128], F32)
                rc = sbuf(ph2, f"rc_{sid}", [128, TT], F32)
                rs = sbuf(ph2, f"rs_{sid}", [128, TT], F32)
                tmk = [sbuf(ph2, f"tmk{i}_{sid}", [128, 256], F32) for i in range(2)]
                tmb = [sbuf(ph2, f"tmb{i}_{sid}", [128, 256], BF16) for i in range(2)]
                P.op("sp", lambda e: e.dma_start(out=perm[:], in_=rope_perm), writes=["perm"], dma=True)
                for tb in range(2):
                    P.op("sp", lambda e, tb=tb: e.dma_start(out=B.xt[:, :4, :].rearrange("p a t -> p (a t)"), in_=kcache[tb * 128:(tb + 1) * 128, :]),
                         writes=[("xt", k) for k in range(KC)], dma=True)
                    for k in range(KC):
                        bk, bkk = banks[k % 4], BK[k % 4]
                        P.op("pe", lambda e, bk=bk, k=k: e.transpose(
                            bk[:, 0:128], B.xt[:, :4, :].rearrange("p a t -> p (a t)")[:, k * 128:(k + 1) * 128], ident[:]),
                            reads=[("xt", 0), "ident"], writes=[bkk])
                        i = cnt["ev"]; cnt["ev"] += 1
                        P.op("act", lambda e, bk=bk, i=i: e.copy(out=B.obf[i % 3][:, 0:128], in_=bk[:, 0:128]), reads=[bkk], writes=[("obf", i % 3)])
                        P.op("pool", lambda e, i=i, k=k, tb=tb: e.dma_start(out=KT[k * 128:(k + 1) * 128, tb * 128:(tb + 1) * 128], in_=B.obf[i % 3][:, 0:128]),
                             reads=[("obf", i % 3)], dma=True)
                    P.op("sp", lambda e, tb=tb: e.dma_start(out=B.xt[:, 4:8, :].rearrange("p a t -> p (a t)"), in_=vcache[tb * 128:(tb + 1) * 128, :]),
                         writes=[("xt", k) for k in range(KC)], dma=True)
                    P.op("dve", lambda e: e.tensor_copy(out=B.hh[:, 0:4, :], in_=B.xt[:, 4:8, :]), reads=[("xt", 0)], writes=[("hh", 0)])
                    P.op("pool", lambda e, tb=tb: e.dma_start(out=VT[tb * 128:(tb + 1) * 128, :], in_=B.hh[:, 0:4, :].rearrange("p a t -> p (a t)")),
                         reads=[("hh", 0)], dma=True)
                for t in (range(NT) if cfg.get("da_p1", 9) > 1 else []):
                    cnd = 0 if t < NTS else 1
                    load_x(B, t)
                    norm_mod(B, l, cnd, 3)
                    if cnd == 0:
                        P.op("sp", lambda e, t=t: e.dma_start(out=rc[:], in_=rope_cos[:, t * TT:(t + 1) * TT]), writes=["rc"], dma=True)
                        P.op("sp", lambda e, t=t: e.dma_start(out=rs[:], in_=rope_sin[:, t * TT:(t + 1) * TT]), writes=["rs"], dma=True)

                    def qk_consume(f, pa, ka, t=t, cnd=cnd, scr=None, coff=0):
                        i = cnt["ev"]; cnt["ev"] += 1
                        ob, obk = B.obf[i % 3], ("obf", i % 3)
                        if cnd == 1:
                            P.op("act", lambda e: e.copy(out=ob[:], in_=pa[:]), reads=[ka], writes=[obk])
                        else:
                            qs, qsk = B.ssb[i % 2], ("ssb", i % 2)
                            t1, t1k = B.tt[i % 2], ("tt", i % 2)
                            pw, pwk = banks[4 + i % 4], BK[4 + i % 4]
                            P.op("act", lambda e: e.copy(out=qs[:], in_=pa[:]), reads=[ka], writes=[qsk])
                            P.op("pe", lambda e: e.matmul(pw[:], lhsT=perm[:], rhs=qs[:], start=True, stop=True), reads=["perm", qsk], writes=[pwk])
                            P.op("dve", lambda e: e.tensor_tensor(out=t1[:], in0=qs[:], in1=rc[:], op=ALU.mult), reads=[qsk, "rc"], writes=[t1k])
                            P.op("dve", lambda e: e.tensor_tensor(out=qs[:], in0=pw[:], in1=rs[:], op=ALU.mult), reads=[pwk, "rs", qsk], writes=[qsk])
                            P.op("dve", lambda e: e.tensor_tensor(out=ob[:], in0=t1[:], in1=qs[:], op=ALU.add), reads=[t1k, qsk], writes=[obk])
                        P.op("pool", lambda e: e.dma_start(out=scr[f * 128:(f + 1) * 128, coff + t * TT:coff + (t + 1) * TT], in_=ob[:]),
                             reads=[obk], dma=True)
                    proj(B, da_wq, KC, lambda k: B.hT[:, k, :], lambda k: ("hT", k), 0, KC,
                         lambda f, pa, ka: qk_consume(f, pa, ka, scr=QT, coff=0))
                    proj(B, da_wk, KC, lambda k: B.hT[:, k, :], lambda k: ("hT", k), 0, KC,
                         lambda f, pa, ka: qk_consume(f, pa, ka, scr=KT, coff=256))
                    for (wap, is_v) in ((da_wv, True), (da_wk, False)):
                        if (not is_v and cnd == 0) or cfg.get("da_p1", 9) < 3:
                            continue
                        for cg in range(8):
                            bw, kw = wload(B, wap, 0, KC, cg * 256, 256)
                            for tb in range(4):
                                it = cnt["it"]; cnt["it"] += 1
                                bk, bkk = banks[it % 4], BK[it % 4]
                                for k in range(KC):
                                    P.op("pe", lambda e, bk=bk, bw=bw, k=k, tb=tb: e.matmul(
                                        bk[:, 0:256], lhsT=B.hT[:, k, tb * 128:(tb + 1) * 128], rhs=bw[:, k, :], start=(k == 0), stop=(k == KC - 1)),
                                        reads=[kw, ("hT", k)], writes=[bkk], sig=(k == KC - 1))
                                i = cnt["ev"]; cnt["ev"] += 1
                                r0 = t * TT + tb * 128
                                P.op("act", lambda e, bk=bk, i=i: e.copy(out=tmk[i % 2][:], in_=bk[:, 0:256]), reads=[bkk], writes=[("tmk", i % 2)])
                                if is_v:
                                    P.op("dve", lambda e, i=i: e.tensor_copy(out=tmb[i % 2][:], in_=tmk[i % 2][:]), reads=[("tmk", i % 2)], writes=[("tmb", i % 2)])
                                    P.op("pool", lambda e, i=i, r0=r0, cg=cg: e.dma_start(out=VT[256 + r0:256 + r0 + 128, cg * 256:(cg + 1) * 256], in_=tmb[i % 2][:]),
                                         reads=[("tmb", i % 2)], dma=True)
                                if cnd == 1:
                                    oo = VO if is_v else KO
                                    p0 = r0 - LS
                                    P.op("pool", lambda e, i=i, p0=p0, cg=cg, oo=oo: e.dma_start(out=oo[p0:p0 + 128, cg * 256:(cg + 1) * 256], in_=tmk[i % 2][:]),
                                         reads=[("tmk", i % 2)], writes=[("oo", is_v)], dma=True)
                if cfg.get("da_p1", 9) >= 3:
                    for r in range(NTP * 4):
                        P.op("sp", lambda e, r=r: e.dma_start(out=o_dk[r * 128:(r + 1) * 128, :], in_=KO[r * 128:(r + 1) * 128, :]), reads=[("oo", False)], dma=True)
                        P.op("sp", lambda e, r=r: e.dma_start(out=o_dv[r * 128:(r + 1) * 128, :], in_=VO[r * 128:(r + 1) * 128, :]), reads=[("oo", True)], dma=True)
                P.barrier()

        def mixer_da_p2():
            SCALE = 1.0 / float(np.sqrt(128.0))
            NKM = (256 + LS) // 128
            with ExitStack() as ph:
                sb_ = lambda name, shape, dt: sbuf(ph, "da_" + name, shape, dt)
                Vh = sb_("Vh", [128, NKM, 256], BF16)
                Kf = [sb_(f"Kf{c}", [128, 256 + LS], BF16) for c in range(2)]
                Qf = [sb_(f"Qf{c}", [128, LS], BF16) for c in range(2)]
                pt = [sb_(f"pt{i}", [128, TT], BF16) for i in range(3)]
                onesb = sb_("onesb", [128, 128], BF16)
                rden = sb_("rden", [128, TT], F32)
                oc0 = [sb_(f"oc0{j}", [128, TT], F32) for j in range(2)]
                dd = [sb_(f"dd{j}", [128, TT], F32) for j in range(2)]
                sqd = [sb_(f"sqd{j}", [128, TT], F32) for j in range(2)]
                rs_ = sb_("rs", [128, TT], F32)
                ob = [sb_(f"ob{j}", [128, TT], BF16) for j in range(2)]
                lc = sb_("lc", [128, 8], F32)
                gc2 = sb_("gc2", [128, 2], F32)
                epsc = sb_("epsc", [128, 1], F32)
                P.op("dve", lambda e: e.memset(epsc[:], EPS), writes=["epsc"])
                P.op("dve", lambda e: e.memset(onesb[:], 1.0), writes=["onesb"])
                for j in range(4):
                    colload(lc[:, j:j + 1], da_lam[j], 1)
                colload(gc2[:], da_subln_g, 2)
                P.op("dve", lambda e: e.tensor_scalar(out=gc2[:], in0=gc2[:], scalar1=1.0 - LAM_INIT, scalar2=None, op0=ALU.mult), reads=["cols"], writes=["cols"])
                P.op("dve", lambda e: e.tensor_tensor(out=lc[:, 4:6], in0=lc[:, 0:4:2], in1=lc[:, 1:4:2], op=ALU.mult), reads=["cols"], writes=["cols"])
                P.op("pe", lambda e: e.matmul(banks[6][:, 0:2], lhsT=ones[:], rhs=lc[:, 4:6], start=True, stop=True), reads=["cols", "ones"], writes=[BK[6]])
                P.op("act", lambda e: e.activation(out=lc[:, 4:6], in_=banks[6][:, 0:2], func=AF.Exp), reads=[BK[6]], writes=["cols"])
                P.op("dve", lambda e: e.tensor_tensor(out=lc[:, 6:7], in0=lc[:, 5:6], in1=lc[:, 4:5], op=ALU.subtract), reads=["cols"], writes=["cols"])
                P.op("dve", lambda e: e.tensor_scalar(out=lc[:, 6:7], in0=lc[:, 6:7], scalar1=-LAM_INIT, scalar2=None, op0=ALU.add), reads=["cols"], writes=["cols"])
                nlam = lc[:, 6:7]
                for (tok0, L, pi) in seqs:
                    k0 = 0 if pi < 0 else 256 + tok0
                    NK = (256 + L) if pi < 0 else L
                    nkc = NK // 128
                    for h in range(8):
                        P.op("sp", lambda e, h=h, k0=k0, NK=NK, nkc=nkc: e.dma_start(
                            out=Vh[:, :nkc, :], in_=VT[k0:k0 + NK, h * 256:(h + 1) * 256].rearrange("(c p) v -> p c v", p=128)),
                            writes=["Vh"], dma=True)
                        for c in range(2):
                            f = h * 2 + c
                            P.op("sp", lambda e, f=f, c=c, k0=k0, NK=NK: e.dma_start(out=Kf[c][:, :NK], in_=KT[f * 128:(f + 1) * 128, k0:k0 + NK]),
                                 writes=[("Kf", c)], dma=True)
                            P.op("sp", lambda e, f=f, c=c, tok0=tok0, L=L: e.dma_start(out=Qf[c][:, :L], in_=QT[f * 128:(f + 1) * 128, tok0:tok0 + L]),
                                 writes=[("Qf", c)], dma=True)
                        for q0 in range(0, L, TT):
                            qw = min(TT, L - q0)
                            for c in range(2):
                                for kc in range(nkc):
                                    bs, bsk = banks[kc % 2], BK[kc % 2]
                                    p_, pk = pt[kc % 3], ("pt", kc % 3)
                                    P.op("pe", lambda e, bs=bs, c=c, kc=kc, q0=q0, qw=qw: e.matmul(
                                        bs[:, :qw], lhsT=Kf[c][:, kc * 128:(kc + 1) * 128], rhs=Qf[c][:, q0:q0 + qw], start=True, stop=True),
                                        reads=[("Kf", c), ("Qf", c)], writes=[bsk])
                                    P.op("act", lambda e, bs=bs, p_=p_, qw=qw: e.activation(out=p_[:, :qw], in_=bs[:, :qw], func=AF.Exp, scale=SCALE),
                                         reads=[bsk], writes=[pk])
                                    last = (kc == nkc - 1)
                                    P.op("pe", lambda e, p_=p_, kc=kc, qw=qw, last=last: e.matmul(
                                        banks[2][:, :qw], lhsT=onesb[:], rhs=p_[:, :qw], start=(kc == 0), stop=last),
                                        reads=["onesb", pk], writes=[BK[2]], sig=False)
                                    for j in range(2):
                                        P.op("pe", lambda e, p_=p_, kc=kc, qw=qw, j=j, last=last: e.matmul(
                                            banks[3 + j][:, :qw], lhsT=Vh[:, kc, j * 128:(j + 1) * 128], rhs=p_[:, :qw], start=(kc == 0), stop=last),
                                            reads=["Vh", pk], writes=[BK[3 + j]], sig=(j == 1))
                                P.op("dve", lambda e, qw=qw: e.reciprocal(out=rden[:, :qw], in_=banks[2][:, :qw]), reads=[BK[2]], writes=["rden"])
                                for j in range(2):
                                    if c == 0:
                                        P.op("dve", lambda e, j=j, qw=qw: e.tensor_tensor(out=oc0[j][:, :qw], in0=banks[3 + j][:, :qw], in1=rden[:, :qw], op=ALU.mult),
                                             reads=[BK[3 + j], "rden"], writes=[("oc0", j)])
                                    else:
                                        P.op("dve", lambda e, j=j, qw=qw: e.tensor_tensor(out=dd[j][:, :qw], in0=banks[3 + j][:, :qw], in1=rden[:, :qw], op=ALU.mult),
                                             reads=[BK[3 + j], "rden"], writes=[("dd", j)])
                                        P.op("dve", lambda e, j=j, qw=qw: e.scalar_tensor_tensor(
                                            out=dd[j][:, :qw], in0=dd[j][:, :qw], scalar=nlam, in1=oc0[j][:, :qw], op0=ALU.mult, op1=ALU.add),
                                            reads=[("dd", j), ("oc0", j), "cols"], writes=[("dd", j)])
                            for j in range(2):
                                P.op("act", lambda e, j=j, qw=qw: e.activation(out=sqd[j][:, :qw], in_=dd[j][:, :qw], func=AF.Square),
                                     reads=[("dd", j)], writes=[("sqd", j)])
                                P.op("pe", lambda e, j=j, qw=qw: e.matmul(banks[5][:, :qw], lhsT=ones[:], rhs=sqd[j][:, :qw], start=(j == 0), stop=(j == 1)),
                                     reads=["ones", ("sqd", j)], writes=[BK[5]], sig=True)
                            P.op("act", lambda e, qw=qw: e.activation(out=rs_[:, :qw], in_=banks[5][:, :qw], func=AF.Sqrt, scale=1.0 / 256, bias=epsc[:]),
                                 reads=[BK[5], "epsc"], writes=["rs"])
                            P.op("dve", lambda e, qw=qw: e.reciprocal(out=rs_[:, :qw], in_=rs_[:, :qw]), reads=["rs"], writes=["rs"])
                            for j in range(2):
                                P.op("dve", lambda e, j=j, qw=qw: e.tensor_tensor(out=dd[j][:, :qw], in0=dd[j][:, :qw], in1=rs_[:, :qw], op=ALU.mult),
                                     reads=[("dd", j), "rs"], writes=[("dd", j)])
                                P.op("act", lambda e, j=j, qw=qw: e.activation(out=ob[j][:, :qw], in_=dd[j][:, :qw], func=AF.Identity, scale=gc2[:, j:j + 1]),
                                     reads=[("dd", j), "cols"], writes=[("ob", j)])
                                P.op("pool", lambda e, j=j, qw=qw, h=h, tok0=tok0, q0=q0: e.dma_start(
                                    out=AT[h * 256 + j * 128:h * 256 + (j + 1) * 128, tok0 + q0:tok0 + q0 + qw], in_=ob[j][:, :qw]),
                                    reads=[("ob", j)], dma=True)
                P.barrier()

        if "ml" in MIX:
            MW = 4096
            ml_w_up = dr("ml_w_up", [D, MW]); ml_w_o = dr("ml_w_o", [D, MW]); ml_w_down = dr("ml_w_down", [MW, D])
            ml_conv_w = dr("ml_conv_w", [4, MW]); ml_conv_b = dr("ml_conv_b", [MW])
            ml_wq = dr("ml_wq", [8, 512, 512]); ml_wk = dr("ml_wk", [8, 512, 512]); ml_wv = dr("ml_wv", [8, 512, 512])
            ml_w_gate = dr("ml_w_gate", [3, MW, 32]); ml_b_gate = dr("ml_b_gate", [32])
            ml_b_o = dr("ml_b_o", [MW]); ml_gn = dr("ml_gn", [MW]); ml_skip = dr("ml_skip", [MW])
            st_C = dr("st_C", [2, 8, 512, 512]); st_n = dr("st_n", [2, 8, 512]); st_m = dr("st_m", [2, 8])
            ml_masks = dr("ml_masks", [2, 64, 64])
            o_C = dr("o_C", [NTP * 2, 2, 8, 512, 512], kind="ExternalOutput")
            o_n = dr("o_n", [NTP * 2, 2, 8, 512], kind="ExternalOutput")
            o_m = dr("o_m", [NTP * 2, 2, 8], kind="ExternalOutput")
            MXI = dr("ml_XI", [MW, NTOK], kind="Internal")
            MXC = dr("ml_XC", [MW, NTOK], BF16, kind="Internal"); MXB = dr("ml_XB", [MW, NTOK], BF16, kind="Internal")
            MOG = dr("ml_OG", [MW, NTOK], BF16, kind="Internal")
            MQ = dr("ml_Q", [MW, NTOK], BF16, kind="Internal"); MK = dr("ml_K", [MW, NTOK], BF16, kind="Internal")
            MKT = dr("ml_KT", [NTOK, MW], BF16, kind="Internal"); MVT = dr("ml_VT", [NTOK, MW], BF16, kind="Internal")
            MG = dr("ml_G", [NTOK, 32], kind=("ExternalOutput" if cfg.get("dbg") else "Internal"))
            CELL = dr("ml_CELL", [MW, NTOK], kind="Internal")
            MYT = dr("ml_YT", [MW, NTOK], BF16, kind="Internal")

        def mixer_ml_p1(B, l):
            prep_GS(l, 1, 4, 5, 1.0)
            with ExitStack() as ph2:
                bo = sbuf(ph2, f"mlbo_{scopeid[0]}", [128, 32], F32)
                colload(bo[:], ml_b_o, 32)
                for t in range(NT):
                    cnd = 0 if t < NTS else 1
                    load_x(B, t)
                    norm_mod(B, l, cnd, 3)
                    proj(B, ml_w_up, KC, lambda k: B.hT[:, k, :], lambda k: ("hT", k), 0, 32,
                         lambda f, pa, ka, t=t: store_fm(B, MXI, f, t, pa, ka, AF.Identity, F32))
                    proj(B, ml_w_o, KC, lambda k: B.hT[:, k, :], lambda k: ("hT", k), 0, 32,
                         lambda f, pa, ka, t=t: store_fm(B, MOG, f, t, pa, ka, AF.Sigmoid, BF16, bias=bo[:, f:f + 1]))
                P.barrier()

        def mixer_ml_p1b():
            with ExitStack() as ph:
                sb_ = lambda name, shape, dt: sbuf(ph, "mlb_" + name, shape, dt)
                xi = sb_("xi", [128, LS + 3], F32)
                xcv = sb_("xcv", [128, LS], F32)
                sg = sb_("sg", [128, LS], F32)
                xcb = sb_("xcb", [128, LS], BF16)
                xib = sb_("xib", [128, LS], BF16)
                cols = sb_("cols", [128, 5 * 32], F32)
                for j in range(4):
                    colload(cols[:, j * 32:(j + 1) * 32], ml_conv_w[j], 32)
                colload(cols[:, 128:160], ml_conv_b, 32)
                col = lambda j, n: cols[:, j * 32 + n:j * 32 + n + 1]
                for n in range(32):
                    for (tok0, L, pi) in seqs:
                        P.op("dve", lambda e: e.memset(xi[:, 0:1], 0.0), writes=["xi"])
                        P.op("dve", lambda e, L=L: e.memset(xi[:, L + 1:L + 3], 0.0), writes=["xi"])
                        P.op("sp", lambda e, n=n, tok0=tok0, L=L: e.dma_start(out=xi[:, 1:L + 1], in_=MXI[n * 128:(n + 1) * 128, tok0:tok0 + L]),
                             writes=["xi"], dma=True)
                        P.op("dve", lambda e, n=n, L=L: e.tensor_scalar(out=xcv[:, :L], in0=xi[:, 0:L], scalar1=col(0, n), scalar2=col(4, n),
                                                                        op0=ALU.mult, op1=ALU.add), reads=["xi", "cols"], writes=["xcv"])
                        for j in range(1, 4):
                            P.op("dve", lambda e, n=n, L=L, j=j: e.scalar_tensor_tensor(
                                out=xcv[:, :L], in0=xi[:, j:j + L], scalar=col(j, n), in1=xcv[:, :L], op0=ALU.mult, op1=ALU.add),
                                reads=["xi", "cols", "xcv"], writes=["xcv"])
                        P.op("act", lambda e, L=L: e.activation(out=xcb[:, :L], in_=xcv[:, :L], func=AF.Silu), reads=["xcv"], writes=["xcb"])
                        P.op("pool", lambda e, L=L: e.tensor_copy(out=xib[:, :L], in_=xi[:, 1:L + 1]), reads=["xi"], writes=["xib"])
                        P.op("pool", lambda e, n=n, tok0=tok0, L=L: e.dma_start(out=MXC[n * 128:(n + 1) * 128, tok0:tok0 + L], in_=xcb[:, :L]),
                             reads=["xcb"], dma=True)
                        P.op("pool", lambda e, n=n, tok0=tok0, L=L: e.dma_start(out=MXB[n * 128:(n + 1) * 128, tok0:tok0 + L], in_=xib[:, :L]),
                             reads=["xib"], dma=True)
                P.barrier()

        def mixer_ml_p2a(B):
            KS = 1.0 / float(np.sqrt(512.0))
            with ExitStack() as ph2:
                sid = scopeid[0]
                wg = sbuf(ph2, f"mlwg_{sid}", [128, 3, 32, 32], BF16)
                bg = sbuf(ph2, f"mlbg_{sid}", [128, 32], F32)
                gsb = sbuf(ph2, f"mlgsb_{sid}", [128, 4, 32], F32)
                tmt = [sbuf(ph2, f"mltmt{i}_{sid}", [128, 512], BF16) for i in range(2)]
                for m_ in range(3):
                    P.op("pool", lambda e, m_=m_: e.dma_start(out=wg[:, m_, :, :], in_=ml_w_gate[m_].rearrange("(c p) j -> p c j", p=128)),
                         writes=["wg"], dma=True)
                bsrc = AP(ml_b_gate.tensor, ml_b_gate.offset, [[0, 128], [1, 32]])
                P.op("sp", lambda e: e.dma_start(out=bg[:], in_=bsrc, allow_slow_non_contiguous=True), writes=["bg"], dma=True)
                for t in range(NT):
                    gcount = [0]
                    for h in range(8):
                        P.op("sp", lambda e, t=t, h=h: e.dma_start(
                            out=B.hT[:, 0:4, :], in_=MXC[h * 512:(h + 1) * 512, t * TT:(t + 1) * TT].rearrange("(k p) t -> p k t", p=128)),
                            writes=[("hT", k) for k in range(4)], dma=True)
                        P.op("sp", lambda e, t=t, h=h: e.dma_start(
                            out=B.hT[:, 4:8, :], in_=MXB[h * 512:(h + 1) * 512, t * TT:(t + 1) * TT].rearrange("(k p) t -> p k t", p=128)),
                            writes=[("hT", k) for k in range(4, 8)], dma=True)

                        def fm_consume(f, pa, ka, m_, scr, scale, t=t, h=h):
                            i = cnt["ev"]; cnt["ev"] += 1
                            ob, obk = B.obf[i % 3], ("obf", i % 3)
                            P.op("act", lambda e: e.activation(out=ob[:], in_=pa[:], func=AF.Identity, scale=scale), reads=[ka], writes=[obk])
                            if scr is not None:
                                P.op("pool", lambda e: e.dma_start(out=scr[(h * 4 + f) * 128:(h * 4 + f + 1) * 128, t * TT:(t + 1) * TT], in_=ob[:]),
                                     reads=[obk], dma=True)
                            for tb in range(4):
                                g = gcount[0]
                                P.op("pe", lambda e, tb=tb, g=g: e.matmul(
                                    banks[4 + tb][:, 0:32], lhsT=ob[:, tb * 128:(tb + 1) * 128], rhs=wg[:, m_, h * 4 + f, :],
                                    start=(g == 0), stop=(g == 95)), reads=[obk, "wg"], writes=[BK[4 + tb]], sig=True)
                            gcount[0] += 1
                        src_c = lambda k: B.hT[:, k, :]
                        src_i = lambda k: B.hT[:, 4 + k, :]
                        proj(B, ml_wq[h], 4, src_c, lambda k: ("hT", k), 0, 4, lambda f, pa, ka: fm_consume(f, pa, ka, 0, MQ, 1.0))
                        proj(B, ml_wk[h], 4, src_c, lambda k: ("hT", k), 0, 4, lambda f, pa, ka: fm_consume(f, pa, ka, 1, MK, KS))
                        proj(B, ml_wv[h], 4, src_i, lambda k: ("hT", 4 + k), 0, 4, lambda f, pa, ka: fm_consume(f, pa, ka, 2, None, 1.0))
                        for (wap, off, scr, scale) in ((ml_wk, 0, MKT, KS), (ml_wv, 4, MVT, 1.0)):
                            bw, kw = wload(B, wap[h], 0, 4, 0, 512)
                            for tb in range(4):
                                it = cnt["it"]; cnt["it"] += 1
                                bk, bkk = banks[it % 4], BK[it % 4]
                                for k in range(4):
                                    P.op("pe", lambda e, bk=bk, bw=bw, k=k, tb=tb, off=off: e.matmul(
                                        bk[:], lhsT=B.hT[:, off + k, tb * 128:(tb + 1) * 128], rhs=bw[:, k, :], start=(k == 0), stop=(k == 3)),
                                        reads=[kw, ("hT", off + k)], writes=[bkk], sig=(k == 3))
                                i = cnt["ev"]; cnt["ev"] += 1
                                r0 = t * TT + tb * 128
                                P.op("act", lambda e, bk=bk, i=i, scale=scale: e.activation(out=tmt[i % 2][:], in_=bk[:], func=AF.Identity, scale=scale),
                                     reads=[bkk], writes=[("tmt", i % 2)])
                                P.op("pool", lambda e, i=i, r0=r0, h=h, scr=scr: e.dma_start(out=scr[r0:r0 + 128, h * 512:(h + 1) * 512], in_=tmt[i % 2][:]),
                                     reads=[("tmt", i % 2)], dma=True)
                    for tb in range(4):
                        P.op("dve", lambda e, tb=tb: e.tensor_tensor(out=gsb[:, tb, :], in0=banks[4 + tb][:, 0:32], in1=bg[:], op=ALU.add),
                             reads=[BK[4 + tb], "bg"], writes=["gsb"])
                    P.op("pool", lambda e, t=t: e.dma_start(out=MG[t * TT:(t + 1) * TT, :].rearrange("(b p) j -> p b j", p=128), in_=gsb[:]),
                         reads=["gsb"], dma=True)
                P.barrier()

        def mixer_ml_p2b():
            with ExitStack() as ph:
                sb_ = lambda name, shape, dt: sbuf(ph, "ml2_" + name, shape, dt)
                GGs = sb_("GGs", [128, LS // 128, 32], F32)
                igr = sb_("igr", [128, LS], F32)
                lfr = sb_("lfr", [128, LS], F32)
                grep = [sb_(f"grep{i}", [128, 128], F32) for i in range(2)]
                S = sb_("S", [128, 4, 512], F32)
                Sb = sb_("Sb", [128, 4, 512], BF16)
                nst = sb_("nst", [128, 4], F32)
                nrep = sb_("nrep", [128, 4, 128], BF16)
                mcol = sb_("mcol", [128, 1], F32)
                qT = [sb_(f"qT{i}", [128, 4, TT], BF16) for i in range(2)]
                kT = [sb_(f"kT{i}", [128, 4, TT], BF16) for i in range(2)]
                ktk = [sb_(f"ktk{i}", [64, 8, 512], BF16) for i in range(2)]
                vtk = [sb_(f"vtk{i}", [64, 8, 512], BF16) for i in range(2)]
                hst = [sb_(f"hst{i}", [128, 4, TT], F32) for i in range(2)]
                msk = sb_("msk", [64, 2, 64], F32)
                onesr = sb_("onesr", [128, 64], F32)
                onesb = sb_("onesb", [64, 128], BF16)
                sm = {n: sb_("r_" + n, [128, 64], F32) for n in ("b", "a", "M", "mt", "wi", "em", "dn")}
                c1 = {n: sb_("c_" + n, [128, 1], F32) for n in ("bl", "mn", "wo", "cs", "t")}
                acol = sb_("acol", [64, 1], F32)
                win = sb_("win", [64, 1], F32)
                winb = sb_("winb", [64, 1], BF16)
                wT = sb_("wT", [64, 64], F32)
                sw = sb_("sw", [64, 64], BF16)
                qs = sb_("qs", [128, 4, 64], BF16)
                vs = sb_("vs", [64, 512], BF16)
                cst = sb_("cst", [128, 512], F32)
                P.op("dve", lambda e: e.memset(onesr[:], 1.0), writes=["onesr"])
                P.op("dve", lambda e: e.memset(onesb[:], 1.0), writes=["onesb"])
                P.op("sp", lambda e: e.dma_start(out=msk[:], in_=ml_masks.rearrange("d s t -> s d t")), writes=["msk"], dma=True)

                def rv(ap_, n):
                    return AP(ap_.tensor, ap_.offset + n - 1, [list(ap_.ap[0]), [-1, n]])
                gi = 0
                for (tok0, L, pi) in seqs:
                    nblk = L // 128
                    P.op("sp", lambda e, tok0=tok0, L=L, nblk=nblk: e.dma_start(
                        out=GGs[:, :nblk, :], in_=MG[tok0:tok0 + L, :].rearrange("(b p) j -> p b j", p=128)), writes=["GGs"], dma=True)
                    for d in range(2):
                        rev = (d == 1)
                        for h in range(8):
                            for (jj, dst, dk) in ((d * 16 + h, igr, "igr"), (d * 16 + 8 + h, lfr, "lfr")):
                                for blk in range(nblk):
                                    gp, gpk = grep[blk % 2], ("grep", blk % 2)
                                    src = GGs[:, blk, jj:jj + 1]
                                    bsrc_ = AP(src.tensor, src.offset, [list(src.ap[0]), [0, 128]])
                                    P.op("dve", lambda e, gp=gp, bsrc_=bsrc_: e.tensor_copy(out=gp[:], in_=bsrc_), reads=["GGs"], writes=[gpk])
                                    bk, bkk = banks[1 + (blk // 4) % 2], BK[1 + (blk // 4) % 2]
                                    P.op("pe", lambda e, gp=gp, bk=bk, blk=blk: e.matmul(
                                        bk[:, (blk % 4) * 128:(blk % 4 + 1) * 128], lhsT=gp[:], rhs=ident[:], start=True, stop=True),
                                        reads=[gpk, "ident"], writes=[bkk])
                                    if blk % 4 == 3 or blk == nblk - 1:
                                        b0 = (blk // 4) * 4
                                        w_ = (blk - b0 + 1) * 128
                                        P.op("act", lambda e, bk=bk, dst=dst, b0=b0, w_=w_: e.copy(out=dst[:, b0 * 128:b0 * 128 + w_], in_=bk[:, :w_]),
                                             reads=[bkk], writes=[dk])
                            P.op("act", lambda e, L=L: e.activation(out=lfr[:, :L], in_=lfr[:, :L], func=AF.Exp, scale=-1.0), reads=["lfr"], writes=["lfr"])
                            P.op("act", lambda e, L=L: e.activation(out=lfr[:, :L], in_=lfr[:, :L], func=AF.Ln, bias=1.0), reads=["lfr"], writes=["lfr"])
                            P.op("dve", lambda e, L=L: e.tensor_scalar(out=lfr[:, :L], in0=lfr[:, :L], scalar1=-1.0, scalar2=None, op0=ALU.mult),
                                 reads=["lfr"], writes=["lfr"])
                            if pi < 0:
                                for vc in range(4):
                                    P.op("sp", lambda e, vc=vc, d=d, h=h: e.dma_start(out=cst[:], in_=st_C[d, h, vc * 128:(vc + 1) * 128, :]), writes=["cst"], dma=True)
                                    for kc in range(4):
                                        P.op("pe", lambda e, kc=kc: e.transpose(banks[3][:, kc * 128:(kc + 1) * 128], cst[:, kc * 128:(kc + 1) * 128], ident[:]),
                                             reads=["cst", "ident"], writes=[BK[3]], sig=(kc == 3))
                                    P.op("act", lambda e, vc=vc: e.copy(out=S[:, :, vc * 128:(vc + 1) * 128], in_=banks[3][:].rearrange("p (k v) -> p k v", k=4)),
                                         reads=[BK[3]], writes=["S"])
                                P.op("sp", lambda e, d=d, h=h: e.dma_start(out=nst[:], in_=st_n[d, h].rearrange("(k p) -> p k", p=128),
                                                                           allow_slow_non_contiguous=True), writes=["nst"], dma=True)
                                msrc = AP(st_m.tensor, st_m.offset + d * 8 + h, [[0, 128], [1, 1]])
                                P.op("sp", lambda e, msrc=msrc: e.dma_start(out=mcol[:], in_=msrc, allow_slow_non_contiguous=True), writes=["mcol"], dma=True)
                            else:
                                P.op("dve", lambda e: e.memset(S[:], 0.0), reads=["S"], writes=["S"])
                                P.op("dve", lambda e: e.memset(nst[:], 0.0), reads=["nst"], writes=["nst"])
                                P.op("dve", lambda e: e.memset(mcol[:], 0.0), reads=["mcol"], writes=["mcol"])
                            P.op("act", lambda e: e.copy(out=Sb[:], in_=S[:]), reads=["S"], writes=["Sb"])
                            nsrc = AP(nst[:].tensor, nst[:].offset, [list(nst[:].ap[0]), [1, 4], [0, 128]])
                            P.op("dve", lambda e, nsrc=nsrc: e.tensor_copy(out=nrep[:], in_=nsrc), reads=["nst"], writes=["nrep"])
                            gw = min(TT, L)
                            ng = L // gw
                            for gi_ in (range(ng) if not rev else range(ng - 1, -1, -1)):
                                g0 = tok0 + gi_ * gw
                                bi = gi % 2
                                gi += 1
                                r0, r1 = h * 512, (h + 1) * 512
                                P.op("sp", lambda e, bi=bi, g0=g0, gw=gw, r0=r0, r1=r1: e.dma_start(
                                    out=qT[bi][:, :, :gw], in_=MQ[r0:r1, g0:g0 + gw].rearrange("(k p) t -> p k t", p=128)), writes=[("qT", bi)], dma=True)
                                P.op("sp", lambda e, bi=bi, g0=g0, gw=gw, r0=r0, r1=r1: e.dma_start(
                                    out=kT[bi][:, :, :gw], in_=MK[r0:r1, g0:g0 + gw].rearrange("(k p) t -> p k t", p=128)), writes=[("kT", bi)], dma=True)
                                P.op("sp", lambda e, bi=bi, g0=g0, gw=gw, r0=r0, r1=r1: e.dma_start(
                                    out=ktk[bi][:, :gw // 64, :], in_=MKT[g0:g0 + gw, r0:r1].rearrange("(c p) f -> p c f", p=64)), writes=[("ktk", bi)], dma=True)
                                P.op("sp", lambda e, bi=bi, g0=g0, gw=gw, r0=r0, r1=r1: e.dma_start(
                                    out=vtk[bi][:, :gw // 64, :], in_=MVT[g0:g0 + gw, r0:r1].rearrange("(c p) f -> p c f", p=64)), writes=[("vtk", bi)], dma=True)
                                ncg = gw // 64
                                for c_ in (range(ncg) if not rev else range(ncg - 1, -1, -1)):
                                    o0 = gi_ * gw + c_ * 64
                                    q0 = c_ * 64
                                    V = (lambda ap_: rv(ap_, 64)) if rev else (lambda ap_: ap_)
                                    last = 0 if rev else 63
                                    rk = ["sm"]
                                    def dv(fn, r=(), w=()):
                                        P.op("dve", fn, reads=list(r) + rk, writes=list(w) + rk)
                                    def ac(fn, r=(), w=()):
                                        P.op("act", fn, reads=list(r) + rk, writes=list(w) + rk)
                                    lf_c, ig_c = lfr[:, o0:o0 + 64], igr[:, o0:o0 + 64]
                                    dv(lambda e, lf_c=lf_c, V=V: e.tensor_tensor_scan(out=V(sm["b"][:]), data0=onesr[:], data1=V(lf_c), initial=0.0,
                                                                                      op0=ALU.mult, op1=ALU.add), r=["lfr", "onesr"])
                                    dv(lambda e, ig_c=ig_c: e.tensor_tensor(out=sm["a"][:], in0=ig_c, in1=sm["b"][:], op=ALU.subtract), r=["igr"])
                                    dv(lambda e, V=V: e.tensor_tensor_scan(out=V(sm["M"][:]), data0=V(sm["a"][:]), data1=V(sm["a"][:]), initial=mcol[:],
                                                                           op0=ALU.max, op1=ALU.max), r=["mcol"])
                                    dv(lambda e: e.tensor_tensor(out=sm["mt"][:], in0=sm["b"][:], in1=sm["M"][:], op=ALU.add))
                                    ac(lambda e: e.activation(out=sm["wi"][:], in_=sm["M"][:], func=AF.Exp, scale=-1.0, bias=mcol[:]), r=["mcol"])
                                    ac(lambda e: e.activation(out=sm["em"][:], in_=sm["mt"][:], func=AF.Exp, scale=-1.0))
                                    P.op("pe", lambda e: e.transpose(banks[0][0:64, 128:256], sm["a"][:], ident[:]), reads=rk + ["ident"], writes=[BK[0]])
                                    dv(lambda e: e.tensor_copy(out=acol[:], in_=banks[0][0:64, 128:129]), r=[BK[0]], w=["acol"])
                                    ac(lambda e: e.activation(out=wT[:], in_=sm["M"][0:64, :], func=AF.Exp, scale=-1.0, bias=acol[:]), r=["acol"], w=["wT"])
                                    dv(lambda e, d=d: e.tensor_tensor(out=wT[:], in0=wT[:], in1=msk[:, d, :], op=ALU.mult), r=["msk", "wT"], w=["wT"])
                                    dv(lambda e, last=last: e.tensor_tensor(out=c1["cs"][:], in0=sm["b"][:, last:last + 1], in1=sm["mt"][:, last:last + 1], op=ALU.subtract))
                                    ac(lambda e: e.activation(out=c1["wo"][:], in_=c1["cs"][:], func=AF.Exp, bias=mcol[:]), r=["mcol"])
                                    ac(lambda e: e.activation(out=win[:], in_=acol[:], func=AF.Exp, bias=c1["cs"][0:64, :]), r=["acol"], w=["win"])
                                    dv(lambda e: e.tensor_copy(out=winb[:], in_=win[:]), r=["win"], w=["winb"])
                                    dv(lambda e, last=last: e.tensor_copy(out=mcol[:], in_=sm["mt"][:, last:last + 1]), r=["mcol"], w=["mcol"])
                                    for kc in range(4):
                                        P.op("pe", lambda e, kc=kc, bi=bi, q0=q0: e.matmul(
                                            banks[0][0:64, 0:64], lhsT=kT[bi][:, kc, q0:q0 + 64], rhs=qT[bi][:, kc, q0:q0 + 64], start=(kc == 0), stop=(kc == 3)),
                                            reads=[("kT", bi), ("qT", bi)], writes=[BK[0]], sig=(kc == 3))
                                    dv(lambda e: e.tensor_tensor(out=sw[:], in0=banks[0][0:64, 0:64], in1=wT[:], op=ALU.mult), r=[BK[0], "wT"], w=["sw"])
                                    wib = AP(sm["wi"][:].tensor, sm["wi"][:].offset, [list(sm["wi"][:].ap[0]), [0, 4], [1, 64]])
                                    dv(lambda e, bi=bi, q0=q0, wib=wib: e.tensor_tensor(out=qs[:], in0=qT[bi][:, :, q0:q0 + 64], in1=wib, op=ALU.mult),
                                       r=[("qT", bi)], w=["qs"])
                                    for vc in range(4):
                                        P.op("pe", lambda e, vc=vc, bi=bi, c_=c_: e.matmul(
                                            banks[1][:, vc * 64:(vc + 1) * 64], lhsT=vtk[bi][:, c_, vc * 128:(vc + 1) * 128], rhs=sw[:], start=True, stop=False),
                                            reads=[("vtk", bi), "sw"], writes=[BK[1]], sig=False)
                                        for kc in range(4):
                                            P.op("pe", lambda e, vc=vc, kc=kc: e.matmul(
                                                banks[1][:, vc * 64:(vc + 1) * 64], lhsT=Sb[:, kc, vc * 128:(vc + 1) * 128], rhs=qs[:, kc, :], start=False, stop=(kc == 3)),
                                                reads=["Sb", "qs"], writes=[BK[1]], sig=(kc == 3 and vc == 3))
                                    P.op("pe", lambda e: e.matmul(banks[2][:, 0:64], lhsT=onesb[:], rhs=sw[:], start=True, stop=False),
                                         reads=["onesb", "sw"], writes=[BK[2]], sig=False)
                                    for kc in range(4):
                                        P.op("pe", lambda e, kc=kc: e.matmul(banks[2][:, 0:64], lhsT=nrep[:, kc, :], rhs=qs[:, kc, :], start=False, stop=(kc == 3)),
                                             reads=["nrep", "qs"], writes=[BK[2]], sig=(kc == 3))
                                    ac(lambda e: e.activation(out=sm["dn"][:], in_=banks[2][:, 0:64], func=AF.Abs), r=[BK[2]])
                                    dv(lambda e: e.tensor_tensor(out=sm["dn"][:], in0=sm["dn"][:], in1=sm["em"][:], op=ALU.max))
                                    dv(lambda e: e.reciprocal(out=sm["dn"][:], in_=sm["dn"][:]))
                                    hb = gi_ % 2
                                    dnb = AP(sm["dn"][:].tensor, sm["dn"][:].offset, [list(sm["dn"][:].ap[0]), [0, 4], [1, 64]])
                                    dv(lambda e, hb=hb, q0=q0, dnb=dnb: e.tensor_tensor(
                                        out=hst[hb][:, :, q0:q0 + 64], in0=banks[1][:, 0:256].rearrange("p (v t) -> p v t", v=4), in1=dnb, op=ALU.mult),
                                       r=[BK[1]], w=[("hst", hb)])
                                    ac(lambda e, bi=bi, c_=c_: e.activation(out=vs[:], in_=vtk[bi][:, c_, :], func=AF.Identity, scale=win[:]),
                                       r=[("vtk", bi), "win"], w=["vs"])
                                    for kc in range(4):
                                        P.op("pe", lambda e, kc=kc, bi=bi, c_=c_: e.matmul(
                                            banks[3 + kc][:], lhsT=ktk[bi][:, c_, kc * 128:(kc + 1) * 128], rhs=vs[:], start=True, stop=True),
                                            reads=[("ktk", bi), "vs"], writes=[BK[3 + kc]])
                                        P.op("pe", lambda e, kc=kc, bi=bi, c_=c_: e.matmul(
                                            banks[7][:, kc:kc + 1], lhsT=ktk[bi][:, c_, kc * 128:(kc + 1) * 128], rhs=winb[:], start=True, stop=True),
                                            reads=[("ktk", bi), "winb"], writes=[BK[7]])
                                    for kc in range(4):
                                        dv(lambda e, kc=kc: e.scalar_tensor_tensor(out=S[:, kc, :], in0=S[:, kc, :], scalar=c1["wo"][:], in1=banks[3 + kc][:],
                                                                                   op0=ALU.mult, op1=ALU.add), r=[BK[3 + kc], "S"], w=["S"])
                                    ac(lambda e: e.copy(out=Sb[:], in_=S[:]), r=["S"], w=["Sb"])
                                    dv(lambda e: e.scalar_tensor_tensor(out=nst[:], in0=nst[:], scalar=c1["wo"][:], in1=banks[7][:, 0:4],
                                                                        op0=ALU.mult, op1=ALU.add), r=[BK[7], "nst"], w=["nst"])
                                    dv(lambda e, nsrc=nsrc: e.tensor_copy(out=nrep[:], in_=nsrc), r=["nst"], w=["nrep"])
                                hb = gi_ % 2
                                if d == 0:
                                    P.op("pool", lambda e, hb=hb, g0=g0, gw=gw, r0=r0, r1=r1: e.dma_start(
                                        out=CELL[r0:r1, g0:g0 + gw].rearrange("(v p) t -> p v t", p=128), in_=hst[hb][:, :, :gw]),
                                        reads=[("hst", hb)], writes=[("CELL", h, g0)], dma=True)
                                else:
                                    P.op("pool", lambda e, hb=hb, g0=g0, gw=gw, r0=r0, r1=r1: e.dma_start(
                                        out=CELL[r0:r1, g0:g0 + gw].rearrange("(v p) t -> p v t", p=128), in_=hst[hb][:, :, :gw], accum_op=ALU.add),
                                        reads=[("hst", hb), ("CELL", h, g0)], writes=[("CELL", h, g0)], dma=True)
                            if pi >= 0:
                                for vc in range(4):
                                    for kc in range(4):
                                        P.op("pe", lambda e, vc=vc, kc=kc: e.transpose(banks[3][:, kc * 128:(kc + 1) * 128], S[:, kc, vc * 128:(vc + 1) * 128], ident[:]),
                                             reads=["S", "ident"], writes=[BK[3]], sig=(kc == 3))
                                    P.op("act", lambda e: e.copy(out=cst[:], in_=banks[3][:]), reads=[BK[3]], writes=["cst"])
                                    P.op("pool", lambda e, vc=vc, d=d, h=h, pi=pi: e.dma_start(out=o_C[pi, d, h, vc * 128:(vc + 1) * 128, :], in_=cst[:]),
                                         reads=["cst"], dma=True)
                                P.op("pool", lambda e, d=d, h=h, pi=pi: e.dma_start(out=o_n[pi, d, h].rearrange("(k p) -> p k", p=128), in_=nst[:],
                                                                                    allow_slow_non_contiguous=True), reads=["nst"], dma=True)
                                mdst = AP(o_m.tensor, o_m.offset + (pi * 2 + d) * 8 + h, [[1, 1], [1, 1]])
                                P.op("pool", lambda e, mdst=mdst: e.dma_start(out=mdst, in_=mcol[0:1, :]), reads=["mcol", "sm"], dma=True)
                P.barrier()

        def mixer_ml_p3(B):
            with ExitStack() as ph2:
                sid = scopeid[0]
                gnc = sbuf(ph2, f"mlgn_{sid}", [128, 32], F32)
                skc = sbuf(ph2, f"mlsk_{sid}", [128, 32], F32)
                og = sbuf(ph2, f"mlog_{sid}", [128, 4, TT], BF16)
                xcb = sbuf(ph2, f"mlxc_{sid}", [128, 4, TT], BF16)
                colload(gnc[:], ml_gn, 32)
                colload(skc[:], ml_skip, 32)
                for t in range(NT):
                    for h in range(8):
                        sl = lambda scr: scr[h * 512:(h + 1) * 512, t * TT:(t + 1) * TT].rearrange("(k p) t -> p k t", p=128)
                        P.op("sp", lambda e, sl=sl: e.dma_start(out=B.xt[:, 0:4, :], in_=sl(CELL)), writes=[("xt", k) for k in range(4)], dma=True)
                        P.op("sp", lambda e, sl=sl: e.dma_start(out=og[:], in_=sl(MOG)), writes=["og"], dma=True)
                        P.op("sp", lambda e, sl=sl: e.dma_start(out=xcb[:], in_=sl(MXC)), writes=["xcb"], dma=True)
                        for k in range(4):
                            P.op("dve", lambda e, k=k: e.tensor_tensor(out=B.xt[:, k, :], in0=B.xt[:, k, :], in1=og[:, k, :], op=ALU.mult),
                                 reads=[("xt", k), "og"], writes=[("xt", k)])
                        rms_rstd(B, lambda k: B.xt[:, k, :], lambda k: ("xt", k), 4, 1.0 / 512)
                        for k in range(4):
                            c = h * 4 + k
                            P.op("dve", lambda e, k=k: e.tensor_tensor(out=B.tt[k % 2][:], in0=B.xt[:, k, :], in1=B.rstd[:], op=ALU.mult),
                                 reads=[("xt", k), "rstd"], writes=[("tt", k % 2)])
                            P.op("act", lambda e, k=k, c=c: e.activation(out=B.tt[k % 2][:], in_=B.tt[k % 2][:], func=AF.Identity, scale=gnc[:, c:c + 1]),
                                 reads=[("tt", k % 2), "cols"], writes=[("tt", k % 2)])
                            i = cnt["ev"]; cnt["ev"] += 1
                            P.op("dve", lambda e, k=k, c=c, i=i: e.scalar_tensor_tensor(
                                out=B.obf[i % 3][:], in0=xcb[:, k, :], scalar=skc[:, c:c + 1], in1=B.tt[k % 2][:], op0=ALU.mult, op1=ALU.add),
                                reads=["xcb", "cols", ("tt", k % 2)], writes=[("obf", i % 3)])
                            P.op("pool", lambda e, c=c, i=i, t=t: e.dma_start(out=MYT[c * 128:(c + 1) * 128, t * TT:(t + 1) * TT], in_=B.obf[i % 3][:]),
                                 reads=[("obf", i % 3)], dma=True)
                P.barrier()

        def mixer_out(B, l, scr, nf, wap):
            prep_GS(l, 1, 4, 5, 1.0)
            for t in range(NT):
                cnd = 0 if t < NTS else 1
                P.op("sp", lambda e, t=t: e.dma_start(
                    out=B.hh[:, :nf, :], in_=scr[:, t * TT:(t + 1) * TT].rearrange("(k p) t -> p k t", p=128)),
                    writes=[("hh", f) for f in range(nf)], dma=True)
                down_res(B, t, cnd, lambda f: B.hh[:, f, :], lambda f: ("hh", f), nf, wap)

        for l in range(LAYERS):
            kind = MIX[l]
            with ExitStack() as ph:
                B = tok_scope(ph)
                ffn_half(B, l, 0, fw["ffn1"][0], fw["ffn1"][1], fw["ffn1"][2], 0, 1, 2)
                if kind == "rg":
                    mixer_rg_p1(B, l)
                if kind == "s5":
                    mixer_s5_p1(B, l)
                if kind == "da":
                    mixer_da_p1(B, l)
                if kind == "ml":
                    mixer_ml_p1(B, l)
                P.barrier()
            if kind == "rg":
                mixer_rg_p2()
            if kind == "s5":
                mixer_s5_p2()
            if kind == "da" and cfg.get("da_stop", 9) > 1:
                mixer_da_p2()
            if kind == "ml":
                mixer_ml_p1b()
                with ExitStack() as ph:
                    B = tok_scope(ph)
                    mixer_ml_p2a(B)
                    P.barrier()
                if cfg.get("ml_stop", 9) > 1:
                    mixer_ml_p2b()
            with ExitStack() as ph:
                B = tok_scope(ph)
                if kind == "rg":
                    mixer_out(B, l, YT, KC, rg_w_out)
                if kind == "da" and cfg.get("da_stop", 9) > 2:
                    mixer_out(B, l, AT, KC, da_wo)
                if kind == "ml" and cfg.get("ml_stop", 9) > 2:
                    mixer_ml_p3(B)
                    mixer_out(B, l, MYT, 32, ml_w_down)
                if kind == "s5":
                    mixer_s5_p3(B, l)
                    mixer_out(B, l, Z2, 8, s5_w_out)
                ffn_half(B, l, 2, fw["ffn2"][0], fw["ffn2"][1], fw["ffn2"][2], 6, 7, 8)
                P.barrier()

        with ExitStack() as ph:
            B = NS()
            B.xt = sbuf(ph, "xt2", [128, KC, TT], F32)
            B.sq = [sbuf(ph, f"sq2{i}", [128, TT], F32) for i in range(2)]
            B.rstd = sbuf(ph, "rstd2", [128, TT], F32)
            B.tt = [sbuf(ph, f"tt2{i}", [128, TT], F32) for i in range(2)]
            B.epsc = sbuf(ph, "epsc2", [128, 1], F32)
            P.op("dve", lambda e: e.memset(B.epsc[:], EPS), writes=["epsc"])
            yo = [sbuf(ph, f"yo{i}", [128, D], F32) for i in range(2)]
            xf = sbuf(ph, "xf", [128, KC, TT], F32)
            for t in range(NT):
                load_x(B, t)
                rms_rstd(B, lambda k: B.xt[:, k, :], lambda k: ("xt", k), KC, 1.0 / D)
                for k in range(KC):
                    P.op("dve", lambda e, k=k: e.tensor_tensor(out=B.tt[k % 2][:], in0=B.xt[:, k, :], in1=B.rstd[:], op=ALU.mult),
                         reads=[("xt", k), "rstd"], writes=[("tt", k % 2)])
                    P.op("act", lambda e, k=k: e.activation(out=xf[:, k, :], in_=B.tt[k % 2][:], func=AF.Identity,
                                                            scale=fgcol[:, k:k + 1]),
                         reads=[("tt", k % 2), "fgcol"], writes=[("xf", k)])
                for tb in range(4):
                    yb = yo[tb % 2]
                    for q in range(4):
                        bk = banks[q]
                        for kk in range(4):
                            k = q * 4 + kk
                            P.op("pe", lambda e, bk=bk, kk=kk, k=k, tb=tb: e.transpose(
                                bk[:, kk * 128:(kk + 1) * 128], xf[:, k, tb * 128:(tb + 1) * 128], ident[:]),
                                reads=[("xf", k), "ident"], writes=[BK[q]], sig=(kk == 3))
                        if q % 2 == 0:
                            P.op("act", lambda e, bk=bk, q=q, yb=yb: e.copy(out=yb[:, q * 512:(q + 1) * 512], in_=bk[:]),
                                 reads=[BK[q]], writes=[("yo", tb % 2, q)])
                        else:
                            P.op("dve", lambda e, bk=bk, q=q, yb=yb: e.tensor_copy(out=yb[:, q * 512:(q + 1) * 512], in_=bk[:]),
                                 reads=[BK[q]], writes=[("yo", tb % 2, q)])
                    r0 = t * TT + tb * 128
                    P.op("pool", lambda e, yb=yb, r0=r0: e.dma_start(out=y_out[r0:r0 + 128, :], in_=yb[:]),
                         reads=[("yo", tb % 2, q) for q in range(4)], dma=True)
            P.barrier()
        P.finish()
        print("ops", P.nops, "waits", P.nwaits, flush=True)
    return nc


def _f32(a):
    return np.ascontiguousarray(np.asarray(a, dtype=np.float32))


SHARED = ["ada_w", "ada_b", "norm_g", "ffn1_w1", "ffn1_w3", "ffn1_w2", "ffn2_w1", "ffn2_w3", "ffn2_w2", "final_norm_g",
          "s5_w_in", "s5_w_glu", "s5_w_out", "s5_a_re", "s5_a_im", "s5_log_dt", "s5_b_re", "s5_b_im", "s5_c_re", "s5_c_im", "s5_d", "s5_b_glu",
          "da_wq", "da_wk", "da_wv", "da_wo", "da_lam", "da_subln_g",
          "ml_w_up", "ml_w_o", "ml_w_down", "ml_conv_w", "ml_conv_b", "ml_wq", "ml_wk", "ml_wv", "ml_w_gate", "ml_b_gate",
          "ml_b_o", "ml_gn", "ml_skip", "rg_w_in", "rg_w_gate", "rg_w_out", "rg_conv_w", "rg_conv_b", "rg_wa", "rg_wx", "rg_ba", "rg_bx", "rg_lam"]


_ROPE = {}


def _rope_consts():
    if not _ROPE:
        pos = np.arange(4096)
        row, col = (pos // 64).astype(np.float32), (pos % 64).astype(np.float32)
        inv = (10000.0 ** (-np.arange(0, 64, 2, dtype=np.float32) / 64)).astype(np.float32)
        cos = np.zeros((128, 4096), np.float32)
        sin = np.zeros((128, 4096), np.float32)
        perm = np.zeros((128, 128), np.float32)
        for d in range(128):
            p = row if d < 64 else col
            ang = (p * inv[d % 32]).astype(np.float32)
            cos[d] = np.cos(ang)
            first = (d % 64) < 32
            sin[d] = -np.sin(ang) if first else np.sin(ang)
            perm[d + 32 if first else d - 32, d] = 1.0
        _ROPE.update(rope_cos=cos, rope_sin=sin, rope_perm=perm)
    return _ROPE


def make_in_map(inp, shared, c, ns_tok=4096):
    b = c // 2
    m = dict(shared)
    xs, xp = inp["x_sample"], inp["x_prompt"]
    m["x_in"] = np.concatenate([_f32(xs[b][:ns_tok]), _f32(xp[2 * c]), _f32(xp[2 * c + 1])], axis=0)
    m["cvec"] = np.stack([_f32(inp["c"][b]), _f32(inp["c_ctx"])], axis=0)
    m["st_rg"] = _f32(inp["state_rglru"][b])
    m["kcache"] = _f32(inp["cache_dattn_k"][b]).reshape(256, D)
    m["vcache"] = _f32(inp["cache_dattn_v"][b]).reshape(256, D)
    m.update(_rope_consts())
    m["st_C"] = _f32(inp["state_mlstm_C"][b]); m["st_n"] = _f32(inp["state_mlstm_n"][b]); m["st_m"] = _f32(inp["state_mlstm_m"][b])
    tri = np.tril(np.ones((64, 64), np.float32))
    m["ml_masks"] = np.stack([tri.T.copy(), tri.copy()], axis=0)
    m["st_s5"] = _f32(inp["state_s5"][b])
    return m


def kernel(**inp):
    n = 8
    nc = build_nc({})
    shared = {k: _f32(inp[k]) for k in SHARED}
    in_maps = [make_in_map(inp, shared, c) for c in range(n)]
    res = run_bass_kernel_spmd(nc, in_maps, core_ids=list(range(n)))
    R = res.results
    y_sample = np.stack([R[2 * b]["y_out"][:4096] for b in range(4)], axis=0)
    y_prompt = np.concatenate([R[c]["y_out"][4096:].reshape(2, 256, D) for c in range(n)], axis=0)
    z = lambda *s: np.zeros(s, np.float32)
    o_rg = np.concatenate([R[c]["o_rg"] for c in range(n)], axis=0) if "o_rg" in R[0] else z(16, 2, 2048)
    o_s5 = np.concatenate([R[c]["o_s5"] for c in range(n)], axis=0) if "o_s5" in R[0] else z(16, 2, 2, 64, 64)
    if "o_C" in R[0]:
        o_dk = np.concatenate([R[c]["o_dk"].reshape(2, 256, 8, 2, 128) for c in range(n)], axis=0)
        o_dv = np.concatenate([R[c]["o_dv"].reshape(2, 256, 8, 256) for c in range(n)], axis=0)
        o_C = np.concatenate([R[c]["o_C"] for c in range(n)], axis=0)
        o_n = np.concatenate([R[c]["o_n"] for c in range(n)], axis=0)
        o_m = np.concatenate([R[c]["o_m"] for c in range(n)], axis=0)
        return (y_prompt, y_sample, o_s5, o_rg, o_dk, o_dv, o_C, o_n, o_m)
    if "o_dk" in R[0]:
        o_dk = np.concatenate([R[c]["o_dk"].reshape(2, 256, 8, 2, 128) for c in range(n)], axis=0)
        o_dv = np.concatenate([R[c]["o_dv"].reshape(2, 256, 8, 256) for c in range(n)], axis=0)
        return (y_prompt, y_sample, o_s5, o_rg, o_dk, o_dv, z(16, 2, 8, 512, 512), z(16, 2, 8, 512), z(16, 2, 8))
    return (y_prompt, y_sample, o_s5, o_rg, z(16, 256, 8, 2, 128), z(16, 256, 8, 256),
            z(16, 2, 8, 512, 512), z(16, 2, 8, 512), z(16, 2, 8))
```

```python
import numpy as np
import concourse.bass as bass
import concourse.mybir as mybir
from concourse.bass_utils import run_bass_kernel_spmd
from contextlib import ExitStack

F32 = mybir.dt.float32
BF16 = mybir.dt.bfloat16
I32 = mybir.dt.int32
AF = mybir.ActivationFunctionType
ALU = mybir.AluOpType
AX = mybir.AxisListType
AP = bass.AP

D = 2048
KC = 16
DFF = 5632
FC = 44
TT = 512
EPS = 1e-6
EPOCH = 12000
NDMASEM = 12


class Op:
    __slots__ = ("eng", "dma", "sem", "val", "sig")


class Prog:
    def __init__(self, nc, es):
        self.nc = nc
        self.es = es
        self.engs = {"pe": nc.tensor, "act": nc.scalar, "dve": nc.vector,
                     "pool": nc.gpsimd, "sp": nc.sync}
        self.count = {e: 0 for e in self.engs}
        self.nops = {e: 0 for e in self.engs}
        self.csems = {e: [] for e in self.engs}
        self.known = {e: {} for e in self.engs}
        self.dsems = {}
        self.dnext = {e: 0 for e in self.engs}
        self.last_w = {}
        self.readers = {}
        self.pending = {e: [] for e in self.engs}
        self.nwaits = 0

    def _csem(self, eng, epoch):
        lst = self.csems[eng]
        while len(lst) <= epoch:
            lst.append(self.es.enter_context(self.nc.semaphore(f"c_{eng}_{len(lst)}")))
        return lst[epoch]

    def _dsem(self, eng):
        if eng not in self.dsems:
            self.dsems[eng] = [[self.es.enter_context(self.nc.semaphore(f"d_{eng}_{i}")), 0]
                               for i in range(NDMASEM)]
        i = self.dnext[eng] % NDMASEM
        self.dnext[eng] += 1
        return self.dsems[eng][i]

    def _wait(self, eng, sem, val):
        k = self.known[eng]
        key = id(sem)
        if k.get(key, -1) >= val:
            return
        k[key] = val
        self.engs[eng].wait_ge(sem, val)
        self.nwaits += 1

    def op(self, eng, fn, reads=(), writes=(), dma=False, sig=True):
        o = Op()
        o.eng = eng
        o.dma = dma
        o.sig = sig or dma
        o.sem = None
        o.val = None
        deps = {}
        for r in reads:
            d = self.last_w.get(r)
            if d is not None:
                deps[id(d)] = d
        for w in writes:
            d = self.last_w.get(w)
            if d is not None:
                deps[id(d)] = d
            for d in self.readers.get(w, ()):
                deps[id(d)] = d
        for d in deps.values():
            if d.dma or dma or d.eng != eng or eng != "pe":
                if d.sem is None:
                    raise RuntimeError("dependency on op that never signals")
                self._wait(eng, d.sem, d.val)
        if dma:
            slot = self._dsem(eng)
            if slot[1] > 0:
                self._wait(eng, slot[0], slot[1])
            slot[1] += 16
            o.sem, o.val = slot[0], slot[1]
            fn(self.engs[eng]).then_inc(o.sem, 16)
        else:
            ins = fn(self.engs[eng])
            if o.sig:
                self.count[eng] += 1
                c = self.count[eng]
                ep, v = (c - 1) // EPOCH, (c - 1) % EPOCH + 1
                o.sem, o.val = self._csem(eng, ep), v
                ins.then_inc(o.sem, 1)
                for p in self.pending[eng]:
                    p.sem, p.val = o.sem, o.val
                self.pending[eng] = []
            else:
                self.pending[eng].append(o)
        self.nops[eng] += 1
        for r in reads:
            self.readers.setdefault(r, []).append(o)
        for w in writes:
            self.last_w[w] = o
            self.readers[w] = []
        return o

    def barrier(self):
        for e in self.engs:
            assert not self.pending[e]
        for e in self.engs:
            for e2 in self.engs:
                c = self.count[e2]
                if c > 0:
                    ep, v = (c - 1) // EPOCH, (c - 1) % EPOCH + 1
                    self._wait(e, self.csems[e2][ep], v)
            for slots in self.dsems.values():
                for sem, total in slots:
                    if total > 0:
                        self._wait(e, sem, total)
        self.last_w = {}
        self.readers = {}

    def finish(self):
        self.barrier()


class Ctx:
    pass


def build_nc(cfg):
    NTS = cfg.get("ns_tiles", 8)
    NTP = cfg.get("np_tiles", 1)
    NT = NTS + NTP
    NTOK = NT * TT
    LAYERS = cfg.get("layers", 4)
    nc = bass.Bass("TRN2", target_bir_lowering=False)
    dr = lambda name, shape, dt=F32, kind="ExternalInput": nc.dram_tensor(name, shape, dt, kind=kind).ap()
    x_in = dr("x_in", [NTOK, D])
    cvec = dr("cvec", [2, D])
    ada_w = dr("ada_w", [4, D, 9 * D])
    ada_b = dr("ada_b", [4, 9 * D])
    norm_g = dr("norm_g", [4, 3, D])
    fw = {}
    for f in ("ffn1", "ffn2"):
        fw[f] = (dr(f + "_w1", [4, D, DFF]), dr(f + "_w3", [4, D, DFF]), dr(f + "_w2", [4, DFF, D]))
    final_g = dr("final_norm_g", [D])
    y_out = dr("y_out", [NTOK, D], kind="ExternalOutput")
    xT = dr("xT_scr", [D, NTOK], kind="Internal")

    es = ExitStack()
    with es:
        P = Prog(nc, es)
        sbuf = lambda es_, name, shape, dt: es_.enter_context(nc.sbuf_tensor(name, shape, dt))
        ident = sbuf(es, "ident", [128, 128], F32)
        ones = sbuf(es, "ones", [128, 128], F32)
        modc = sbuf(es, "modc", [128, 4 * 2 * 9 * KC], F32)
        gcol = sbuf(es, "gcol", [128, 4 * 3 * KC], F32)
        fgcol = sbuf(es, "fgcol", [128, KC], F32)
        GS = sbuf(es, "GS", [128, 2 * 2 * 3 * KC], F32)
        banks = [es.enter_context(nc.psum_tensor(f"bank{i}", [128, 512], F32)) for i in range(8)]
        BK = [("bank", i) for i in range(8)]

        def mcol(l, c, v, k):
            i = ((l * 2 + c) * 9 + v) * KC + k
            return modc[:, i:i + 1]

        P.op("dve", lambda e: e.memset(ident[:], 0.0), writes=["ident"])
        P.op("pool", lambda e: e.affine_select(out=ident[:], in_=ident[:], pattern=[[-1, 128]],
                                               compare_op=ALU.not_equal, fill=1.0, base=0,
                                               channel_multiplier=1), reads=["ident"], writes=["ident"])
        P.op("dve", lambda e: e.memset(ones[:], 1.0), writes=["ones"])
        for lj in range(12):
            P.op("sp", lambda e, lj=lj: e.dma_start(out=gcol[:, lj * KC:(lj + 1) * KC],
                                                    in_=norm_g[lj // 3, lj % 3].rearrange("(k p) -> p k", p=128),
                                                    allow_slow_non_contiguous=True), writes=["gcol"], dma=True)
        P.op("sp", lambda e: e.dma_start(out=fgcol[:], in_=final_g.rearrange("(k p) -> p k", p=128),
                                         allow_slow_non_contiguous=True), writes=["fgcol"], dma=True)

        with ExitStack() as ph:
            scT = sbuf(ph, "scT", [128, 2, KC], F32)
            sig0 = sbuf(ph, "sig0", [128, 2, KC], F32)
            abT = sbuf(ph, "abT", [128, 4 * 9 * KC], F32)
            astg = [sbuf(ph, f"astg{i}", [128, KC, 512], F32) for i in range(2)]
            for c in range(2):
                P.op("sp", lambda e, c=c: e.dma_start(out=scT[:, c, :], in_=cvec[c].rearrange("(k p) -> p k", p=128),
                                                      allow_slow_non_contiguous=True), writes=["scT"], dma=True)
            for l in range(4):
                P.op("sp", lambda e, l=l: e.dma_start(out=abT[:, l * 144:(l + 1) * 144],
                                                      in_=ada_b[l].rearrange("(j p) -> p j", p=128),
                                                      allow_slow_non_contiguous=True), writes=["abT"], dma=True)
            P.op("act", lambda e: e.activation(out=sig0[:], in_=scT[:], func=AF.Sigmoid), reads=["scT"], writes=["sig0"])
            P.op("dve", lambda e: e.tensor_tensor(out=scT[:], in0=scT[:], in1=sig0[:], op=ALU.mult),
                 reads=["scT", "sig0"], writes=["scT"])
            blk = 0
            for l in range(LAYERS):
                for jb in range(36):
                    st = astg[blk % 2]
                    skey = ("astg", blk % 2)
                    P.op("sp", lambda e, st=st, l=l, jb=jb: e.dma_start(
                        out=st[:], in_=ada_w[l, :, jb * 512:(jb + 1) * 512].rearrange("(k p) c -> p k c", p=128)),
                        writes=[skey], dma=True)
                    bk = banks[blk % 2]
                    for jj in range(4):
                        for k in range(KC):
                            P.op("pe", lambda e, st=st, bk=bk, jj=jj, k=k: e.matmul(
                                bk[:, jj * 2:jj * 2 + 2], lhsT=st[:, k, jj * 128:(jj + 1) * 128], rhs=scT[:, :, k],
                                start=(k == 0), stop=(k == KC - 1)),
                                reads=[skey, "scT"], writes=[BK[blk % 2]], sig=(k == KC - 1))
                    for c in range(2):
                        j0 = jb * 4
                        v, k0 = j0 // KC, j0 % KC
                        i0 = ((l * 2 + c) * 9 + v) * KC + k0
                        b0 = l * 9 * KC + j0
                        P.op("dve", lambda e, bk=bk, c=c, i0=i0, b0=b0: e.tensor_tensor(
                            out=modc[:, i0:i0 + 4], in0=bk[:, c:8:2], in1=abT[:, b0:b0 + 4], op=ALU.add),
                            reads=[BK[blk % 2], "abT"], writes=["modc"])
                    blk += 1
            P.barrier()

        MIX = cfg.get("mixers", ["s5", "rg", "da", "ml"])
        LS = NTS * TT
        seqs = [(0, LS, -1)] + [(LS + i * 256, 256, i) for i in range(NTP * 2)]
        scopeid = [0]

        def xT_tile_ap(t):
            return xT[:, t * TT:(t + 1) * TT].rearrange("(k p) t -> p k t", p=128)

        def colload(dst, src1d, n):
            P.op("sp", lambda e: e.dma_start(out=dst, in_=src1d.rearrange("(k p) -> p k", p=128),
                                             allow_slow_non_contiguous=True), writes=["cols"], dma=True)

        with ExitStack() as ph:
            xin = [sbuf(ph, f"xin{i}", [128, D], F32) for i in range(4)]
            xo = sbuf(ph, "xo", [128, KC, TT], F32)
            for t in range(NT):
                for tb in range(4):
                    r0 = t * TT + tb * 128
                    P.op("sp", lambda e, tb=tb, r0=r0: e.dma_start(out=xin[tb][:], in_=x_in[r0:r0 + 128, :]),
                         writes=[("xin", tb)], dma=True)
                for k in range(KC):
                    bk = banks[k % 4]
                    for tb in range(4):
                        P.op("pe", lambda e, bk=bk, tb=tb, k=k: e.transpose(
                            bk[:, tb * 128:(tb + 1) * 128], xin[tb][:, k * 128:(k + 1) * 128], ident[:]),
                            reads=[("xin", tb), "ident"], writes=[BK[k % 4]], sig=(tb == 3))
                    if k % 2 == 0:
                        P.op("act", lambda e, bk=bk, k=k: e.copy(out=xo[:, k, :], in_=bk[:]),
                             reads=[BK[k % 4]], writes=[("xo", k)])
                    else:
                        P.op("dve", lambda e, bk=bk, k=k: e.tensor_copy(out=xo[:, k, :], in_=bk[:]),
                             reads=[BK[k % 4]], writes=[("xo", k)])
                P.op("pool", lambda e, t=t: e.dma_start(out=xT_tile_ap(t), in_=xo[:]),
                     reads=[("xo", k) for k in range(KC)], writes=[("xT", t, k) for k in range(KC)], dma=True)
            P.barrier()

        class NS:
            pass

        cnt = {"stg": 0, "wb": 0, "xc": 0, "it": 0, "ev": 0}
        NSTG, NWB = 2, 4

        def tok_scope(ph):
            B = NS()
            sid = scopeid[0]
            scopeid[0] += 1
            sb_ = lambda name, shape, dt: sbuf(ph, f"{name}_{sid}", shape, dt)
            B.xt = sb_("xt", [128, KC, TT], F32)
            B.hT = sb_("hT", [128, KC, TT], BF16)
            B.hh = sb_("hh", [128, FC, TT], BF16)
            B.stg = [sb_(f"stg{i}", [128, 4096], F32) for i in range(NSTG)]
            B.wbs = [sb_(f"wb{i}", [128, 4096], BF16) for i in range(NWB)]
            B.sq = [sb_(f"sq{i}", [128, TT], F32) for i in range(2)]
            B.rstd = sb_("rstd", [128, TT], F32)
            B.tt = [sb_(f"tt{i}", [128, TT], F32) for i in range(2)]
            B.ssb = [sb_(f"ssb{i}", [128, TT], F32) for i in range(2)]
            B.xc = [sb_(f"xc{i}", [128, TT], F32) for i in range(3)]
            B.ost = [sb_(f"ost{i}", [128, TT], F32) for i in range(3)]
            B.obf = [sb_(f"obf{i}", [128, TT], BF16) for i in range(3)]
            B.epsc = sb_("epsc", [128, 1], F32)
            P.op("dve", lambda e: e.memset(B.epsc[:], EPS), writes=["epsc"])
            return B

        class PreW:
            def __init__(self, name, pieces):
                self.off = {}
                o = 0
                for p_ in pieces:
                    self.off[p_] = o
                    o += p_[1] * p_[3]
                self.t = dr(name, [128, o], BF16, kind="Internal")
                self.key = name
                self.pieces = pieces

            def get(self, k0, nk, c0, ncol):
                o = self.off[(k0, nk, c0, ncol)]
                return self.t[:, o:o + nk * ncol]

        UP_PIECES = [(0, KC, g * 256, 256) for g in range(FC // 2)]
        DN_PIECES = [(f0, min(8, FC - f0), dg * 512, 512) for dg in range(4) for f0 in range(0, FC, 8)]

        def wload(B, wap, k0, nk, c0, ncol):
            if isinstance(wap, PreW):
                bi = cnt["wb"] % NWB
                cnt["wb"] += 1
                bv = B.wbs[bi][:, :nk * ncol].rearrange("p (k c) -> p k c", k=nk)
                P.op("sp", lambda e: e.dma_start(out=B.wbs[bi][:, :nk * ncol], in_=wap.get(k0, nk, c0, ncol)),
                     reads=[wap.key], writes=[("wb", bi)], dma=True)
                return bv, ("wb", bi)
            si = cnt["stg"] % NSTG
            cnt["stg"] += 1
            bi = cnt["wb"] % NWB
            cnt["wb"] += 1
            sv = B.stg[si][:, :nk * ncol].rearrange("p (k c) -> p k c", k=nk)
            bv = B.wbs[bi][:, :nk * ncol].rearrange("p (k c) -> p k c", k=nk)
            P.op("sp", lambda e: e.dma_start(
                out=sv, in_=wap[k0 * 128:(k0 + nk) * 128, c0:c0 + ncol].rearrange("(k p) c -> p k c", p=128)),
                writes=[("stg", si)], dma=True)
            P.op("pool", lambda e: e.tensor_copy(out=bv, in_=sv), reads=[("stg", si)], writes=[("wb", bi)])
            return bv, ("wb", bi)

        def load_x(B, t):
            P.op("sp", lambda e: e.dma_start(out=B.xt[:], in_=xT_tile_ap(t)),
                 reads=[("xT", t, k) for k in range(KC)], writes=[("xt", k) for k in range(KC)], dma=True)

        def rms_rstd(B, src, skey, nk, inv_n):
            for k in range(nk):
                P.op("act", lambda e, k=k: e.activation(out=B.sq[k % 2][:], in_=src(k), func=AF.Square),
                     reads=[skey(k)], writes=[("sq", k % 2)])
                P.op("pe", lambda e, k=k: e.matmul(banks[4][:], lhsT=ones[:], rhs=B.sq[k % 2][:],
                                                   start=(k == 0), stop=(k == nk - 1)),
                     reads=["ones", ("sq", k % 2)], writes=[BK[4]], sig=True)
            P.op("act", lambda e: e.activation(out=B.rstd[:], in_=banks[4][:], func=AF.Sqrt,
                                               scale=inv_n, bias=B.epsc[:]),
                 reads=[BK[4], "epsc"], writes=["rstd"])
            P.op("dve", lambda e: e.reciprocal(out=B.rstd[:], in_=B.rstd[:]), reads=["rstd"], writes=["rstd"])

        def prep_GS(l, j, vsc, vg, gmul):
            for c in range(2):
                i0 = ((l * 2 + c) * 9 + vsc) * KC
                g0 = (l * 3 + j) * KC
                o0 = (c * 3 + 0) * KC
                P.op("dve", lambda e, i0=i0, g0=g0, o0=o0: e.scalar_tensor_tensor(
                    out=GS[:, o0:o0 + KC], in0=modc[:, i0:i0 + KC], scalar=1.0, in1=gcol[:, g0:g0 + KC],
                    op0=ALU.add, op1=ALU.mult), reads=["modc", "gcol"], writes=["GS"])
                i1 = ((l * 2 + c) * 9 + vg) * KC
                o1 = (c * 3 + 1) * KC
                P.op("dve", lambda e, i1=i1, o1=o1: e.tensor_scalar(
                    out=GS[:, o1:o1 + KC], in0=modc[:, i1:i1 + KC], scalar1=gmul, scalar2=None, op0=ALU.mult),
                    reads=["modc"], writes=["GS"])

        def norm_mod(B, l, cnd, vsh):
            rms_rstd(B, lambda k: B.xt[:, k, :], lambda k: ("xt", k), KC, 1.0 / D)
            for k in range(KC):
                P.op("dve", lambda e, k=k: e.tensor_tensor(out=B.tt[k % 2][:], in0=B.xt[:, k, :], in1=B.rstd[:], op=ALU.mult),
                     reads=[("xt", k), "rstd"], writes=[("tt", k % 2)])
                gi = (cnd * 3) * KC + k
                P.op("act", lambda e, k=k, gi=gi: e.activation(out=B.hT[:, k, :], in_=B.tt[k % 2][:], func=AF.Identity,
                                                               scale=GS[:, gi:gi + 1], bias=mcol(l, cnd, vsh, k)),
                     reads=[("tt", k % 2), "GS", "modc"], writes=[("hT", k)])

        def proj(B, wap, nk, src, skey, c0, nchunks, consume, pair=None):
            NG = (nchunks + 1) // 2
            pre = {}

            def issue(g):
                nc_ = min(2, nchunks - g * 2)
                a = wload(B, wap, 0, nk, c0 + g * 256, nc_ * 128)
                b = wload(B, pair[0], 0, nk, c0 + g * 256, nc_ * 128) if pair else None
                pre[g] = (a, b, nc_)
            issue(0)
            for g in range(NG):
                if g + 1 < NG:
                    issue(g + 1)
                (b1, k1), bb, nc_ = pre.pop(g)
                for fc in range(nc_):
                    f = g * 2 + fc
                    it = cnt["it"]
                    cnt["it"] += 1
                    pa, pb = banks[(it % 2) * 2], banks[(it % 2) * 2 + 1]
                    ka, kb = BK[(it % 2) * 2], BK[(it % 2) * 2 + 1]
                    for k in range(nk):
                        P.op("pe", lambda e, pa=pa, b1=b1, fc=fc, k=k: e.matmul(
                            pa[:], lhsT=b1[:, k, fc * 128:(fc + 1) * 128], rhs=src(k),
                            start=(k == 0), stop=(k == nk - 1)),
                            reads=[k1, skey(k)], writes=[ka], sig=(k == nk - 1))
                    if pair:
                        b3, k3 = bb
                        for k in range(nk):
                            P.op("pe", lambda e, pb=pb, b3=b3, fc=fc, k=k: e.matmul(
                                pb[:], lhsT=b3[:, k, fc * 128:(fc + 1) * 128], rhs=src(k),
                                start=(k == 0), stop=(k == nk - 1)),
                                reads=[k3, skey(k)], writes=[kb], sig=(k == nk - 1))
                        pair[1](f, pa, ka, pb, kb)
                    else:
                        consume(f, pa, ka)

        def down_res(B, t, cnd, src, skey, nf, wap):
            pieces = [(f0, min(8, nf - f0)) for f0 in range(0, nf, 8)]
            for dg in range(4):
                pre2 = {}

                def issue2(pi, dg=dg):
                    f0, n_ = pieces[pi]
                    pre2[pi] = wload(B, wap, f0, n_, dg * 512, 512)
                issue2(0)
                for pi, (f0, n_) in enumerate(pieces):
                    if pi + 1 < len(pieces):
                        issue2(pi + 1)
                    bw, kw = pre2.pop(pi)
                    for dc in range(4):
                        for fl in range(n_):
                            f = f0 + fl
                            P.op("pe", lambda e, dc=dc, fl=fl, f=f, bw=bw: e.matmul(
                                banks[4 + dc][:], lhsT=bw[:, fl, dc * 128:(dc + 1) * 128], rhs=src(f),
                                start=(f == 0), stop=(f == nf - 1)),
                                reads=[kw, skey(f)], writes=[BK[4 + dc]], sig=(fl == n_ - 1))
                for dc in range(4):
                    kk = dg * 4 + dc
                    xi = cnt["xc"] % 3
                    cnt["xc"] += 1
                    P.op("sp", lambda e, xi=xi, kk=kk: e.dma_start(
                        out=B.xc[xi][:], in_=xT[kk * 128:(kk + 1) * 128, t * TT:(t + 1) * TT]),
                        reads=[("xT", t, kk)], writes=[("xc", xi)], dma=True)
                    gc = (cnd * 3 + 1) * KC + kk
                    P.op("dve", lambda e, xi=xi, dc=dc, gc=gc: e.scalar_tensor_tensor(
                        out=B.ost[xi][:], in0=banks[4 + dc][:], scalar=GS[:, gc:gc + 1], in1=B.xc[xi][:],
                        op0=ALU.mult, op1=ALU.add),
                        reads=[BK[4 + dc], "GS", ("xc", xi)], writes=[("ost", xi)])
                    P.op("pool", lambda e, xi=xi, kk=kk: e.dma_start(
                        out=xT[kk * 128:(kk + 1) * 128, t * TT:(t + 1) * TT], in_=B.ost[xi][:]),
                        reads=[("ost", xi)], writes=[("xT", t, kk)], dma=True)

        def ffn_half(B, l, j, w1, w3, w2, vsh, vsc, vg):
            if cfg.get("no_ffn"):
                return
            prep_GS(l, j, vsc, vg, 0.5)
            for t in range(NT):
                cnd = 0 if t < NTS else 1
                load_x(B, t)
                norm_mod(B, l, cnd, vsh)

                def gate(f, pa, ka, pb, kb):
                    i = cnt["ev"]
                    cnt["ev"] += 1
                    P.op("act", lambda e: e.activation(out=B.ssb[i % 2][:], in_=pa[:], func=AF.Silu),
                         reads=[ka], writes=[("ssb", i % 2)])
                    P.op("dve", lambda e: e.tensor_tensor(out=B.hh[:, f, :], in0=B.ssb[i % 2][:], in1=pb[:], op=ALU.mult),
                         reads=[("ssb", i % 2), kb], writes=[("hh", f)])
                proj(B, w1, KC, lambda k: B.hT[:, k, :], lambda k: ("hT", k), 0, FC, None, pair=(w3, gate))
                down_res(B, t, cnd, lambda f: B.hh[:, f, :], lambda f: ("hh", f), FC, w2)

        def store_fm(B, scr, f, t, bank, bkey, func, dt, bias=None, ncols=TT):
            i = cnt["ev"]
            cnt["ev"] += 1
            tile_ = (B.ost if dt == F32 else B.obf)[i % 3]
            key = ("ost" if dt == F32 else "obf", i % 3)
            kw = {} if bias is None else {"bias": bias}
            P.op("act", lambda e: e.activation(out=tile_[:, :ncols], in_=bank[:, :ncols], func=func, **kw),
                 reads=[bkey, "cols"], writes=[key])
            P.op("pool", lambda e: e.dma_start(out=scr[f * 128:(f + 1) * 128, t * TT:t * TT + ncols], in_=tile_[:, :ncols]),
                 reads=[key], writes=[(scr.name, f, t)], dma=True)

        if "rg" in MIX:
            rg_w_in = dr("rg_w_in", [D, D]); rg_w_gate = dr("rg_w_gate", [D, D]); rg_w_out = dr("rg_w_out", [D, D])
            rg_conv_w = dr("rg_conv_w", [4, D]); rg_conv_b = dr("rg_conv_b", [D])
            rg_wa = dr("rg_wa", [2, 16, 128, 128]); rg_wx = dr("rg_wx", [2, 16, 128, 128])
            rg_ba = dr("rg_ba", [2, D]); rg_bx = dr("rg_bx", [2, D]); rg_lam = dr("rg_lam", [2, D])
            st_rg = dr("st_rg", [2, D])
            o_rg = dr("o_rg", [NTP * 2, 2, D], kind="ExternalOutput")
            XI = dr("rg_XI", [D, NTOK], kind="Internal")
            GT = dr("rg_GT", [D, NTOK], BF16, kind="Internal")
            YT = dr("rg_YT", [D, NTOK], BF16, kind="Internal")

        def mixer_rg_p1(B, l):
            prep_GS(l, 1, 4, 5, 1.0)
            for t in range(NT):
                cnd = 0 if t < NTS else 1
                load_x(B, t)
                norm_mod(B, l, cnd, 3)
                proj(B, rg_w_in, KC, lambda k: B.hT[:, k, :], lambda k: ("hT", k), 0, KC,
                     lambda f, pa, ka, t=t: store_fm(B, XI, f, t, pa, ka, AF.Identity, F32))
                proj(B, rg_w_gate, KC, lambda k: B.hT[:, k, :], lambda k: ("hT", k), 0, KC,
                     lambda f, pa, ka, t=t: store_fm(B, GT, f, t, pa, ka, AF.Gelu_apprx_tanh, BF16))

        def mixer_rg_p2():
            LM = LS
            with ExitStack() as ph:
                sb_ = lambda name, shape, dt: sbuf(ph, "rg2_" + name, shape, dt)
                xi = sb_("xi", [128, LM + 3], F32)
                xcv = sb_("xcv", [128, LM], F32)
                r_ = sb_("r", [128, LM], F32)
                ig = sb_("ig", [128, LM], F32)
                a_ = sb_("a", [128, LM], F32)
                a2 = sb_("a2", [128, LM], F32)
                hs = [sb_(f"hs{i}", [128, LM], F32) for i in range(2)]
                gt = sb_("gt", [128, LM], BF16)
                yg = sb_("yg", [128, LM], BF16)
                wab = sb_("wab", [128, 2, 2, 128], F32)
                cols = sb_("cols", [128, 16 * 16], F32)
                sp8 = sb_("sp8", [128, 2 * KC], F32)
                zc = sb_("zc", [128, 1], F32)
                CW, CB, BA, BX, LAMC, H0 = 0, 4, 5, 7, 9, 11
                for j in range(4):
                    colload(cols[:, (CW + j) * KC:(CW + j + 1) * KC], rg_conv_w[j], KC)
                colload(cols[:, CB * KC:(CB + 1) * KC], rg_conv_b, KC)
                for d in range(2):
                    colload(cols[:, (BA + d) * KC:(BA + d + 1) * KC], rg_ba[d], KC)
                    colload(cols[:, (BX + d) * KC:(BX + d + 1) * KC], rg_bx[d], KC)
                    colload(cols[:, (LAMC + d) * KC:(LAMC + d + 1) * KC], rg_lam[d], KC)
                    colload(cols[:, (H0 + d) * KC:(H0 + d + 1) * KC], st_rg[d], KC)
                P.op("dve", lambda e: e.memset(zc[:], 0.0), writes=["zc"])
                P.op("act", lambda e: e.activation(out=sp8[:], in_=cols[:, LAMC * KC:(LAMC + 2) * KC], func=AF.Exp, scale=-1.0),
                     reads=["cols"], writes=["sp8"])
                P.op("act", lambda e: e.activation(out=sp8[:], in_=sp8[:], func=AF.Ln, bias=1.0), reads=["sp8"], writes=["sp8"])
                P.op("dve", lambda e: e.tensor_scalar(out=sp8[:], in0=sp8[:], scalar1=-8.0, scalar2=None, op0=ALU.mult),
                     reads=["sp8"], writes=["sp8"])
                col = lambda base, n: cols[:, base * KC + n:base * KC + n + 1]
                bki = 0
                for n in range(KC):
                    for d in range(2):
                        P.op("sp", lambda e, n=n, d=d: e.dma_start(out=wab[:, 0, d, :], in_=rg_wa[d, n]), writes=["wab"], dma=True)
                        P.op("sp", lambda e, n=n, d=d: e.dma_start(out=wab[:, 1, d, :], in_=rg_wx[d, n]), writes=["wab"], dma=True)
                    for (tok0, L, pi) in seqs:
                        P.op("dve", lambda e: e.memset(xi[:, 0:1], 0.0), writes=["xi"])
                        P.op("dve", lambda e, L=L: e.memset(xi[:, L + 1:L + 3], 0.0), writes=["xi"])
                        P.op("sp", lambda e, n=n, tok0=tok0, L=L: e.dma_start(
                            out=xi[:, 1:L + 1], in_=XI[n * 128:(n + 1) * 128, tok0:tok0 + L]), writes=["xi"], dma=True)
                        P.op("sp", lambda e, n=n, tok0=tok0, L=L: e.dma_start(
                            out=gt[:, :L], in_=GT[n * 128:(n + 1) * 128, tok0:tok0 + L]), writes=["gt"], dma=True)
                        P.op("dve", lambda e, n=n, L=L: e.tensor_scalar(
                            out=xcv[:, :L], in0=xi[:, 0:L], scalar1=col(CW, n), scalar2=col(CB, n), op0=ALU.mult, op1=ALU.add),
                            reads=["xi", "cols"], writes=["xcv"])
                        for j in range(1, 4):
                            P.op("dve", lambda e, n=n, L=L, j=j: e.scalar_tensor_tensor(
                                out=xcv[:, :L], in0=xi[:, j:j + L], scalar=col(CW + j, n), in1=xcv[:, :L],
                                op0=ALU.mult, op1=ALU.add), reads=["xi", "cols", "xcv"], writes=["xcv"])
                        for d in range(2):
                            for (gi, dst, bcol, dk) in ((0, r_, BA, "r"), (1, ig, BX, "ig")):
                                for c0 in range(0, L, TT):
                                    w = min(TT, L - c0)
                                    bk, bkk = banks[bki % 4], BK[bki % 4]
                                    bki += 1
                                    P.op("pe", lambda e, bk=bk, gi=gi, d=d, c0=c0, w=w: e.matmul(
                                        bk[:, :w], lhsT=wab[:, gi, d, :], rhs=xcv[:, c0:c0 + w], start=True, stop=True),
                                        reads=["wab", "xcv"], writes=[bkk])
                                    P.op("act", lambda e, bk=bk, dst=dst, c0=c0, w=w, bcol=bcol, d=d, n=n: e.activation(
                                        out=dst[:, c0:c0 + w], in_=bk[:, :w], func=AF.Sigmoid, bias=col(bcol + d, n)),
                                        reads=[bkk, "cols"], writes=[dk])
                            P.op("act", lambda e, L=L, d=d, n=n: e.activation(
                                out=a_[:, :L], in_=r_[:, :L], func=AF.Exp, scale=sp8[:, d * KC + n:d * KC + n + 1]),
                                reads=["r", "sp8"], writes=["a"])
                            P.op("dve", lambda e, L=L: e.tensor_tensor(out=a2[:, :L], in0=a_[:, :L], in1=a_[:, :L], op=ALU.mult),
                                 reads=["a"], writes=["a2"])
                            P.op("act", lambda e, L=L: e.activation(out=a2[:, :L], in_=a2[:, :L], func=AF.Sqrt, scale=-1.0, bias=1.0),
                                 reads=["a2"], writes=["a2"])
                            P.op("dve", lambda e, L=L: e.tensor_tensor(out=ig[:, :L], in0=ig[:, :L], in1=xcv[:, :L], op=ALU.mult),
                                 reads=["ig", "xcv"], writes=["ig"])
                            P.op("dve", lambda e, L=L: e.tensor_tensor(out=a2[:, :L], in0=a2[:, :L], in1=ig[:, :L], op=ALU.mult),
                                 reads=["ig", "a2"], writes=["a2"])
                            h0 = col(H0 + d, n) if pi < 0 else zc[:]
                            if d == 0:
                                o_, x0, x1 = hs[0][:, :L], a_[:, :L], a2[:, :L]
                            else:
                                rv = lambda ap_, L=L: AP(ap_.tensor, ap_.offset + L - 1, [list(ap_.ap[0]), [-1, L]])
                                o_, x0, x1 = rv(hs[1][:, :L]), rv(a_[:, :L]), rv(a2[:, :L])
                            P.op("dve", lambda e, o_=o_, x0=x0, x1=x1, h0=h0: e.tensor_tensor_scan(
                                out=o_, data0=x0, data1=x1, initial=h0, op0=ALU.mult, op1=ALU.add),
                                reads=["a", "a2", "cols", "zc"], writes=[("hs", d)])
                            if pi >= 0:
                                fc_ = L - 1 if d == 0 else 0
                                P.op("pool", lambda e, d=d, n=n, pi=pi, fc_=fc_: e.dma_start(
                                    out=o_rg[pi, d, n * 128:(n + 1) * 128].rearrange("(p o) -> p o", o=1),
                                    in_=hs[d][:, fc_:fc_ + 1]), reads=[("hs", d)], dma=True)
                        P.op("dve", lambda e, L=L: e.tensor_tensor(out=hs[0][:, :L], in0=hs[0][:, :L], in1=hs[1][:, :L], op=ALU.add),
                             reads=[("hs", 0), ("hs", 1)], writes=[("hs", 0)])
                        P.op("dve", lambda e, L=L: e.tensor_tensor(out=yg[:, :L], in0=hs[0][:, :L], in1=gt[:, :L], op=ALU.mult),
                             reads=[("hs", 0), "gt"], writes=["yg"])
                        P.op("pool", lambda e, n=n, tok0=tok0, L=L: e.dma_start(
                            out=YT[n * 128:(n + 1) * 128, tok0:tok0 + L], in_=yg[:, :L]), reads=["yg"], dma=True)
                P.barrier()

        if "s5" in MIX:
            s5_w_in = dr("s5_w_in", [D, 1024]); s5_w_glu = dr("s5_w_glu", [1024, 1024]); s5_w_out = dr("s5_w_out", [1024, D])
            s5_a_re = dr("s5_a_re", [2, 64, 64]); s5_a_im = dr("s5_a_im", [2, 64, 64]); s5_log_dt = dr("s5_log_dt", [2, 64])
            s5_b_re = dr("s5_b_re", [2, 64, 64, 16]); s5_b_im = dr("s5_b_im", [2, 64, 64, 16])
            s5_c_re = dr("s5_c_re", [2, 64, 16, 64]); s5_c_im = dr("s5_c_im", [2, 64, 16, 64])
            s5_d = dr("s5_d", [1024]); s5_b_glu = dr("s5_b_glu", [1024])
            st_s5 = dr("st_s5", [2, 2, 64, 64])
            o_s5 = dr("o_s5", [NTP * 2, 2, 2, 64, 64], kind="ExternalOutput")
            UT = dr("s5_UT", [1024, NTOK], kind="Internal")
            YS = dr("s5_YS", [1024, NTOK], kind="Internal")
            ZT = dr("s5_ZT", [1024, NTOK], BF16, kind="Internal")
            Z2 = dr("s5_Z2", [1024, NTOK], BF16, kind="Internal")

        def mixer_s5_p1(B, l):
            prep_GS(l, 1, 4, 5, 1.0)
            for t in range(NT):
                cnd = 0 if t < NTS else 1
                load_x(B, t)
                norm_mod(B, l, cnd, 3)
                proj(B, s5_w_in, KC, lambda k: B.hT[:, k, :], lambda k: ("hT", k), 0, 8,
                     lambda f, pa, ka, t=t: store_fm(B, UT, f, t, pa, ka, AF.Identity, F32))

        def mixer_s5_p2():
            TWO_PI = 6.283185307179586
            with ExitStack() as ph:
                sb_ = lambda name, shape, dt: sbuf(ph, "s5_" + name, shape, dt)
                W = {n: sb_(n, [128, 32, 128], F32) for n in ("WPr", "WPi", "WMr", "WMi")}
                MB = sb_("MB", [128, 8, 4, 2, 128], BF16)
                MC = sb_("MC", [128, 32, 2, 128], BF16)
                Ct = [sb_(f"Ct{i}", [128, 32, 128], F32) for i in range(2)]
                Bt = Ct
                Ht = [sb_(f"Ht{i}", [128, 32, 128], BF16) for i in range(2)]
                T1, T2 = Ct[0], Ct[1]
                TG = [sb_(f"TG{i}", [128, 8, 128], F32) for i in range(2)]
                onesr = sb_("onesr", [128, 128], F32)
                uf = sb_("uf", [128, 8, 128], F32)
                ub = sb_("ub", [128, 8, 128], BF16)
                yf = sb_("yf", [128, 8, 128], F32)
                yo_ = sb_("yo", [128, 8, 128], F32)
                y2 = sb_("y2", [128, 8, 128], F32)
                zb = sb_("zb", [128, 8, 128], BF16)
                dcol = sb_("dcol", [128, 8], F32)
                sm = {n: sb_("sm_" + n, [128, 32], F32) for n in
                      ("are", "aim", "ldt", "dt", "mag", "ang", "sn", "cs", "lr", "li", "den", "lm1", "cr", "ci",
                       "ir", "ii", "t1", "t2", "t3", "pr", "pi", "qr", "qi", "kr", "ki", "hr", "hi", "h0r", "h0i")}
                smi = sb_("smi", [128, 32], I32)
                BB = {n: sb_(n, [128, 32, 16], F32) for n in ("bre", "bim", "bbr", "bbi")}
                XB = Ct[0]
                CN = sb_("CN", [128, 8, 128], F32)
                P.op("dve", lambda e: e.memset(onesr[:], 1.0), writes=["onesr"])
                colload(dcol[:], s5_d, 8)

                def tt(out, a, b, op, eng="dve", r=("tab",), w=("tab",)):
                    P.op(eng, lambda e: e.tensor_tensor(out=out, in0=a, in1=b, op=op), reads=list(r), writes=list(w))

                def ts(out, a, s1, s2, op0, op1=None, r=("tab",), w=("tab",)):
                    if op1 is None:
                        P.op("dve", lambda e: e.tensor_scalar(out=out, in0=a, scalar1=s1, scalar2=None, op0=op0), reads=list(r), writes=list(w))
                    else:
                        P.op("dve", lambda e: e.tensor_scalar(out=out, in0=a, scalar1=s1, scalar2=s2, op0=op0, op1=op1), reads=list(r), writes=list(w))

                def act(out, a, func, r=("tab",), w=("tab",), **kw):
                    P.op("act", lambda e: e.activation(out=out, in_=a, func=func, **kw), reads=list(r), writes=list(w))

                def cmul(outr, outi, ar, ai, br, bi, t1, t2, eng2="dve", r=("tab",), w=("tab",)):
                    tt(t1, ar, br, ALU.mult, r=r, w=w); tt(t2, ai, bi, ALU.mult, r=r, w=w)
                    tt(outr, t1, t2, ALU.subtract, eng=eng2, r=r, w=w)
                    tt(t1, ar, bi, ALU.mult, r=r, w=w); tt(t2, ai, br, ALU.mult, r=r, w=w)
                    tt(outi, t1, t2, ALU.add, eng=eng2, r=r, w=w)

                def sinturn(out, turns):
                    P.op("dve", lambda e: e.tensor_copy(out=smi[:], in_=turns), reads=["tab"], writes=["tab"])
                    P.op("dve", lambda e: e.tensor_copy(out=sm["t2"][:], in_=smi[:]), reads=["tab"], writes=["tab"])
                    tt(sm["t2"][:], turns, sm["t2"][:], ALU.subtract)
                    ts(sm["t2"][:], sm["t2"][:], 0.4999999, -0.4999999, ALU.min, ALU.max)
                    act(out, sm["t2"][:], AF.Sin, scale=TWO_PI)

                def bc(ap2, n):
                    return AP(ap2.tensor, ap2.offset, [list(ap2.ap[0]), list(ap2.ap[1]), [0, n]])

                def sm_load(dst, src2d):
                    P.op("sp", lambda e: e.dma_start(out=dst, in_=src2d.rearrange("g p -> (g p)").rearrange("(s q) -> q s", q=128),
                                                     allow_slow_non_contiguous=True), writes=["tab"], dma=True)

                for d in range(cfg.get("s5_ndir", 2)):
                    sm_load(sm["are"][:], s5_a_re[d]); sm_load(sm["aim"][:], s5_a_im[d])
                    for h in range(2):
                        src = s5_log_dt[d]
                        bsrc = AP(src.tensor, src.offset + h, [[0, 64], [2, 32]])
                        P.op("sp", lambda e, h=h, bsrc=bsrc: e.dma_start(out=sm["ldt"][h * 64:(h + 1) * 64, :], in_=bsrc,
                                                                         allow_slow_non_contiguous=True), writes=["tab"], dma=True)
                    act(sm["dt"][:], sm["ldt"][:], AF.Exp)
                    tt(sm["t1"][:], sm["are"][:], sm["dt"][:], ALU.mult)
                    act(sm["mag"][:], sm["t1"][:], AF.Exp)
                    tt(sm["ang"][:], sm["aim"][:], sm["dt"][:], ALU.mult)
                    ts(sm["ang"][:], sm["ang"][:], 1.0 / TWO_PI, None, ALU.mult)
                    sinturn(sm["sn"][:], sm["ang"][:])
                    ts(sm["t1"][:], sm["ang"][:], 0.25, None, ALU.add)
                    sinturn(sm["cs"][:], sm["t1"][:])
                    tt(sm["lr"][:], sm["mag"][:], sm["cs"][:], ALU.mult)
                    tt(sm["li"][:], sm["mag"][:], sm["sn"][:], ALU.mult)
                    tt(sm["t1"][:], sm["are"][:], sm["are"][:], ALU.mult); tt(sm["t2"][:], sm["aim"][:], sm["aim"][:], ALU.mult)
                    tt(sm["den"][:], sm["t1"][:], sm["t2"][:], ALU.add)
                    P.op("dve", lambda e: e.reciprocal(out=sm["den"][:], in_=sm["den"][:]), reads=["tab"], writes=["tab"])
                    ts(sm["lm1"][:], sm["lr"][:], -1.0, None, ALU.add)
                    tt(sm["t1"][:], sm["lm1"][:], sm["are"][:], ALU.mult); tt(sm["t2"][:], sm["li"][:], sm["aim"][:], ALU.mult)
                    tt(sm["t1"][:], sm["t1"][:], sm["t2"][:], ALU.add); tt(sm["cr"][:], sm["t1"][:], sm["den"][:], ALU.mult)
                    tt(sm["t1"][:], sm["li"][:], sm["are"][:], ALU.mult); tt(sm["t2"][:], sm["lm1"][:], sm["aim"][:], ALU.mult)
                    tt(sm["t1"][:], sm["t1"][:], sm["t2"][:], ALU.subtract); tt(sm["ci"][:], sm["t1"][:], sm["den"][:], ALU.mult)
                    tt(sm["t1"][:], sm["lr"][:], sm["lr"][:], ALU.mult); tt(sm["t2"][:], sm["li"][:], sm["li"][:], ALU.mult)
                    tt(sm["t1"][:], sm["t1"][:], sm["t2"][:], ALU.add)
                    P.op("dve", lambda e: e.reciprocal(out=sm["t1"][:], in_=sm["t1"][:]), reads=["tab"], writes=["tab"])
                    tt(sm["ir"][:], sm["lr"][:], sm["t1"][:], ALU.mult)
                    tt(sm["ii"][:], sm["li"][:], sm["t1"][:], ALU.mult)
                    ts(sm["ii"][:], sm["ii"][:], -1.0, None, ALU.mult)
                    for (Tr, Ti, mr, mi) in ((W["WPr"], W["WPi"], sm["lr"], sm["li"]), (W["WMr"], W["WMi"], sm["ir"], sm["ii"])):
                        P.op("dve", lambda e, Tr=Tr: e.memset(Tr[:, :, 0:1], 1.0), writes=["tab"])
                        P.op("dve", lambda e, Ti=Ti: e.memset(Ti[:, :, 0:1], 0.0), writes=["tab"])
                        P.op("dve", lambda e, mr=mr: e.tensor_copy(out=sm["pr"][:], in_=mr[:]), reads=["tab"], writes=["tab"])
                        P.op("dve", lambda e, mi=mi: e.tensor_copy(out=sm["pi"][:], in_=mi[:]), reads=["tab"], writes=["tab"])
                        for k in range(7):
                            n = 1 << k
                            cmul(Tr[:, :, n:2 * n], Ti[:, :, n:2 * n], Tr[:, :, 0:n], Ti[:, :, 0:n],
                                 bc(sm["pr"][:], n), bc(sm["pi"][:], n), T1[:, :, 0:n], T2[:, :, 0:n])
                            if k < 6:
                                cmul(sm["qr"][:], sm["qi"][:], sm["pr"][:], sm["pi"][:], sm["pr"][:], sm["pi"][:], sm["t1"][:], sm["t2"][:])
                                P.op("dve", lambda e: e.tensor_copy(out=sm["pr"][:], in_=sm["qr"][:]), reads=["tab"], writes=["tab"])
                                P.op("dve", lambda e: e.tensor_copy(out=sm["pi"][:], in_=sm["qi"][:]), reads=["tab"], writes=["tab"])
                    for (nm, src) in (("bre", s5_b_re), ("bim", s5_b_im)):
                        P.op("sp", lambda e, nm=nm, src=src: e.dma_start(
                            out=BB[nm][:], in_=src[d].rearrange("g p c -> (g p) c").rearrange("(s q) c -> q s c", q=128)),
                            writes=["tab"], dma=True)
                    cmul(BB["bbr"][:], BB["bbi"][:], bc(sm["cr"][:], 16), bc(sm["ci"][:], 16), BB["bre"][:], BB["bim"][:],
                         T1[:, :, 0:16], T2[:, :, 0:16])
                    for ri, nm in enumerate(("bbr", "bbi")):
                        P.op("dve", lambda e: e.memset(XB[:], 0.0), reads=["tab"], writes=["tab"])
                        for r_ in range(4):
                            for h in range(2):
                                off = 16 * ((2 * r_ + h) % 8)
                                P.op("dve", lambda e, r_=r_, h=h, off=off, nm=nm: e.tensor_copy(
                                    out=XB[h * 64:(h + 1) * 64, r_::4, off:off + 16], in_=BB[nm][h * 64:(h + 1) * 64, r_::4, :]),
                                    reads=["tab"], writes=["tab"])
                        for sbi in range(32):
                            bk = banks[sbi % 4]
                            P.op("pe", lambda e, bk=bk, sbi=sbi: e.transpose(bk[:, 0:128], XB[:, sbi, :], ident[:]),
                                 reads=["tab", "ident"], writes=[BK[sbi % 4]])
                            P.op("act", lambda e, bk=bk, sbi=sbi, ri=ri: e.copy(out=MB[:, sbi // 4, sbi % 4, ri, :], in_=bk[:, 0:128]),
                                 reads=[BK[sbi % 4]], writes=["MB"])
                    P.op("dve", lambda e: e.memset(MC[:], 0.0), reads=["MC"], writes=["MC"])
                    for ri, src in enumerate((s5_c_re, s5_c_im)):
                        for hh_ in range(2):
                            P.op("sp", lambda e, src=src, hh_=hh_: e.dma_start(
                                out=CN[:, :, hh_ * 64:(hh_ + 1) * 64],
                                in_=src[d].rearrange("g c p -> (g c) p").rearrange("(k r) p -> r k p", r=128)),
                                writes=["CN"], dma=True)
                        for cc in range(8):
                            bk = banks[cc % 4]
                            P.op("pe", lambda e, bk=bk, cc=cc: e.transpose(bk[:, 0:128], CN[:, cc, :], ident[:]),
                                 reads=["CN", "ident"], writes=[BK[cc % 4]])
                            for sbl in range(4):
                                for h in range(2):
                                    gl = 2 * sbl + h
                                    P.op("act", lambda e, bk=bk, cc=cc, sbl=sbl, h=h, gl=gl, ri=ri: e.activation(
                                        out=MC[h * 64:(h + 1) * 64, cc * 4 + sbl, ri, 16 * gl:16 * gl + 16],
                                        in_=bk[h * 64:(h + 1) * 64, 16 * gl:16 * gl + 16], func=AF.Identity,
                                        scale=(1.0 if ri == 0 else -1.0)),
                                        reads=[BK[cc % 4]], writes=["MC"])
                    P.barrier()
                    for (tok0, L, pi) in (seqs if cfg.get("s5_stop", 9) > 1 else []):
                        nch = L // 128
                        if pi < 0:
                            sm_load(sm["h0r"][:], st_s5[d, 0]); sm_load(sm["h0i"][:], st_s5[d, 1])
                            cmul(sm["kr"][:], sm["ki"][:], sm["lr"][:], sm["li"][:], sm["h0r"][:], sm["h0i"][:], sm["t1"][:], sm["t2"][:],
                                 r=("tab", "K"), w=("tab", "K"))
                        else:
                            P.op("dve", lambda e: e.memset(sm["kr"][:], 0.0), reads=["K"], writes=["K"])
                            P.op("dve", lambda e: e.memset(sm["ki"][:], 0.0), reads=["K"], writes=["K"])
                        order = range(nch) if d == 0 else range(nch - 1, -1, -1)
                        for ci_, ch in enumerate(order):
                            t0 = tok0 + ch * 128
                            P.op("sp", lambda e, t0=t0: e.dma_start(out=uf[:], in_=UT[:, t0:t0 + 128].rearrange("(c p) t -> p c t", p=128)),
                                 writes=["uf"], dma=True)
                            P.op("pool", lambda e: e.tensor_copy(out=ub[:], in_=uf[:]), reads=["uf"], writes=["ub"])
                            if d == 1:
                                P.op("sp", lambda e, t0=t0: e.dma_start(out=yf[:], in_=YS[:, t0:t0 + 128].rearrange("(c p) t -> p c t", p=128)),
                                     writes=["yf"], dma=True)

                            def tv(tile_, g0, ng, rev):
                                a = tile_[:, g0:g0 + ng, :]
                                if not rev:
                                    return a
                                return AP(a.tensor, a.offset + 127, [list(a.ap[0]), list(a.ap[1]), [-1, 128]])
                            rev = (d == 1)
                            for grp in range(8):
                                pr_, pi_ = banks[(grp % 3) * 2], banks[(grp % 3) * 2 + 1]
                                kr_, ki_ = BK[(grp % 3) * 2], BK[(grp % 3) * 2 + 1]
                                for sbl in range(4):
                                    P.op("pe", lambda e, pr_=pr_, grp=grp, sbl=sbl: e.matmul(
                                        pr_[:, sbl * 128:(sbl + 1) * 128], lhsT=MB[:, grp, sbl, 0, :], rhs=ub[:, grp, :], start=True, stop=True),
                                        reads=["MB", "ub"], writes=[kr_], sig=(sbl == 3))
                                for sbl in range(4):
                                    P.op("pe", lambda e, pi_=pi_, grp=grp, sbl=sbl: e.matmul(
                                        pi_[:, sbl * 128:(sbl + 1) * 128], lhsT=MB[:, grp, sbl, 1, :], rhs=ub[:, grp, :], start=True, stop=True),
                                        reads=["MB", "ub"], writes=[ki_], sig=(sbl == 3))
                                g0 = grp * 4
                                tg1 = TG[0][:, (grp % 2) * 4:(grp % 2) * 4 + 4, :]
                                tg2 = TG[1][:, (grp % 2) * 4:(grp % 2) * 4 + 4, :]
                                pr3 = pr_[:].rearrange("p (s t) -> p s t", s=4)
                                pi3 = pi_[:].rearrange("p (s t) -> p s t", s=4)
                                wr, wi = tv(W["WMr"], g0, 4, rev), tv(W["WMi"], g0, 4, rev)
                                kk = ("g", grp % 2)
                                tt(tg1, pr3, wr, ALU.mult, r=(kr_, "tab", kk), w=(kk,))
                                tt(tg2, pi3, wi, ALU.mult, r=(ki_, "tab", kk), w=(kk,))
                                tt(Bt[0][:, g0:g0 + 4, :], tg1, tg2, ALU.subtract, eng="pool", r=(kk, ("C", grp)), w=(kk, ("B", grp)))
                                tt(tg1, pi3, wr, ALU.mult, r=(ki_, "tab", kk), w=(kk,))
                                tt(tg2, pr3, wi, ALU.mult, r=(kr_, "tab", kk), w=(kk,))
                                tt(Bt[1][:, g0:g0 + 4, :], tg1, tg2, ALU.add, eng="pool", r=(kk,), w=(kk, ("B", grp)))
                                for sbi in range(g0, g0 + 4):
                                    for ri in range(2):
                                        o_ = Ct[ri][:, sbi, :]; x_ = Bt[ri][:, sbi, :]; on_ = onesr[:]
                                        if rev:
                                            o_ = AP(o_.tensor, o_.offset + 127, [list(o_.ap[0]), [-1, 128]])
                                            x_ = AP(x_.tensor, x_.offset + 127, [list(x_.ap[0]), [-1, 128]])
                                        kcol = (sm["kr"] if ri == 0 else sm["ki"])[:, sbi:sbi + 1]
                                        P.op("dve", lambda e, o_=o_, x_=x_, on_=on_, kcol=kcol: e.tensor_tensor_scan(
                                            out=o_, data0=on_, data1=x_, initial=kcol, op0=ALU.mult, op1=ALU.add),
                                            reads=[("B", grp), "onesr", "K"], writes=[("C", grp)])
                                wr, wi = tv(W["WPr"], g0, 4, rev), tv(W["WPi"], g0, 4, rev)
                                cr3, ci3 = Ct[0][:, g0:g0 + 4, :], Ct[1][:, g0:g0 + 4, :]
                                tt(tg1, cr3, wr, ALU.mult, r=(("C", grp), "tab", kk), w=(kk,))
                                tt(tg2, ci3, wi, ALU.mult, r=(("C", grp), "tab", kk), w=(kk,))
                                tt(Ht[0][:, g0:g0 + 4, :], tg1, tg2, ALU.subtract, eng="pool", r=(kk, ("H", grp)), w=(kk, ("H", grp)))
                                tt(tg1, ci3, wr, ALU.mult, r=(("C", grp), "tab", kk), w=(kk,))
                                tt(tg2, cr3, wi, ALU.mult, r=(("C", grp), "tab", kk), w=(kk,))
                                tt(Ht[1][:, g0:g0 + 4, :], tg1, tg2, ALU.add, eng="pool", r=(kk,), w=(kk, ("H", grp)))
                            lc = 0 if rev else 127
                            wl = 127
                            Cg = [("C", g) for g in range(8)]
                            cmul(sm["hr"][:], sm["hi"][:], W["WPr"][:, :, wl], W["WPi"][:, :, wl], Ct[0][:, :, lc], Ct[1][:, :, lc],
                                 sm["t1"][:], sm["t2"][:], r=["tab", "K"] + Cg, w=("tab", "K"))
                            cmul(sm["kr"][:], sm["ki"][:], sm["lr"][:], sm["li"][:], sm["hr"][:], sm["hi"][:], sm["t1"][:], sm["t2"][:],
                                 r=("tab", "K"), w=("tab", "K"))
                            if pi >= 0 and ci_ == nch - 1:
                                for ri, nm in enumerate(("hr", "hi")):
                                    P.op("pool", lambda e, ri=ri, nm=nm, pi=pi: e.dma_start(
                                        out=o_s5[pi, d, ri].rearrange("g p -> (g p)").rearrange("(s q) -> q s", q=128), in_=sm[nm][:],
                                        allow_slow_non_contiguous=True), reads=["tab", "K"], dma=True)
                            for cc in range(8):
                                bk, bkk = banks[6 + cc // 4], BK[6 + cc // 4]
                                i_ = 0
                                for sbl in range(4):
                                    for ri in range(2):
                                        sbi = cc * 4 + sbl
                                        P.op("pe", lambda e, bk=bk, cc=cc, sbi=sbi, ri=ri, i_=i_: e.matmul(
                                            bk[:, (cc % 4) * 128:(cc % 4 + 1) * 128], lhsT=MC[:, sbi, ri, :], rhs=Ht[ri][:, sbi, :],
                                            start=(i_ == 0), stop=(i_ == 7)),
                                            reads=["MC", ("H", cc)], writes=[bkk], sig=(i_ == 7))
                                        i_ += 1
                            for hb in range(2):
                                bk, bkk = banks[6 + hb], BK[6 + hb]
                                ysl = (yo_ if d == 0 else y2)[:, hb * 4:(hb + 1) * 4, :]
                                if d == 0:
                                    P.op("act", lambda e, bk=bk, ysl=ysl: e.copy(out=ysl, in_=bk[:].rearrange("p (c t) -> p c t", c=4)),
                                         reads=[bkk], writes=[("yo", hb)])
                                else:
                                    P.op("dve", lambda e, bk=bk, ysl=ysl, hb=hb: e.tensor_tensor(
                                        out=ysl, in0=bk[:].rearrange("p (c t) -> p c t", c=4), in1=yf[:, hb * 4:(hb + 1) * 4, :], op=ALU.add),
                                        reads=[bkk, "yf"], writes=[("y2", hb)])
                            if d == 0:
                                P.op("pool", lambda e, t0=t0: e.dma_start(out=YS[:, t0:t0 + 128].rearrange("(c p) t -> p c t", p=128), in_=yo_[:]),
                                     reads=[("yo", 0), ("yo", 1)], dma=True)
                            else:
                                for cc in range(8):
                                    P.op("dve", lambda e, cc=cc: e.scalar_tensor_tensor(
                                        out=y2[:, cc, :], in0=uf[:, cc, :], scalar=dcol[:, cc:cc + 1], in1=y2[:, cc, :], op0=ALU.mult, op1=ALU.add),
                                        reads=["uf", "cols", ("y2", cc // 4)], writes=[("y2", cc // 4)])
                                P.op("act", lambda e: e.activation(out=zb[:], in_=y2[:], func=AF.Gelu_apprx_tanh),
                                     reads=[("y2", 0), ("y2", 1)], writes=["zb"])
                                P.op("pool", lambda e, t0=t0: e.dma_start(out=ZT[:, t0:t0 + 128].rearrange("(c p) t -> p c t", p=128), in_=zb[:]),
                                     reads=["zb"], dma=True)
                P.barrier()

        def mixer_s5_p3(B, l):
            with ExitStack() as ph2:
                bg = sbuf(ph2, f"bglu_{scopeid[0]}", [128, 8], F32)
                colload(bg[:], s5_b_glu, 8)
                for t in range(NT):
                    P.op("sp", lambda e, t=t: e.dma_start(
                        out=B.hT[:, :8, :], in_=ZT[:, t * TT:(t + 1) * TT].rearrange("(k p) t -> p k t", p=128)),
                        writes=[("hT", k) for k in range(8)], dma=True)

                    def glu(f, pa, ka, t=t):
                        i = cnt["ev"]
                        cnt["ev"] += 1
                        P.op("act", lambda e: e.activation(out=B.ssb[i % 2][:], in_=pa[:], func=AF.Sigmoid, bias=bg[:, f:f + 1]),
                             reads=[ka, "cols"], writes=[("ssb", i % 2)])
                        P.op("dve", lambda e: e.tensor_tensor(out=B.obf[i % 3][:], in0=B.ssb[i % 2][:], in1=B.hT[:, f, :], op=ALU.mult),
                             reads=[("ssb", i % 2), ("hT", f)], writes=[("obf", i % 3)])
                        P.op("pool", lambda e: e.dma_start(out=Z2[f * 128:(f + 1) * 128, t * TT:(t + 1) * TT], in_=B.obf[i % 3][:]),
                             reads=[("obf", i % 3)], dma=True)
                    proj(B, s5_w_glu, 8, lambda k: B.hT[:, k, :], lambda k: ("hT", k), 0, 8, glu)
                P.barrier()

        if "da" in MIX:
            da_wq = dr("da_wq", [D, D]); da_wk = dr("da_wk", [D, D]); da_wv = dr("da_wv", [D, D]); da_wo = dr("da_wo", [D, D])
            da_lam = dr("da_lam", [4, 128]); da_subln_g = dr("da_subln_g", [256])
            kcache = dr("kcache", [256, D]); vcache = dr("vcache", [256, D])
            rope_cos = dr("rope_cos", [128, 4096]); rope_sin = dr("rope_sin", [128, 4096]); rope_perm = dr("rope_perm", [128, 128])
            o_dk = dr("o_dk", [NTP * 512, D], kind="ExternalOutput")
            o_dv = dr("o_dv", [NTP * 512, D], kind="ExternalOutput")
            QT = dr("da_QT", [D, NTOK], BF16, kind="Internal")
            KT = dr("da_KT", [D, 256 + NTOK], BF16, kind="Internal")
            VT = dr("da_VT", [256 + NTOK, D], BF16, kind="Internal")
            AT = dr("da_AT", [D, NTOK], BF16, kind="Internal")
            KO = dr("da_KO", [NTP * 512, D], kind="Internal")
            VO = dr("da_VO", [NTP * 512, D], kind="Internal")
            LAM_INIT = 0.8 - 0.6 * float(np.exp(-0.3 * 2))

        def mixer_da_p1(B, l):
            prep_GS(l, 1, 4, 5, 1.0)
            with ExitStack() as ph2:
                sid = scopeid[0]
                perm = sbuf(ph2, f"perm_{sid}", [128, 128], F32)
                rc = sbuf(ph2, f"rc_{sid}", [128, TT], F32)
                rs = sbuf(ph2, f"rs_{sid}", [128, TT], F32)
                tmk = [sbuf(ph2, f"tmk{i}_{sid}", [128, 256], F32) for i in range(2)]
                tmb = [sbuf(ph2, f"tmb{i}_{sid}", [128, 256], BF16) for i in range(2)]
                P.op("sp", lambda e: e.dma_start(out=perm[:], in_=rope_perm), writes=["perm"], dma=True)
                for tb in range(2):
                    P.op("sp", lambda e, tb=tb: e.dma_start(out=B.xt[:, :4, :].rearrange("p a t -> p (a t)"), in_=kcache[tb * 128:(tb + 1) * 128, :]),
                         writes=[("xt", k) for k in range(KC)], dma=True)
                    for k in range(KC):
                        bk, bkk = banks[k % 4], BK[k % 4]
                        P.op("pe", lambda e, bk=bk, k=k: e.transpose(
                            bk[:, 0:128], B.xt[:, :4, :].rearrange("p a t -> p (a t)")[:, k * 128:(k + 1) * 128], ident[:]),
                            reads=[("xt", 0), "ident"], writes=[bkk])
                        i = cnt["ev"]; cnt["ev"] += 1
                        P.op("act", lambda e, bk=bk, i=i: e.copy(out=B.obf[i % 3][:, 0:128], in_=bk[:, 0:128]), reads=[bkk], writes=[("obf", i % 3)])
                        P.op("pool", lambda e, i=i, k=k, tb=tb: e.dma_start(out=KT[k * 128:(k + 1) * 128, tb * 128:(tb + 1) * 128], in_=B.obf[i % 3][:, 0:128]),
                             reads=[("obf", i % 3)], dma=True)
                    P.op("sp", lambda e, tb=tb: e.dma_start(out=B.xt[:, 4:8, :].rearrange("p a t -> p (a t)"), in_=vcache[tb * 128:(tb + 1) * 128, :]),
                         writes=[("xt", k) for k in range(KC)], dma=True)
                    P.op("dve", lambda e: e.tensor_copy(out=B.hh[:, 0:4, :], in_=B.xt[:, 4:8, :]), reads=[("xt", 0)], writes=[("hh", 0)])
                    P.op("pool", lambda e, tb=tb: e.dma_start(out=VT[tb * 128:(tb + 1) * 128, :], in_=B.hh[:, 0:4, :].rearrange("p a t -> p (a t)")),
                         reads=[("hh", 0)], dma=True)
                for t in (range(NT) if cfg.get("da_p1", 9) > 1 else []):
                    cnd = 0 if t < NTS else 1
                    load_x(B, t)
                    norm_mod(B, l, cnd, 3)
                    if cnd == 0:
                        P.op("sp", lambda e, t=t: e.dma_start(out=rc[:], in_=rope_cos[:, t * TT:(t + 1) * TT]), writes=["rc"], dma=True)
                        P.op("sp", lambda e, t=t: e.dma_start(out=rs[:], in_=rope_sin[:, t * TT:(t + 1) * TT]), writes=["rs"], dma=True)

                    def qk_consume(f, pa, ka, t=t, cnd=cnd, scr=None, coff=0):
                        i = cnt["ev"]; cnt["ev"] += 1
                        ob, obk = B.obf[i % 3], ("obf", i % 3)
                        if cnd == 1:
                            P.op("act", lambda e: e.copy(out=ob[:], in_=pa[:]), reads=[ka], writes=[obk])
                        else:
                            qs, qsk = B.ssb[i % 2], ("ssb", i % 2)
                            t1, t1k = B.tt[i % 2], ("tt", i % 2)
                            pw, pwk = banks[4 + i % 4], BK[4 + i % 4]
                            P.op("act", lambda e: e.copy(out=qs[:], in_=pa[:]), reads=[ka], writes=[qsk])
                            P.op("pe", lambda e: e.matmul(pw[:], lhsT=perm[:], rhs=qs[:], start=True, stop=True), reads=["perm", qsk], writes=[pwk])
                            P.op("dve", lambda e: e.tensor_tensor(out=t1[:], in0=qs[:], in1=rc[:], op=ALU.mult), reads=[qsk, "rc"], writes=[t1k])
                            P.op("dve", lambda e: e.tensor_tensor(out=qs[:], in0=pw[:], in1=rs[:], op=ALU.mult), reads=[pwk, "rs", qsk], writes=[qsk])
                            P.op("dve", lambda e: e.tensor_tensor(out=ob[:], in0=t1[:], in1=qs[:], op=ALU.add), reads=[t1k, qsk], writes=[obk])
                        P.op("pool", lambda e: e.dma_start(out=scr[f * 128:(f + 1) * 128, coff + t * TT:coff + (t + 1) * TT], in_=ob[:]),
                             reads=[obk], dma=True)
                    proj(B, da_wq, KC, lambda k: B.hT[:, k, :], lambda k: ("hT", k), 0, KC,
                         lambda f, pa, ka: qk_consume(f, pa, ka, scr=QT, coff=0))
                    proj(B, da_wk, KC, lambda k: B.hT[:, k, :], lambda k: ("hT", k), 0, KC,
                         lambda f, pa, ka: qk_consume(f, pa, ka, scr=KT, coff=256))
                    for (wap, is_v) in ((da_wv, True), (da_wk, False)):
                        if (not is_v and cnd == 0) or cfg.get("da_p1", 9) < 3:
                            continue
                        for cg in range(8):
                            bw, kw = wload(B, wap, 0, KC, cg * 256, 256)
                            for tb in range(4):
                                it = cnt["it"]; cnt["it"] += 1
                                bk, bkk = banks[it % 4], BK[it % 4]
                                for k in range(KC):
                                    P.op("pe", lambda e, bk=bk, bw=bw, k=k, tb=tb: e.matmul(
                                        bk[:, 0:256], lhsT=B.hT[:, k, tb * 128:(tb + 1) * 128], rhs=bw[:, k, :], start=(k == 0), stop=(k == KC - 1)),
                                        reads=[kw, ("hT", k)], writes=[bkk], sig=(k == KC - 1))
                                i = cnt["ev"]; cnt["ev"] += 1
                                r0 = t * TT + tb * 128
                                P.op("act", lambda e, bk=bk, i=i: e.copy(out=tmk[i % 2][:], in_=bk[:, 0:256]), reads=[bkk], writes=[("tmk", i % 2)])
                                if is_v:
                                    P.op("dve", lambda e, i=i: e.tensor_copy(out=tmb[i % 2][:], in_=tmk[i % 2][:]), reads=[("tmk", i % 2)], writes=[("tmb", i % 2)])
                                    P.op("pool", lambda e, i=i, r0=r0, cg=cg: e.dma_start(out=VT[256 + r0:256 + r0 + 128, cg * 256:(cg + 1) * 256], in_=tmb[i % 2][:]),
                                         reads=[("tmb", i % 2)], dma=True)
                                if cnd == 1:
                                    oo = VO if is_v else KO
                                    p0 = r0 - LS
                                    P.op("pool", lambda e, i=i, p0=p0, cg=cg, oo=oo: e.dma_start(out=oo[p0:p0 + 128, cg * 256:(cg + 1) * 256], in_=tmk[i % 2][:]),
                                         reads=[("tmk", i % 2)], writes=[("oo", is_v)], dma=True)
                if cfg.get("da_p1", 9) >= 3:
                    for r in range(NTP * 4):
                        P.op("sp", lambda e, r=r: e.dma_start(out=o_dk[r * 128:(r + 1) * 128, :], in_=KO[r * 128:(r + 1) * 128, :]), reads=[("oo", False)], dma=True)
                        P.op("sp", lambda e, r=r: e.dma_start(out=o_dv[r * 128:(r + 1) * 128, :], in_=VO[r * 128:(r + 1) * 128, :]), reads=[("oo", True)], dma=True)
                P.barrier()

        def mixer_da_p2():
            SCALE = 1.0 / float(np.sqrt(128.0))
            NKM = (256 + LS) // 128
            with ExitStack() as ph:
                sb_ = lambda name, shape, dt: sbuf(ph, "da_" + name, shape, dt)
                Vh = sb_("Vh", [128, NKM, 256], BF16)
                Kf = [sb_(f"Kf{c}", [128, 256 + LS], BF16) for c in range(2)]
                Qf = [sb_(f"Qf{c}", [128, LS], BF16) for c in range(2)]
                pt = [sb_(f"pt{i}", [128, TT], BF16) for i in range(3)]
                onesb = sb_("onesb", [128, 128], BF16)
                rden = sb_("rden", [128, TT], F32)
                oc0 = [sb_(f"oc0{j}", [128, TT], F32) for j in range(2)]
                dd = [sb_(f"dd{j}", [128, TT], F32) for j in range(2)]
                sqd = [sb_(f"sqd{j}", [128, TT], F32) for j in range(2)]
                rs_ = sb_("rs", [128, TT], F32)
                ob = [sb_(f"ob{j}", [128, TT], BF16) for j in range(2)]
                lc = sb_("lc", [128, 8], F32)
                gc2 = sb_("gc2", [128, 2], F32)
                epsc = sb_("epsc", [128, 1], F32)
                P.op("dve", lambda e: e.memset(epsc[:], EPS), writes=["epsc"])
                P.op("dve", lambda e: e.memset(onesb[:], 1.0), writes=["onesb"])
                for j in range(4):
                    colload(lc[:, j:j + 1], da_lam[j], 1)
                colload(gc2[:], da_subln_g, 2)
                P.op("dve", lambda e: e.tensor_scalar(out=gc2[:], in0=gc2[:], scalar1=1.0 - LAM_INIT, scalar2=None, op0=ALU.mult), reads=["cols"], writes=["cols"])
                P.op("dve", lambda e: e.tensor_tensor(out=lc[:, 4:6], in0=lc[:, 0:4:2], in1=lc[:, 1:4:2], op=ALU.mult), reads=["cols"], writes=["cols"])
                P.op("pe", lambda e: e.matmul(banks[6][:, 0:2], lhsT=ones[:], rhs=lc[:, 4:6], start=True, stop=True), reads=["cols", "ones"], writes=[BK[6]])
                P.op("act", lambda e: e.activation(out=lc[:, 4:6], in_=banks[6][:, 0:2], func=AF.Exp), reads=[BK[6]], writes=["cols"])
                P.op("dve", lambda e: e.tensor_tensor(out=lc[:, 6:7], in0=lc[:, 5:6], in1=lc[:, 4:5], op=ALU.subtract), reads=["cols"], writes=["cols"])
                P.op("dve", lambda e: e.tensor_scalar(out=lc[:, 6:7], in0=lc[:, 6:7], scalar1=-LAM_INIT, scalar2=None, op0=ALU.add), reads=["cols"], writes=["cols"])
                nlam = lc[:, 6:7]
                for (tok0, L, pi) in seqs:
                    k0 = 0 if pi < 0 else 256 + tok0
                    NK = (256 + L) if pi < 0 else L
                    nkc = NK // 128
                    for h in range(8):
                        P.op("sp", lambda e, h=h, k0=k0, NK=NK, nkc=nkc: e.dma_start(
                            out=Vh[:, :nkc, :], in_=VT[k0:k0 + NK, h * 256:(h + 1) * 256].rearrange("(c p) v -> p c v", p=128)),
                            writes=["Vh"], dma=True)
                        for c in range(2):
                            f = h * 2 + c
                            P.op("sp", lambda e, f=f, c=c, k0=k0, NK=NK: e.dma_start(out=Kf[c][:, :NK], in_=KT[f * 128:(f + 1) * 128, k0:k0 + NK]),
                                 writes=[("Kf", c)], dma=True)
                            P.op("sp", lambda e, f=f, c=c, tok0=tok0, L=L: e.dma_start(out=Qf[c][:, :L], in_=QT[f * 128:(f + 1) * 128, tok0:tok0 + L]),
                                 writes=[("Qf", c)], dma=True)
                        for q0 in range(0, L, TT):
                            qw = min(TT, L - q0)
                            for c in range(2):
                                for kc in range(nkc):
                                    bs, bsk = banks[kc % 2], BK[kc % 2]
                                    p_, pk = pt[kc % 3], ("pt", kc % 3)
                                    P.op("pe", lambda e, bs=bs, c=c, kc=kc, q0=q0, qw=qw: e.matmul(
                                        bs[:, :qw], lhsT=Kf[c][:, kc * 128:(kc + 1) * 128], rhs=Qf[c][:, q0:q0 + qw], start=True, stop=True),
                                        reads=[("Kf", c), ("Qf", c)], writes=[bsk])
                                    P.op("act", lambda e, bs=bs, p_=p_, qw=qw: e.activation(out=p_[:, :qw], in_=bs[:, :qw], func=AF.Exp, scale=SCALE),
                                         reads=[bsk], writes=[pk])
                                    last = (kc == nkc - 1)
                                    P.op("pe", lambda e, p_=p_, kc=kc, qw=qw, last=last: e.matmul(
                                        banks[2][:, :qw], lhsT=onesb[:], rhs=p_[:, :qw], start=(kc == 0), stop=last),
                                        reads=["onesb", pk], writes=[BK[2]], sig=False)
                                    for j in range(2):
                                        P.op("pe", lambda e, p_=p_, kc=kc, qw=qw, j=j, last=last: e.matmul(
                                            banks[3 + j][:, :qw], lhsT=Vh[:, kc, j * 128:(j + 1) * 128], rhs=p_[:, :qw], start=(kc == 0), stop=last),
                                            reads=["Vh", pk], writes=[BK[3 + j]], sig=(j == 1))
                                P.op("dve", lambda e, qw=qw: e.reciprocal(out=rden[:, :qw], in_=banks[2][:, :qw]), reads=[BK[2]], writes=["rden"])
                                for j in range(2):
                                    if c == 0:
                                        P.op("dve", lambda e, j=j, qw=qw: e.tensor_tensor(out=oc0[j][:, :qw], in0=banks[3 + j][:, :qw], in1=rden[:, :qw], op=ALU.mult),
                                             reads=[BK[3 + j], "rden"], writes=[("oc0", j)])
                                    else:
                                        P.op("dve", lambda e, j=j, qw=qw: e.tensor_tensor(out=dd[j][:, :qw], in0=banks[3 + j][:, :qw], in1=rden[:, :qw], op=ALU.mult),
                                             reads=[BK[3 + j], "rden"], writes=[("dd", j)])
                                        P.op("dve", lambda e, j=j, qw=qw: e.scalar_tensor_tensor(
                                            out=dd[j][:, :qw], in0=dd[j][:, :qw], scalar=nlam, in1=oc0[j][:, :qw], op0=ALU.mult, op1=ALU.add),
                                            reads=[("dd", j), ("oc0", j), "cols"], writes=[("dd", j)])
                            for j in range(2):
                                P.op("act", lambda e, j=j, qw=qw: e.activation(out=sqd[j][:, :qw], in_=dd[j][:, :qw], func=AF.Square),
                                     reads=[("dd", j)], writes=[("sqd", j)])
                                P.op("pe", lambda e, j=j, qw=qw: e.matmul(banks[5][:, :qw], lhsT=ones[:], rhs=sqd[j][:, :qw], start=(j == 0), stop=(j == 1)),
                                     reads=["ones", ("sqd", j)], writes=[BK[5]], sig=True)
                            P.op("act", lambda e, qw=qw: e.activation(out=rs_[:, :qw], in_=banks[5][:, :qw], func=AF.Sqrt, scale=1.0 / 256, bias=epsc[:]),
                                 reads=[BK[5], "epsc"], writes=["rs"])
                            P.op("dve", lambda e, qw=qw: e.reciprocal(out=rs_[:, :qw], in_=rs_[:, :qw]), reads=["rs"], writes=["rs"])
                            for j in range(2):
                                P.op("dve", lambda e, j=j, qw=qw: e.tensor_tensor(out=dd[j][:, :qw], in0=dd[j][:, :qw], in1=rs_[:, :qw], op=ALU.mult),
                                     reads=[("dd", j), "rs"], writes=[("dd", j)])
                                P.op("act", lambda e, j=j, qw=qw: e.activation(out=ob[j][:, :qw], in_=dd[j][:, :qw], func=AF.Identity, scale=gc2[:, j:j + 1]),
                                     reads=[("dd", j), "cols"], writes=[("ob", j)])
                                P.op("pool", lambda e, j=j, qw=qw, h=h, tok0=tok0, q0=q0: e.dma_start(
                                    out=AT[h * 256 + j * 128:h * 256 + (j + 1) * 128, tok0 + q0:tok0 + q0 + qw], in_=ob[j][:, :qw]),
                                    reads=[("ob", j)], dma=True)
                P.barrier()

        if "ml" in MIX:
            MW = 4096
            ml_w_up = dr("ml_w_up", [D, MW]); ml_w_o = dr("ml_w_o", [D, MW]); ml_w_down = dr("ml_w_down", [MW, D])
            ml_conv_w = dr("ml_conv_w", [4, MW]); ml_conv_b = dr("ml_conv_b", [MW])
            ml_wq = dr("ml_wq", [8, 512, 512]); ml_wk = dr("ml_wk", [8, 512, 512]); ml_wv = dr("ml_wv", [8, 512, 512])
            ml_w_gate = dr("ml_w_gate", [3, MW, 32]); ml_b_gate = dr("ml_b_gate", [32])
            ml_b_o = dr("ml_b_o", [MW]); ml_gn = dr("ml_gn", [MW]); ml_skip = dr("ml_skip", [MW])
            st_C = dr("st_C", [2, 8, 512, 512]); st_n = dr("st_n", [2, 8, 512]); st_m = dr("st_m", [2, 8])
            ml_masks = dr("ml_masks", [2, 64, 64])
            o_C = dr("o_C", [NTP * 2, 2, 8, 512, 512], kind="ExternalOutput")
            o_n = dr("o_n", [NTP * 2, 2, 8, 512], kind="ExternalOutput")
            o_m = dr("o_m", [NTP * 2, 2, 8], kind="ExternalOutput")
            MXI = dr("ml_XI", [MW, NTOK], kind="Internal")
            MXC = dr("ml_XC", [MW, NTOK], BF16, kind="Internal"); MXB = dr("ml_XB", [MW, NTOK], BF16, kind="Internal")
            MOG = dr("ml_OG", [MW, NTOK], BF16, kind="Internal")
            MQ = dr("ml_Q", [MW, NTOK], BF16, kind="Internal"); MK = dr("ml_K", [MW, NTOK], BF16, kind="Internal")
            MKT = dr("ml_KT", [NTOK, MW], BF16, kind="Internal"); MVT = dr("ml_VT", [NTOK, MW], BF16, kind="Internal")
            MG = dr("ml_G", [NTOK, 32], kind=("ExternalOutput" if cfg.get("dbg") else "Internal"))
            CELL = dr("ml_CELL", [MW, NTOK], kind="Internal")
            MYT = dr("ml_YT", [MW, NTOK], BF16, kind="Internal")

        def mixer_ml_p1(B, l):
            prep_GS(l, 1, 4, 5, 1.0)
            with ExitStack() as ph2:
                bo = sbuf(ph2, f"mlbo_{scopeid[0]}", [128, 32], F32)
                colload(bo[:], ml_b_o, 32)
                for t in range(NT):
                    cnd = 0 if t < NTS else 1
                    load_x(B, t)
                    norm_mod(B, l, cnd, 3)
                    proj(B, ml_w_up, KC, lambda k: B.hT[:, k, :], lambda k: ("hT", k), 0, 32,
                         lambda f, pa, ka, t=t: store_fm(B, MXI, f, t, pa, ka, AF.Identity, F32))
                    proj(B, ml_w_o, KC, lambda k: B.hT[:, k, :], lambda k: ("hT", k), 0, 32,
                         lambda f, pa, ka, t=t: store_fm(B, MOG, f, t, pa, ka, AF.Sigmoid, BF16, bias=bo[:, f:f + 1]))
                P.barrier()

        def mixer_ml_p1b():
            with ExitStack() as ph:
                sb_ = lambda name, shape, dt: sbuf(ph, "mlb_" + name, shape, dt)
                xi = sb_("xi", [128, LS + 3], F32)
                xcv = sb_("xcv", [128, LS], F32)
                sg = sb_("sg", [128, LS], F32)
                xcb = sb_("xcb", [128, LS], BF16)
                xib = sb_("xib", [128, LS], BF16)
                cols = sb_("cols", [128, 5 * 32], F32)
                for j in range(4):
                    colload(cols[:, j * 32:(j + 1) * 32], ml_conv_w[j], 32)
                colload(cols[:, 128:160], ml_conv_b, 32)
                col = lambda j, n: cols[:, j * 32 + n:j * 32 + n + 1]
                for n in range(32):
                    for (tok0, L, pi) in seqs:
                        P.op("dve", lambda e: e.memset(xi[:, 0:1], 0.0), writes=["xi"])
                        P.op("dve", lambda e, L=L: e.memset(xi[:, L + 1:L + 3], 0.0), writes=["xi"])
                        P.op("sp", lambda e, n=n, tok0=tok0, L=L: e.dma_start(out=xi[:, 1:L + 1], in_=MXI[n * 128:(n + 1) * 128, tok0:tok0 + L]),
                             writes=["xi"], dma=True)
                        P.op("dve", lambda e, n=n, L=L: e.tensor_scalar(out=xcv[:, :L], in0=xi[:, 0:L], scalar1=col(0, n), scalar2=col(4, n),
                                                                        op0=ALU.mult, op1=ALU.add), reads=["xi", "cols"], writes=["xcv"])
                        for j in range(1, 4):
                            P.op("dve", lambda e, n=n, L=L, j=j: e.scalar_tensor_tensor(
                                out=xcv[:, :L], in0=xi[:, j:j + L], scalar=col(j, n), in1=xcv[:, :L], op0=ALU.mult, op1=ALU.add),
                                reads=["xi", "cols", "xcv"], writes=["xcv"])
                        P.op("act", lambda e, L=L: e.activation(out=xcb[:, :L], in_=xcv[:, :L], func=AF.Silu), reads=["xcv"], writes=["xcb"])
                        P.op("pool", lambda e, L=L: e.tensor_copy(out=xib[:, :L], in_=xi[:, 1:L + 1]), reads=["xi"], writes=["xib"])
                        P.op("pool", lambda e, n=n, tok0=tok0, L=L: e.dma_start(out=MXC[n * 128:(n + 1) * 128, tok0:tok0 + L], in_=xcb[:, :L]),
                             reads=["xcb"], dma=True)
                        P.op("pool", lambda e, n=n, tok0=tok0, L=L: e.dma_start(out=MXB[n * 128:(n + 1) * 128, tok0:tok0 + L], in_=xib[:, :L]),
                             reads=["xib"], dma=True)
                P.barrier()

        def mixer_ml_p2a(B):
            KS = 1.0 / float(np.sqrt(512.0))
            with ExitStack() as ph2:
                sid = scopeid[0]
                wg = sbuf(ph2, f"mlwg_{sid}", [128, 3, 32, 32], BF16)
                bg = sbuf(ph2, f"mlbg_{sid}", [128, 32], F32)
                gsb = sbuf(ph2, f"mlgsb_{sid}", [128, 4, 32], F32)
                tmt = [sbuf(ph2, f"mltmt{i}_{sid}", [128, 512], BF16) for i in range(2)]
                for m_ in range(3):
                    P.op("pool", lambda e, m_=m_: e.dma_start(out=wg[:, m_, :, :], in_=ml_w_gate[m_].rearrange("(c p) j -> p c j", p=128)),
                         writes=["wg"], dma=True)
                bsrc = AP(ml_b_gate.tensor, ml_b_gate.offset, [[0, 128], [1, 32]])
                P.op("sp", lambda e: e.dma_start(out=bg[:], in_=bsrc, allow_slow_non_contiguous=True), writes=["bg"], dma=True)
                for t in range(NT):
                    gcount = [0]
                    for h in range(8):
                        P.op("sp", lambda e, t=t, h=h: e.dma_start(
                            out=B.hT[:, 0:4, :], in_=MXC[h * 512:(h + 1) * 512, t * TT:(t + 1) * TT].rearrange("(k p) t -> p k t", p=128)),
                            writes=[("hT", k) for k in range(4)], dma=True)
                        P.op("sp", lambda e, t=t, h=h: e.dma_start(
                            out=B.hT[:, 4:8, :], in_=MXB[h * 512:(h + 1) * 512, t * TT:(t + 1) * TT].rearrange("(k p) t -> p k t", p=128)),
                            writes=[("hT", k) for k in range(4, 8)], dma=True)

                        def fm_consume(f, pa, ka, m_, scr, scale, t=t, h=h):
                            i = cnt["ev"]; cnt["ev"] += 1
                            ob, obk = B.obf[i % 3], ("obf", i % 3)
                            P.op("act", lambda e: e.activation(out=ob[:], in_=pa[:], func=AF.Identity, scale=scale), reads=[ka], writes=[obk])
                            if scr is not None:
                                P.op("pool", lambda e: e.dma_start(out=scr[(h * 4 + f) * 128:(h * 4 + f + 1) * 128, t * TT:(t + 1) * TT], in_=ob[:]),
                                     reads=[obk], dma=True)
                            for tb in range(4):
                                g = gcount[0]
                                P.op("pe", lambda e, tb=tb, g=g: e.matmul(
                                    banks[4 + tb][:, 0:32], lhsT=ob[:, tb * 128:(tb + 1) * 128], rhs=wg[:, m_, h * 4 + f, :],
                                    start=(g == 0), stop=(g == 95)), reads=[obk, "wg"], writes=[BK[4 + tb]], sig=True)
                            gcount[0] += 1
                        src_c = lambda k: B.hT[:, k, :]
                        src_i = lambda k: B.hT[:, 4 + k, :]
                        proj(B, ml_wq[h], 4, src_c, lambda k: ("hT", k), 0, 4, lambda f, pa, ka: fm_consume(f, pa, ka, 0, MQ, 1.0))
                        proj(B, ml_wk[h], 4, src_c, lambda k: ("hT", k), 0, 4, lambda f, pa, ka: fm_consume(f, pa, ka, 1, MK, KS))
                        proj(B, ml_wv[h], 4, src_i, lambda k: ("hT", 4 + k), 0, 4, lambda f, pa, ka: fm_consume(f, pa, ka, 2, None, 1.0))
                        for (wap, off, scr, scale) in ((ml_wk, 0, MKT, KS), (ml_wv, 4, MVT, 1.0)):
                            bw, kw = wload(B, wap[h], 0, 4, 0, 512)
                            for tb in range(4):
                                it = cnt["it"]; cnt["it"] += 1
                                bk, bkk = banks[it % 4], BK[it % 4]
                                for k in range(4):
                                    P.op("pe", lambda e, bk=bk, bw=bw, k=k, tb=tb, off=off: e.matmul(
                                        bk[:], lhsT=B.hT[:, off + k, tb * 128:(tb + 1) * 128], rhs=bw[:, k, :], start=(k == 0), stop=(k == 3)),
                                        reads=[kw, ("hT", off + k)], writes=[bkk], sig=(k == 3))
                                i = cnt["ev"]; cnt["ev"] += 1
                                r0 = t * TT + tb * 128
                                P.op("act", lambda e, bk=bk, i=i, scale=scale: e.activation(out=tmt[i % 2][:], in_=bk[:], func=AF.Identity, scale=scale),
                                     reads=[bkk], writes=[("tmt", i % 2)])
                                P.op("pool", lambda e, i=i, r0=r0, h=h, scr=scr: e.dma_start(out=scr[r0:r0 + 128, h * 512:(h + 1) * 512], in_=tmt[i % 2][:]),
                                     reads=[("tmt", i % 2)], dma=True)
                    for tb in range(4):
                        P.op("dve", lambda e, tb=tb: e.tensor_tensor(out=gsb[:, tb, :], in0=banks[4 + tb][:, 0:32], in1=bg[:], op=ALU.add),
                             reads=[BK[4 + tb], "bg"], writes=["gsb"])
                    P.op("pool", lambda e, t=t: e.dma_start(out=MG[t * TT:(t + 1) * TT, :].rearrange("(b p) j -> p b j", p=128), in_=gsb[:]),
                         reads=["gsb"], dma=True)
                P.barrier()

        def mixer_ml_p2b():
            with ExitStack() as ph:
                sb_ = lambda name, shape, dt: sbuf(ph, "ml2_" + name, shape, dt)
                GGs = sb_("GGs", [128, LS // 128, 32], F32)
                igr = sb_("igr", [128, LS], F32)
                lfr = sb_("lfr", [128, LS], F32)
                grep = [sb_(f"grep{i}", [128, 128], F32) for i in range(2)]
                S = sb_("S", [128, 4, 512], F32)
                Sb = sb_("Sb", [128, 4, 512], BF16)
                nst = sb_("nst", [128, 4], F32)
                nrep = sb_("nrep", [128, 4, 128], BF16)
                mcol = sb_("mcol", [128, 1], F32)
                qT = [sb_(f"qT{i}", [128, 4, TT], BF16) for i in range(2)]
                kT = [sb_(f"kT{i}", [128, 4, TT], BF16) for i in range(2)]
                ktk = [sb_(f"ktk{i}", [64, 8, 512], BF16) for i in range(2)]
                vtk = [sb_(f"vtk{i}", [64, 8, 512], BF16) for i in range(2)]
                hst = [sb_(f"hst{i}", [128, 4, TT], F32) for i in range(2)]
                msk = sb_("msk", [64, 2, 64], F32)
                onesr = sb_("onesr", [128, 64], F32)
                onesb = sb_("onesb", [64, 128], BF16)
                sm = {n: sb_("r_" + n, [128, 64], F32) for n in ("b", "a", "M", "mt", "wi", "em", "dn")}
                c1 = {n: sb_("c_" + n, [128, 1], F32) for n in ("bl", "mn", "wo", "cs", "t")}
                acol = sb_("acol", [64, 1], F32)
                win = sb_("win", [64, 1], F32)
                winb = sb_("winb", [64, 1], BF16)
                wT = sb_("wT", [64, 64], F32)
                sw = sb_("sw", [64, 64], BF16)
                qs = sb_("qs", [128, 4, 64], BF16)
                vs = sb_("vs", [64, 512], BF16)
                cst = sb_("cst", [128, 512], F32)
                P.op("dve", lambda e: e.memset(onesr[:], 1.0), writes=["onesr"])
                P.op("dve", lambda e: e.memset(onesb[:], 1.0), writes=["onesb"])
                P.op("sp", lambda e: e.dma_start(out=msk[:], in_=ml_masks.rearrange("d s t -> s d t")), writes=["msk"], dma=True)

                def rv(ap_, n):
                    return AP(ap_.tensor, ap_.offset + n - 1, [list(ap_.ap[0]), [-1, n]])
                gi = 0
                for (tok0, L, pi) in seqs:
                    nblk = L // 128
                    P.op("sp", lambda e, tok0=tok0, L=L, nblk=nblk: e.dma_start(
                        out=GGs[:, :nblk, :], in_=MG[tok0:tok0 + L, :].rearrange("(b p) j -> p b j", p=128)), writes=["GGs"], dma=True)
                    for d in range(2):
                        rev = (d == 1)
                        for h in range(8):
                            for (jj, dst, dk) in ((d * 16 + h, igr, "igr"), (d * 16 + 8 + h, lfr, "lfr")):
                                for blk in range(nblk):
                                    gp, gpk = grep[blk % 2], ("grep", blk % 2)
                                    src = GGs[:, blk, jj:jj + 1]
                                    bsrc_ = AP(src.tensor, src.offset, [list(src.ap[0]), [0, 128]])
                                    P.op("dve", lambda e, gp=gp, bsrc_=bsrc_: e.tensor_copy(out=gp[:], in_=bsrc_), reads=["GGs"], writes=[gpk])
                                    bk, bkk = banks[1 + (blk // 4) % 2], BK[1 + (blk // 4) % 2]
                                    P.op("pe", lambda e, gp=gp, bk=bk, blk=blk: e.matmul(
                                        bk[:, (blk % 4) * 128:(blk % 4 + 1) * 128], lhsT=gp[:], rhs=ident[:], start=True, stop=True),
                                        reads=[gpk, "ident"], writes=[bkk])
                                    if blk % 4 == 3 or blk == nblk - 1:
                                        b0 = (blk // 4) * 4
                                        w_ = (blk - b0 + 1) * 128
                                        P.op("act", lambda e, bk=bk, dst=dst, b0=b0, w_=w_: e.copy(out=dst[:, b0 * 128:b0 * 128 + w_], in_=bk[:, :w_]),
                                             reads=[bkk], writes=[dk])
                            P.op("act", lambda e, L=L: e.activation(out=lfr[:, :L], in_=lfr[:, :L], func=AF.Exp, scale=-1.0), reads=["lfr"], writes=["lfr"])
                            P.op("act", lambda e, L=L: e.activation(out=lfr[:, :L], in_=lfr[:, :L], func=AF.Ln, bias=1.0), reads=["lfr"], writes=["lfr"])
                            P.op("dve", lambda e, L=L: e.tensor_scalar(out=lfr[:, :L], in0=lfr[:, :L], scalar1=-1.0, scalar2=None, op0=ALU.mult),
                                 reads=["lfr"], writes=["lfr"])
                            if pi < 0:
                                for vc in range(4):
                                    P.op("sp", lambda e, vc=vc, d=d, h=h: e.dma_start(out=cst[:], in_=st_C[d, h, vc * 128:(vc + 1) * 128, :]), writes=["cst"], dma=True)
                                    for kc in range(4):
                                        P.op("pe", lambda e, kc=kc: e.transpose(banks[3][:, kc * 128:(kc + 1) * 128], cst[:, kc * 128:(kc + 1) * 128], ident[:]),
                                             reads=["cst", "ident"], writes=[BK[3]], sig=(kc == 3))
                                    P.op("act", lambda e, vc=vc: e.copy(out=S[:, :, vc * 128:(vc + 1) * 128], in_=banks[3][:].rearrange("p (k v) -> p k v", k=4)),
                                         reads=[BK[3]], writes=["S"])
                                P.op("sp", lambda e, d=d, h=h: e.dma_start(out=nst[:], in_=st_n[d, h].rearrange("(k p) -> p k", p=128),
                                                                           allow_slow_non_contiguous=True), writes=["nst"], dma=True)
                                msrc = AP(st_m.tensor, st_m.offset + d * 8 + h, [[0, 128], [1, 1]])
                                P.op("sp", lambda e, msrc=msrc: e.dma_start(out=mcol[:], in_=msrc, allow_slow_non_contiguous=True), writes=["mcol"], dma=True)
                            else:
                                P.op("dve", lambda e: e.memset(S[:], 0.0), reads=["S"], writes=["S"])
                                P.op("dve", lambda e: e.memset(nst[:], 0.0), reads=["nst"], writes=["nst"])
                                P.op("dve", lambda e: e.memset(mcol[:], 0.0), reads=["mcol"], writes=["mcol"])
                            P.op("act", lambda e: e.copy(out=Sb[:], in_=S[:]), reads=["S"], writes=["Sb"])
                            nsrc = AP(nst[:].tensor, nst[:].offset, [list(nst[:].ap[0]), [1, 4], [0, 128]])
                            P.op("dve", lambda e, nsrc=nsrc: e.tensor_copy(out=nrep[:], in_=nsrc), reads=["nst"], writes=["nrep"])
                            gw = min(TT, L)
                            ng = L // gw
                            for gi_ in (range(ng) if not rev else range(ng - 1, -1, -1)):
                                g0 = tok0 + gi_ * gw
                                bi = gi % 2
                                gi += 1
                                r0, r1 = h * 512, (h + 1) * 512
                                P.op("sp", lambda e, bi=bi, g0=g0, gw=gw, r0=r0, r1=r1: e.dma_start(
                                    out=qT[bi][:, :, :gw], in_=MQ[r0:r1, g0:g0 + gw].rearrange("(k p) t -> p k t", p=128)), writes=[("qT", bi)], dma=True)
                                P.op("sp", lambda e, bi=bi, g0=g0, gw=gw, r0=r0, r1=r1: e.dma_start(
                                    out=kT[bi][:, :, :gw], in_=MK[r0:r1, g0:g0 + gw].rearrange("(k p) t -> p k t", p=128)), writes=[("kT", bi)], dma=True)
                                P.op("sp", lambda e, bi=bi, g0=g0, gw=gw, r0=r0, r1=r1: e.dma_start(
                                    out=ktk[bi][:, :gw // 64, :], in_=MKT[g0:g0 + gw, r0:r1].rearrange("(c p) f -> p c f", p=64)), writes=[("ktk", bi)], dma=True)
                                P.op("sp", lambda e, bi=bi, g0=g0, gw=gw, r0=r0, r1=r1: e.dma_start(
                                    out=vtk[bi][:, :gw // 64, :], in_=MVT[g0:g0 + gw, r0:r1].rearrange("(c p) f -> p c f", p=64)), writes=[("vtk", bi)], dma=True)
                                ncg = gw // 64
                                for c_ in (range(ncg) if not rev else range(ncg - 1, -1, -1)):
                                    o0 = gi_ * gw + c_ * 64
                                    q0 = c_ * 64
                                    V = (lambda ap_: rv(ap_, 64)) if rev else (lambda ap_: ap_)
                                    last = 0 if rev else 63
                                    rk = ["sm"]
                                    def dv(fn, r=(), w=()):
                                        P.op("dve", fn, reads=list(r) + rk, writes=list(w) + rk)
                                    def ac(fn, r=(), w=()):
                                        P.op("act", fn, reads=list(r) + rk, writes=list(w) + rk)
                                    lf_c, ig_c = lfr[:, o0:o0 + 64], igr[:, o0:o0 + 64]
                                    dv(lambda e, lf_c=lf_c, V=V: e.tensor_tensor_scan(out=V(sm["b"][:]), data0=onesr[:], data1=V(lf_c), initial=0.0,
                                                                                      op0=ALU.mult, op1=ALU.add), r=["lfr", "onesr"])
                                    dv(lambda e, ig_c=ig_c: e.tensor_tensor(out=sm["a"][:], in0=ig_c, in1=sm["b"][:], op=ALU.subtract), r=["igr"])
                                    dv(lambda e, V=V: e.tensor_tensor_scan(out=V(sm["M"][:]), data0=V(sm["a"][:]), data1=V(sm["a"][:]), initial=mcol[:],
                                                                           op0=ALU.max, op1=ALU.max), r=["mcol"])
                                    dv(lambda e: e.tensor_tensor(out=sm["mt"][:], in0=sm["b"][:], in1=sm["M"][:], op=ALU.add))
                                    ac(lambda e: e.activation(out=sm["wi"][:], in_=sm["M"][:], func=AF.Exp, scale=-1.0, bias=mcol[:]), r=["mcol"])
                                    ac(lambda e: e.activation(out=sm["em"][:], in_=sm["mt"][:], func=AF.Exp, scale=-1.0))
                                    P.op("pe", lambda e: e.transpose(banks[0][0:64, 128:256], sm["a"][:], ident[:]), reads=rk + ["ident"], writes=[BK[0]])
                                    dv(lambda e: e.tensor_copy(out=acol[:], in_=banks[0][0:64, 128:129]), r=[BK[0]], w=["acol"])
                                    ac(lambda e: e.activation(out=wT[:], in_=sm["M"][0:64, :], func=AF.Exp, scale=-1.0, bias=acol[:]), r=["acol"], w=["wT"])
                                    dv(lambda e, d=d: e.tensor_tensor(out=wT[:], in0=wT[:], in1=msk[:, d, :], op=ALU.mult), r=["msk", "wT"], w=["wT"])
                                    dv(lambda e, last=last: e.tensor_tensor(out=c1["cs"][:], in0=sm["b"][:, last:last + 1], in1=sm["mt"][:, last:last + 1], op=ALU.subtract))
                                    ac(lambda e: e.activation(out=c1["wo"][:], in_=c1["cs"][:], func=AF.Exp, bias=mcol[:]), r=["mcol"])
                                    ac(lambda e: e.activation(out=win[:], in_=acol[:], func=AF.Exp, bias=c1["cs"][0:64, :]), r=["acol"], w=["win"])
                                    dv(lambda e: e.tensor_copy(out=winb[:], in_=win[:]), r=["win"], w=["winb"])
                                    dv(lambda e, last=last: e.tensor_copy(out=mcol[:], in_=sm["mt"][:, last:last + 1]), r=["mcol"], w=["mcol"])
                                    for kc in range(4):
                                        P.op("pe", lambda e, kc=kc, bi=bi, q0=q0: e.matmul(
                                            banks[0][0:64, 0:64], lhsT=kT[bi][:, kc, q0:q0 + 64], rhs=qT[bi][:, kc, q0:q0 + 64], start=(kc == 0), stop=(kc == 3)),
                                            reads=[("kT", bi), ("qT", bi)], writes=[BK[0]], sig=(kc == 3))
                                    dv(lambda e: e.tensor_tensor(out=sw[:], in0=banks[0][0:64, 0:64], in1=wT[:], op=ALU.mult), r=[BK[0], "wT"], w=["sw"])
                                    wib = AP(sm["wi"][:].tensor, sm["wi"][:].offset, [list(sm["wi"][:].ap[0]), [0, 4], [1, 64]])
                                    dv(lambda e, bi=bi, q0=q0, wib=wib: e.tensor_tensor(out=qs[:], in0=qT[bi][:, :, q0:q0 + 64], in1=wib, op=ALU.mult),
                                       r=[("qT", bi)], w=["qs"])
                                    for vc in range(4):
                                        P.op("pe", lambda e, vc=vc, bi=bi, c_=c_: e.matmul(
                                            banks[1][:, vc * 64:(vc + 1) * 64], lhsT=vtk[bi][:, c_, vc * 128:(vc + 1) * 128], rhs=sw[:], start=True, stop=False),
                                            reads=[("vtk", bi), "sw"], writes=[BK[1]], sig=False)
                                        for kc in range(4):
                                            P.op("pe", lambda e, vc=vc, kc=kc: e.matmul(
                                                banks[1][:, vc * 64:(vc + 1) * 64], lhsT=Sb[:, kc, vc * 128:(vc + 1) * 128], rhs=qs[:, kc, :], start=False, stop=(kc == 3)),
                                                reads=["Sb", "qs"], writes=[BK[1]], sig=(kc == 3 and vc == 3))
                                    P.op("pe", lambda e: e.matmul(banks[2][:, 0:64], lhsT=onesb[:], rhs=sw[:], start=True, stop=False),
                                         reads=["onesb", "sw"], writes=[BK[2]], sig=False)
                                    for kc in range(4):
                                        P.op("pe", lambda e, kc=kc: e.matmul(banks[2][:, 0:64], lhsT=nrep[:, kc, :], rhs=qs[:, kc, :], start=False, stop=(kc == 3)),
                                             reads=["nrep", "qs"], writes=[BK[2]], sig=(kc == 3))
                                    ac(lambda e: e.activation(out=sm["dn"][:], in_=banks[2][:, 0:64], func=AF.Abs), r=[BK[2]])
                                    dv(lambda e: e.tensor_tensor(out=sm["dn"][:], in0=sm["dn"][:], in1=sm["em"][:], op=ALU.max))
                                    dv(lambda e: e.reciprocal(out=sm["dn"][:], in_=sm["dn"][:]))
                                    hb = gi_ % 2
                                    dnb = AP(sm["dn"][:].tensor, sm["dn"][:].offset, [list(sm["dn"][:].ap[0]), [0, 4], [1, 64]])
                                    dv(lambda e, hb=hb, q0=q0, dnb=dnb: e.tensor_tensor(
                                        out=hst[hb][:, :, q0:q0 + 64], in0=banks[1][:, 0:256].rearrange("p (v t) -> p v t", v=4), in1=dnb, op=ALU.mult),
                                       r=[BK[1]], w=[("hst", hb)])
                                    ac(lambda e, bi=bi, c_=c_: e.activation(out=vs[:], in_=vtk[bi][:, c_, :], func=AF.Identity, scale=win[:]),
                                       r=[("vtk", bi), "win"], w=["vs"])
                                    for kc in range(4):
                                        P.op("pe", lambda e, kc=kc, bi=bi, c_=c_: e.matmul(
                                            banks[3 + kc][:], lhsT=ktk[bi][:, c_, kc * 128:(kc + 1) * 128], rhs=vs[:], start=True, stop=True),
                                            reads=[("ktk", bi), "vs"], writes=[BK[3 + kc]])
                                        P.op("pe", lambda e, kc=kc, bi=bi, c_=c_: e.matmul(
                                            banks[7][:, kc:kc + 1], lhsT=ktk[bi][:, c_, kc * 128:(kc + 1) * 128], rhs=winb[:], start=True, stop=True),
                                            reads=[("ktk", bi), "winb"], writes=[BK[7]])
                                    for kc in range(4):
                                        dv(lambda e, kc=kc: e.scalar_tensor_tensor(out=S[:, kc, :], in0=S[:, kc, :], scalar=c1["wo"][:], in1=banks[3 + kc][:],
                                                                                   op0=ALU.mult, op1=ALU.add), r=[BK[3 + kc], "S"], w=["S"])
                                    ac(lambda e: e.copy(out=Sb[:], in_=S[:]), r=["S"], w=["Sb"])
                                    dv(lambda e: e.scalar_tensor_tensor(out=nst[:], in0=nst[:], scalar=c1["wo"][:], in1=banks[7][:, 0:4],
                                                                        op0=ALU.mult, op1=ALU.add), r=[BK[7], "nst"], w=["nst"])
                                    dv(lambda e, nsrc=nsrc: e.tensor_copy(out=nrep[:], in_=nsrc), r=["nst"], w=["nrep"])
                                hb = gi_ % 2
                                if d == 0:
                                    P.op("pool", lambda e, hb=hb, g0=g0, gw=gw, r0=r0, r1=r1: e.dma_start(
                                        out=CELL[r0:r1, g0:g0 + gw].rearrange("(v p) t -> p v t", p=128), in_=hst[hb][:, :, :gw]),
                                        reads=[("hst", hb)], writes=[("CELL", h, g0)], dma=True)
                                else:
                                    P.op("pool", lambda e, hb=hb, g0=g0, gw=gw, r0=r0, r1=r1: e.dma_start(
                                        out=CELL[r0:r1, g0:g0 + gw].rearrange("(v p) t -> p v t", p=128), in_=hst[hb][:, :, :gw], accum_op=ALU.add),
                                        reads=[("hst", hb), ("CELL", h, g0)], writes=[("CELL", h, g0)], dma=True)
                            if pi >= 0:
                                for vc in range(4):
                                    for kc in range(4):
                                        P.op("pe", lambda e, vc=vc, kc=kc: e.transpose(banks[3][:, kc * 128:(kc + 1) * 128], S[:, kc, vc * 128:(vc + 1) * 128], ident[:]),
                                             reads=["S", "ident"], writes=[BK[3]], sig=(kc == 3))
                                    P.op("act", lambda e: e.copy(out=cst[:], in_=banks[3][:]), reads=[BK[3]], writes=["cst"])
                                    P.op("pool", lambda e, vc=vc, d=d, h=h, pi=pi: e.dma_start(out=o_C[pi, d, h, vc * 128:(vc + 1) * 128, :], in_=cst[:]),
                                         reads=["cst"], dma=True)
                                P.op("pool", lambda e, d=d, h=h, pi=pi: e.dma_start(out=o_n[pi, d, h].rearrange("(k p) -> p k", p=128), in_=nst[:],
                                                                                    allow_slow_non_contiguous=True), reads=["nst"], dma=True)
                                mdst = AP(o_m.tensor, o_m.offset + (pi * 2 + d) * 8 + h, [[1, 1], [1, 1]])
                                P.op("pool", lambda e, mdst=mdst: e.dma_start(out=mdst, in_=mcol[0:1, :]), reads=["mcol", "sm"], dma=True)
                P.barrier()

        def mixer_ml_p3(B):
            with ExitStack() as ph2:
                sid = scopeid[0]
                gnc = sbuf(ph2, f"mlgn_{sid}", [128, 32], F32)
                skc = sbuf(ph2, f"mlsk_{sid}", [128, 32], F32)
                og = sbuf(ph2, f"mlog_{sid}", [128, 4, TT], BF16)
                xcb = sbuf(ph2, f"mlxc_{sid}", [128, 4, TT], BF16)
                colload(gnc[:], ml_gn, 32)
                colload(skc[:], ml_skip, 32)
                for t in range(NT):
                    for h in range(8):
                        sl = lambda scr: scr[h * 512:(h + 1) * 512, t * TT:(t + 1) * TT].rearrange("(k p) t -> p k t", p=128)
                        P.op("sp", lambda e, sl=sl: e.dma_start(out=B.xt[:, 0:4, :], in_=sl(CELL)), writes=[("xt", k) for k in range(4)], dma=True)
                        P.op("sp", lambda e, sl=sl: e.dma_start(out=og[:], in_=sl(MOG)), writes=["og"], dma=True)
                        P.op("sp", lambda e, sl=sl: e.dma_start(out=xcb[:], in_=sl(MXC)), writes=["xcb"], dma=True)
                        for k in range(4):
                            P.op("dve", lambda e, k=k: e.tensor_tensor(out=B.xt[:, k, :], in0=B.xt[:, k, :], in1=og[:, k, :], op=ALU.mult),
                                 reads=[("xt", k), "og"], writes=[("xt", k)])
                        rms_rstd(B, lambda k: B.xt[:, k, :], lambda k: ("xt", k), 4, 1.0 / 512)
                        for k in range(4):
                            c = h * 4 + k
                            P.op("dve", lambda e, k=k: e.tensor_tensor(out=B.tt[k % 2][:], in0=B.xt[:, k, :], in1=B.rstd[:], op=ALU.mult),
                                 reads=[("xt", k), "rstd"], writes=[("tt", k % 2)])
                            P.op("act", lambda e, k=k, c=c: e.activation(out=B.tt[k % 2][:], in_=B.tt[k % 2][:], func=AF.Identity, scale=gnc[:, c:c + 1]),
                                 reads=[("tt", k % 2), "cols"], writes=[("tt", k % 2)])
                            i = cnt["ev"]; cnt["ev"] += 1
                            P.op("dve", lambda e, k=k, c=c, i=i: e.scalar_tensor_tensor(
                                out=B.obf[i % 3][:], in0=xcb[:, k, :], scalar=skc[:, c:c + 1], in1=B.tt[k % 2][:], op0=ALU.mult, op1=ALU.add),
                                reads=["xcb", "cols", ("tt", k % 2)], writes=[("obf", i % 3)])
                            P.op("pool", lambda e, c=c, i=i, t=t: e.dma_start(out=MYT[c * 128:(c + 1) * 128, t * TT:(t + 1) * TT], in_=B.obf[i % 3][:]),
                                 reads=[("obf", i % 3)], dma=True)
                P.barrier()

        def mixer_out(B, l, scr, nf, wap):
            prep_GS(l, 1, 4, 5, 1.0)
            for t in range(NT):
                cnd = 0 if t < NTS else 1
                P.op("sp", lambda e, t=t: e.dma_start(
                    out=B.hh[:, :nf, :], in_=scr[:, t * TT:(t + 1) * TT].rearrange("(k p) t -> p k t", p=128)),
                    writes=[("hh", f) for f in range(nf)], dma=True)
                down_res(B, t, cnd, lambda f: B.hh[:, f, :], lambda f: ("hh", f), nf, wap)

        PW = {}
        if cfg.get("precast", True):
            with ExitStack() as ph:
                pst = [sbuf(ph, f"pst{i}", [128, 4096], F32) for i in range(3)]
                pwb = [sbuf(ph, f"pwb{i}", [128, 4096], BF16) for i in range(3)]
                pi_ = 0
                for l in range(LAYERS):
                    for f in ("ffn1", "ffn2"):
                        for mi in range(3):
                            pieces = UP_PIECES if mi < 2 else DN_PIECES
                            pw = PreW(f"pw_{f}_{l}_{mi}", pieces)
                            PW[(f, l, mi)] = pw
                            wsrc = fw[f][mi][l]
                            for (k0, nk, c0, ncol) in pieces:
                                i = pi_ % 3
                                pi_ += 1
                                n_ = nk * ncol
                                sv = pst[i][:, :n_].rearrange("p (k c) -> p k c", k=nk)
                                P.op("sp", lambda e, sv=sv, wsrc=wsrc, k0=k0, nk=nk, c0=c0, ncol=ncol: e.dma_start(
                                    out=sv, in_=wsrc[k0 * 128:(k0 + nk) * 128, c0:c0 + ncol].rearrange("(k p) c -> p k c", p=128)),
                                    writes=[("pst", i)], dma=True)
                                if pi_ % 3 == 0:
                                    P.op("pool", lambda e, i=i, n_=n_: e.tensor_copy(out=pwb[i][:, :n_], in_=pst[i][:, :n_]), reads=[("pst", i)], writes=[("pwb", i)])
                                elif pi_ % 3 == 1:
                                    P.op("act", lambda e, i=i, n_=n_: e.copy(out=pwb[i][:, :n_], in_=pst[i][:, :n_]), reads=[("pst", i)], writes=[("pwb", i)])
                                else:
                                    P.op("dve", lambda e, i=i, n_=n_: e.tensor_copy(out=pwb[i][:, :n_], in_=pst[i][:, :n_]), reads=[("pst", i)], writes=[("pwb", i)])
                                P.op("pool", lambda e, i=i, n_=n_, pw=pw, k0=k0, nk=nk, c0=c0, ncol=ncol: e.dma_start(
                                    out=pw.get(k0, nk, c0, ncol), in_=pwb[i][:, :n_]), reads=[("pwb", i)], writes=[pw.key], dma=True)
                P.barrier()

        def FW(f, l, mi):
            return PW[(f, l, mi)] if (f, l, mi) in PW else fw[f][mi][l]

        for l in range(LAYERS):
            kind = MIX[l]
            with ExitStack() as ph:
                B = tok_scope(ph)
                ffn_half(B, l, 0, FW("ffn1", l, 0), FW("ffn1", l, 1), FW("ffn1", l, 2), 0, 1, 2)
                if kind == "rg":
                    mixer_rg_p1(B, l)
                if kind == "s5":
                    mixer_s5_p1(B, l)
                if kind == "da":
                    mixer_da_p1(B, l)
                if kind == "ml":
                    mixer_ml_p1(B, l)
                P.barrier()
            if kind == "rg":
                mixer_rg_p2()
            if kind == "s5":
                mixer_s5_p2()
            if kind == "da" and cfg.get("da_stop", 9) > 1:
                mixer_da_p2()
            if kind == "ml":
                mixer_ml_p1b()
                with ExitStack() as ph:
                    B = tok_scope(ph)
                    mixer_ml_p2a(B)
                    P.barrier()
                if cfg.get("ml_stop", 9) > 1:
                    mixer_ml_p2b()
            with ExitStack() as ph:
                B = tok_scope(ph)
                if kind == "rg":
                    mixer_out(B, l, YT, KC, rg_w_out)
                if kind == "da" and cfg.get("da_stop", 9) > 2:
                    mixer_out(B, l, AT, KC, da_wo)
                if kind == "ml" and cfg.get("ml_stop", 9) > 2:
                    mixer_ml_p3(B)
                    mixer_out(B, l, MYT, 32, ml_w_down)
                if kind == "s5":
                    mixer_s5_p3(B, l)
                    mixer_out(B, l, Z2, 8, s5_w_out)
                ffn_half(B, l, 2, FW("ffn2", l, 0), FW("ffn2", l, 1), FW("ffn2", l, 2), 6, 7, 8)
                P.barrier()

        with ExitStack() as ph:
            B = NS()
            B.xt = sbuf(ph, "xt2", [128, KC, TT], F32)
            B.sq = [sbuf(ph, f"sq2{i}", [128, TT], F32) for i in range(2)]
            B.rstd = sbuf(ph, "rstd2", [128, TT], F32)
            B.tt = [sbuf(ph, f"tt2{i}", [128, TT], F32) for i in range(2)]
            B.epsc = sbuf(ph, "epsc2", [128, 1], F32)
            P.op("dve", lambda e: e.memset(B.epsc[:], EPS), writes=["epsc"])
            yo = [sbuf(ph, f"yo{i}", [128, D], F32) for i in range(2)]
            xf = sbuf(ph, "xf", [128, KC, TT], F32)
            for t in range(NT):
                load_x(B, t)
                rms_rstd(B, lambda k: B.xt[:, k, :], lambda k: ("xt", k), KC, 1.0 / D)
                for k in range(KC):
                    P.op("dve", lambda e, k=k: e.tensor_tensor(out=B.tt[k % 2][:], in0=B.xt[:, k, :], in1=B.rstd[:], op=ALU.mult),
                         reads=[("xt", k), "rstd"], writes=[("tt", k % 2)])
                    P.op("act", lambda e, k=k: e.activation(out=xf[:, k, :], in_=B.tt[k % 2][:], func=AF.Identity,
                                                            scale=fgcol[:, k:k + 1]),
                         reads=[("tt", k % 2), "fgcol"], writes=[("xf", k)])
                for tb in range(4):
                    yb = yo[tb % 2]
                    for q in range(4):
                        bk = banks[q]
                        for kk in range(4):
                            k = q * 4 + kk
                            P.op("pe", lambda e, bk=bk, kk=kk, k=k, tb=tb: e.transpose(
                                bk[:, kk * 128:(kk + 1) * 128], xf[:, k, tb * 128:(tb + 1) * 128], ident[:]),
                                reads=[("xf", k), "ident"], writes=[BK[q]], sig=(kk == 3))
                        if q % 2 == 0:
                            P.op("act", lambda e, bk=bk, q=q, yb=yb: e.copy(out=yb[:, q * 512:(q + 1) * 512], in_=bk[:]),
                                 reads=[BK[q]], writes=[("yo", tb % 2, q)])
                        else:
                            P.op("dve", lambda e, bk=bk, q=q, yb=yb: e.tensor_copy(out=yb[:, q * 512:(q + 1) * 512], in_=bk[:]),
                                 reads=[BK[q]], writes=[("yo", tb % 2, q)])
                    r0 = t * TT + tb * 128
                    P.op("pool", lambda e, yb=yb, r0=r0: e.dma_start(out=y_out[r0:r0 + 128, :], in_=yb[:]),
                         reads=[("yo", tb % 2, q) for q in range(4)], dma=True)
            P.barrier()
        P.finish()
        print("ops", P.nops, "waits", P.nwaits, flush=True)
    return nc


def _f32(a):
    return np.ascontiguousarray(np.asarray(a, dtype=np.float32))


SHARED = ["ada_w", "ada_b", "norm_g", "ffn1_w1", "ffn1_w3", "ffn1_w2", "ffn2_w1", "ffn2_w3", "ffn2_w2", "final_norm_g",
          "s5_w_in", "s5_w_glu", "s5_w_out", "s5_a_re", "s5_a_im", "s5_log_dt", "s5_b_re", "s5_b_im", "s5_c_re", "s5_c_im", "s5_d", "s5_b_glu",
          "da_wq", "da_wk", "da_wv", "da_wo", "da_lam", "da_subln_g",
          "ml_w_up", "ml_w_o", "ml_w_down", "ml_conv_w", "ml_conv_b", "ml_wq", "ml_wk", "ml_wv", "ml_w_gate", "ml_b_gate",
          "ml_b_o", "ml_gn", "ml_skip", "rg_w_in", "rg_w_gate", "rg_w_out", "rg_conv_w", "rg_conv_b", "rg_wa", "rg_wx", "rg_ba", "rg_bx", "rg_lam"]


_ROPE = {}


def _rope_consts():
    if not _ROPE:
        pos = np.arange(4096)
        row, col = (pos // 64).astype(np.float32), (pos % 64).astype(np.float32)
        inv = (10000.0 ** (-np.arange(0, 64, 2, dtype=np.float32) / 64)).astype(np.float32)
        cos = np.zeros((128, 4096), np.float32)
        sin = np.zeros((128, 4096), np.float32)
        perm = np.zeros((128, 128), np.float32)
        for d in range(128):
            p = row if d < 64 else col
            ang = (p * inv[d % 32]).astype(np.float32)
            cos[d] = np.cos(ang)
            first = (d % 64) < 32
            sin[d] = -np.sin(ang) if first else np.sin(ang)
            perm[d + 32 if first else d - 32, d] = 1.0
        _ROPE.update(rope_cos=cos, rope_sin=sin, rope_perm=perm)
    return _ROPE


def make_in_map(inp, shared, c, ns_tok=4096):
    b = c // 2
    m = dict(shared)
    xs, xp = inp["x_sample"], inp["x_prompt"]
    m["x_in"] = np.concatenate([_f32(xs[b][:ns_tok]), _f32(xp[2 * c]), _f32(xp[2 * c + 1])], axis=0)
    m["cvec"] = np.stack([_f32(inp["c"][b]), _f32(inp["c_ctx"])], axis=0)
    m["st_rg"] = _f32(inp["state_rglru"][b])
    m["kcache"] = _f32(inp["cache_dattn_k"][b]).reshape(256, D)
    m["vcache"] = _f32(inp["cache_dattn_v"][b]).reshape(256, D)
    m.update(_rope_consts())
    m["st_C"] = _f32(inp["state_mlstm_C"][b]); m["st_n"] = _f32(inp["state_mlstm_n"][b]); m["st_m"] = _f32(inp["state_mlstm_m"][b])
    tri = np.tril(np.ones((64, 64), np.float32))
    m["ml_masks"] = np.stack([tri.T.copy(), tri.copy()], axis=0)
    m["st_s5"] = _f32(inp["state_s5"][b])
    return m


def kernel(**inp):
    n = 8
    nc = build_nc({})
    shared = {k: _f32(inp[k]) for k in SHARED}
    in_maps = [make_in_map(inp, shared, c) for c in range(n)]
    res = run_bass_kernel_spmd(nc, in_maps, core_ids=list(range(n)))
    R = res.results
    y_sample = np.stack([R[2 * b]["y_out"][:4096] for b in range(4)], axis=0)
    y_prompt = np.concatenate([R[c]["y_out"][4096:].reshape(2, 256, D) for c in range(n)], axis=0)
    z = lambda *s: np.zeros(s, np.float32)
    o_rg = np.concatenate([R[c]["o_rg"] for c in range(n)], axis=0) if "o_rg" in R[0] else z(16, 2, 2048)
    o_s5 = np.concatenate([R[c]["o_s5"] for c in range(n)], axis=0) if "o_s5" in R[0] else z(16, 2, 2, 64, 64)
    if "o_C" in R[0]:
        o_dk = np.concatenate([R[c]["o_dk"].reshape(2, 256, 8, 2, 128) for c in range(n)], axis=0)
        o_dv = np.concatenate([R[c]["o_dv"].reshape(2, 256, 8, 256) for c in range(n)], axis=0)
        o_C = np.concatenate([R[c]["o_C"] for c in range(n)], axis=0)
        o_n = np.concatenate([R[c]["o_n"] for c in range(n)], axis=0)
        o_m = np.concatenate([R[c]["o_m"] for c in range(n)], axis=0)
        return (y_prompt, y_sample, o_s5, o_rg, o_dk, o_dv, o_C, o_n, o_m)
    if "o_dk" in R[0]:
        o_dk = np.concatenate([R[c]["o_dk"].reshape(2, 256, 8, 2, 128) for c in range(n)], axis=0)
        o_dv = np.concatenate([R[c]["o_dv"].reshape(2, 256, 8, 256) for c in range(n)], axis=0)
        return (y_prompt, y_sample, o_s5, o_rg, o_dk, o_dv, z(16, 2, 8, 512, 512), z(16, 2, 8, 512), z(16, 2, 8))
    return (y_prompt, y_sample, o_s5, o_rg, z(16, 256, 8, 2, 128), z(16, 256, 8, 256),
            z(16, 2, 8, 512, 512), z(16, 2, 8, 512), z(16, 2, 8))
```

```python
import numpy as np
import concourse.bass as bass
import concourse.mybir as mybir
from concourse.bass_utils import run_bass_kernel_spmd
from contextlib import ExitStack

F32 = mybir.dt.float32
BF16 = mybir.dt.bfloat16
I32 = mybir.dt.int32
AF = mybir.ActivationFunctionType
ALU = mybir.AluOpType
AX = mybir.AxisListType
AP = bass.AP

D = 2048
KC = 16
DFF = 5632
FC = 44
TT = 512
EPS = 1e-6
EPOCH = 12000
NDMASEM = 12


class Op:
    __slots__ = ("eng", "dma", "sem", "val", "sig")


class Prog:
    def __init__(self, nc, es):
        self.nc = nc
        self.es = es
        self.engs = {"pe": nc.tensor, "act": nc.scalar, "dve": nc.vector,
                     "pool": nc.gpsimd, "sp": nc.sync}
        self.count = {e: 0 for e in self.engs}
        self.nops = {e: 0 for e in self.engs}
        self.csems = {e: [] for e in self.engs}
        self.known = {e: {} for e in self.engs}
        self.dsems = {}
        self.dnext = {e: 0 for e in self.engs}
        self.last_w = {}
        self.readers = {}
        self.pending = {e: [] for e in self.engs}
        self.nwaits = 0

    def _csem(self, eng, epoch):
        lst = self.csems[eng]
        while len(lst) <= epoch:
            lst.append(self.es.enter_context(self.nc.semaphore(f"c_{eng}_{len(lst)}")))
        return lst[epoch]

    def _dsem(self, eng):
        if eng not in self.dsems:
            self.dsems[eng] = [[self.es.enter_context(self.nc.semaphore(f"d_{eng}_{i}")), 0]
                               for i in range(NDMASEM)]
        i = self.dnext[eng] % NDMASEM
        self.dnext[eng] += 1
        return self.dsems[eng][i]

    def _wait(self, eng, sem, val):
        k = self.known[eng]
        key = id(sem)
        if k.get(key, -1) >= val:
            return
        k[key] = val
        self.engs[eng].wait_ge(sem, val)
        self.nwaits += 1

    def op(self, eng, fn, reads=(), writes=(), dma=False, sig=True):
        o = Op()
        o.eng = eng
        o.dma = dma
        o.sig = sig or dma
        o.sem = None
        o.val = None
        deps = {}
        for r in reads:
            d = self.last_w.get(r)
            if d is not None:
                deps[id(d)] = d
        for w in writes:
            d = self.last_w.get(w)
            if d is not None:
                deps[id(d)] = d
            for d in self.readers.get(w, ()):
                deps[id(d)] = d
        for d in deps.values():
            if d.dma or dma or d.eng != eng or eng != "pe":
                if d.sem is None:
                    raise RuntimeError("dependency on op that never signals")
                self._wait(eng, d.sem, d.val)
        if dma:
            slot = self._dsem(eng)
            if slot[1] > 0:
                self._wait(eng, slot[0], slot[1])
            slot[1] += 16
            o.sem, o.val = slot[0], slot[1]
            fn(self.engs[eng]).then_inc(o.sem, 16)
        else:
            ins = fn(self.engs[eng])
            if o.sig:
                self.count[eng] += 1
                c = self.count[eng]
                ep, v = (c - 1) // EPOCH, (c - 1) % EPOCH + 1
                o.sem, o.val = self._csem(eng, ep), v
                ins.then_inc(o.sem, 1)
                for p in self.pending[eng]:
                    p.sem, p.val = o.sem, o.val
                self.pending[eng] = []
            else:
                self.pending[eng].append(o)
        self.nops[eng] += 1
        for r in reads:
            self.readers.setdefault(r, []).append(o)
        for w in writes:
            self.last_w[w] = o
            self.readers[w] = []
        return o

    def barrier(self):
        for e in self.engs:
            assert not self.pending[e]
        for e in self.engs:
            for e2 in self.engs:
                c = self.count[e2]
                if c > 0:
                    ep, v = (c - 1) // EPOCH, (c - 1) % EPOCH + 1
                    self._wait(e, self.csems[e2][ep], v)
            for slots in self.dsems.values():
                for sem, total in slots:
                    if total > 0:
                        self._wait(e, sem, total)
        self.last_w = {}
        self.readers = {}

    def finish(self):
        self.barrier()


class Ctx:
    pass


def build_nc(cfg):
    NTS = cfg.get("ns_tiles", 8)
    NTP = cfg.get("np_tiles", 1)
    NT = NTS + NTP
    NTOK = NT * TT
    LAYERS = cfg.get("layers", 4)
    nc = bass.Bass("TRN2", target_bir_lowering=False)
    dr = lambda name, shape, dt=F32, kind="ExternalInput": nc.dram_tensor(name, shape, dt, kind=kind).ap()
    x_in = dr("x_in", [NTOK, D])
    cvec = dr("cvec", [2, D])
    ada_w = dr("ada_w", [4, D, 9 * D])
    ada_b = dr("ada_b", [4, 9 * D])
    norm_g = dr("norm_g", [4, 3, D])
    fw = {}
    for f in ("ffn1", "ffn2"):
        fw[f] = (dr(f + "_w1", [4, D, DFF]), dr(f + "_w3", [4, D, DFF]), dr(f + "_w2", [4, DFF, D]))
    final_g = dr("final_norm_g", [D])
    y_out = dr("y_out", [NTOK, D], kind="ExternalOutput")
    xT = dr("xT_scr", [D, NTOK], kind="Internal")

    es = ExitStack()
    with es:
        P = Prog(nc, es)
        sbuf = lambda es_, name, shape, dt: es_.enter_context(nc.sbuf_tensor(name, shape, dt))
        ident = sbuf(es, "ident", [128, 128], F32)
        ones = sbuf(es, "ones", [128, 128], F32)
        modc = sbuf(es, "modc", [128, 4 * 2 * 9 * KC], F32)
        gcol = sbuf(es, "gcol", [128, 4 * 3 * KC], F32)
        fgcol = sbuf(es, "fgcol", [128, KC], F32)
        GS = sbuf(es, "GS", [128, 2 * 2 * 3 * KC], F32)
        banks = [es.enter_context(nc.psum_tensor(f"bank{i}", [128, 512], F32)) for i in range(8)]
        BK = [("bank", i) for i in range(8)]

        def mcol(l, c, v, k):
            i = ((l * 2 + c) * 9 + v) * KC + k
            return modc[:, i:i + 1]

        P.op("dve", lambda e: e.memset(ident[:], 0.0), writes=["ident"])
        P.op("pool", lambda e: e.affine_select(out=ident[:], in_=ident[:], pattern=[[-1, 128]],
                                               compare_op=ALU.not_equal, fill=1.0, base=0,
                                               channel_multiplier=1), reads=["ident"], writes=["ident"])
        P.op("dve", lambda e: e.memset(ones[:], 1.0), writes=["ones"])
        for lj in range(12):
            P.op("sp", lambda e, lj=lj: e.dma_start(out=gcol[:, lj * KC:(lj + 1) * KC],
                                                    in_=norm_g[lj // 3, lj % 3].rearrange("(k p) -> p k", p=128),
                                                    allow_slow_non_contiguous=True), writes=["gcol"], dma=True)
        P.op("sp", lambda e: e.dma_start(out=fgcol[:], in_=final_g.rearrange("(k p) -> p k", p=128),
                                         allow_slow_non_contiguous=True), writes=["fgcol"], dma=True)

        with ExitStack() as ph:
            scT = sbuf(ph, "scT", [128, 2, KC], F32)
            sig0 = sbuf(ph, "sig0", [128, 2, KC], F32)
            abT = sbuf(ph, "abT", [128, 4 * 9 * KC], F32)
            astg = [sbuf(ph, f"astg{i}", [128, KC, 512], F32) for i in range(2)]
            for c in range(2):
                P.op("sp", lambda e, c=c: e.dma_start(out=scT[:, c, :], in_=cvec[c].rearrange("(k p) -> p k", p=128),
                                                      allow_slow_non_contiguous=True), writes=["scT"], dma=True)
            for l in range(4):
                P.op("sp", lambda e, l=l: e.dma_start(out=abT[:, l * 144:(l + 1) * 144],
                                                      in_=ada_b[l].rearrange("(j p) -> p j", p=128),
                                                      allow_slow_non_contiguous=True), writes=["abT"], dma=True)
            P.op("act", lambda e: e.activation(out=sig0[:], in_=scT[:], func=AF.Sigmoid), reads=["scT"], writes=["sig0"])
            P.op("dve", lambda e: e.tensor_tensor(out=scT[:], in0=scT[:], in1=sig0[:], op=ALU.mult),
                 reads=["scT", "sig0"], writes=["scT"])
            blk = 0
            for l in range(LAYERS):
                for jb in range(36):
                    st = astg[blk % 2]
                    skey = ("astg", blk % 2)
                    P.op("sp", lambda e, st=st, l=l, jb=jb: e.dma_start(
                        out=st[:], in_=ada_w[l, :, jb * 512:(jb + 1) * 512].rearrange("(k p) c -> p k c", p=128)),
                        writes=[skey], dma=True)
                    bk = banks[blk % 2]
                    for jj in range(4):
                        for k in range(KC):
                            P.op("pe", lambda e, st=st, bk=bk, jj=jj, k=k: e.matmul(
                                bk[:, jj * 2:jj * 2 + 2], lhsT=st[:, k, jj * 128:(jj + 1) * 128], rhs=scT[:, :, k],
                                start=(k == 0), stop=(k == KC - 1)),
                                reads=[skey, "scT"], writes=[BK[blk % 2]], sig=(k == KC - 1))
                    for c in range(2):
                        j0 = jb * 4
                        v, k0 = j0 // KC, j0 % KC
                        i0 = ((l * 2 + c) * 9 + v) * KC + k0
                        b0 = l * 9 * KC + j0
                        P.op("dve", lambda e, bk=bk, c=c, i0=i0, b0=b0: e.tensor_tensor(
                            out=modc[:, i0:i0 + 4], in0=bk[:, c:8:2], in1=abT[:, b0:b0 + 4], op=ALU.add),
                            reads=[BK[blk % 2], "abT"], writes=["modc"])
                    blk += 1
            P.barrier()

        MIX = cfg.get("mixers", ["s5", "rg", "da", "ml"])
        LS = NTS * TT
        seqs = [(0, LS, -1)] + [(LS + i * 256, 256, i) for i in range(NTP * 2)]
        scopeid = [0]

        def xT_tile_ap(t):
            return xT[:, t * TT:(t + 1) * TT].rearrange("(k p) t -> p k t", p=128)

        def colload(dst, src1d, n):
            P.op("sp", lambda e: e.dma_start(out=dst, in_=src1d.rearrange("(k p) -> p k", p=128),
                                             allow_slow_non_contiguous=True), writes=["cols"], dma=True)

        with ExitStack() as ph:
            xin = [sbuf(ph, f"xin{i}", [128, D], F32) for i in range(4)]
            xo = sbuf(ph, "xo", [128, KC, TT], F32)
            for t in range(NT):
                for tb in range(4):
                    r0 = t * TT + tb * 128
                    P.op("sp", lambda e, tb=tb, r0=r0: e.dma_start(out=xin[tb][:], in_=x_in[r0:r0 + 128, :]),
                         writes=[("xin", tb)], dma=True)
                for k in range(KC):
                    bk = banks[k % 4]
                    for tb in range(4):
                        P.op("pe", lambda e, bk=bk, tb=tb, k=k: e.transpose(
                            bk[:, tb * 128:(tb + 1) * 128], xin[tb][:, k * 128:(k + 1) * 128], ident[:]),
                            reads=[("xin", tb), "ident"], writes=[BK[k % 4]], sig=(tb == 3))
                    if k % 2 == 0:
                        P.op("act", lambda e, bk=bk, k=k: e.copy(out=xo[:, k, :], in_=bk[:]),
                             reads=[BK[k % 4]], writes=[("xo", k)])
                    else:
                        P.op("dve", lambda e, bk=bk, k=k: e.tensor_copy(out=xo[:, k, :], in_=bk[:]),
                             reads=[BK[k % 4]], writes=[("xo", k)])
                P.op("pool", lambda e, t=t: e.dma_start(out=xT_tile_ap(t), in_=xo[:]),
                     reads=[("xo", k) for k in range(KC)], writes=[("xT", t, k) for k in range(KC)], dma=True)
            P.barrier()

        class NS:
            pass

        cnt = {"stg": 0, "wb": 0, "xc": 0, "it": 0, "ev": 0}
        NSTG, NWB = 2, 4

        def tok_scope(ph):
            B = NS()
            sid = scopeid[0]
            scopeid[0] += 1
            sb_ = lambda name, shape, dt: sbuf(ph, f"{name}_{sid}", shape, dt)
            B.xt = sb_("xt", [128, KC, TT], F32)
            B.hT = sb_("hT", [128, KC, TT], BF16)
            B.hh = sb_("hh", [128, FC, TT], BF16)
            B.stg = [sb_(f"stg{i}", [128, 4096], F32) for i in range(NSTG)]
            B.wbs = [sb_(f"wb{i}", [128, 4096], BF16) for i in range(NWB)]
            B.sq = [sb_(f"sq{i}", [128, TT], F32) for i in range(2)]
            B.rstd = sb_("rstd", [128, TT], F32)
            B.tt = [sb_(f"tt{i}", [128, TT], F32) for i in range(2)]
            B.ssb = [sb_(f"ssb{i}", [128, TT], F32) for i in range(2)]
            B.xc = [sb_(f"xc{i}", [128, TT], F32) for i in range(3)]
            B.ost = [sb_(f"ost{i}", [128, TT], F32) for i in range(3)]
            B.obf = [sb_(f"obf{i}", [128, TT], BF16) for i in range(3)]
            B.epsc = sb_("epsc", [128, 1], F32)
            P.op("dve", lambda e: e.memset(B.epsc[:], EPS), writes=["epsc"])
            return B

        class PreW:
            def __init__(self, name, pieces):
                self.off = {}
                o = 0
                for p_ in pieces:
                    self.off[p_] = o
                    o += p_[1] * p_[3]
                self.t = dr(name, [128, o], BF16, kind="Internal")
                self.key = name
                self.pieces = pieces

            def get(self, k0, nk, c0, ncol):
                o = self.off[(k0, nk, c0, ncol)]
                return self.t[:, o:o + nk * ncol]

        wcache = {}
        UP_PIECES = [(0, KC, g * 256, 256) for g in range(FC // 2)]
        DN_PIECES = [(f0, min(8, FC - f0), dg * 512, 512) for dg in range(4) for f0 in range(0, FC, 8)]

        def wload(B, wap, k0, nk, c0, ncol):
            if isinstance(wap, PreW):
                bi = cnt["wb"] % NWB
                cnt["wb"] += 1
                bv = B.wbs[bi][:, :nk * ncol].rearrange("p (k c) -> p k c", k=nk)
                P.op("sp", lambda e: e.dma_start(out=B.wbs[bi][:, :nk * ncol], in_=wap.get(k0, nk, c0, ncol)),
                     reads=[wap.key], writes=[("wb", bi)], dma=True)
                return bv, ("wb", bi)
            ckey = (wap.name, int(wap.offset), k0, nk, c0, ncol)
            bi = cnt["wb"] % NWB
            cnt["wb"] += 1
            bv = B.wbs[bi][:, :nk * ncol].rearrange("p (k c) -> p k c", k=nk)
            if ckey in wcache:
                ct, ck = wcache[ckey]
                P.op("sp", lambda e: e.dma_start(out=B.wbs[bi][:, :nk * ncol], in_=ct), reads=[ck], writes=[("wb", bi)], dma=True)
                return bv, ("wb", bi)
            si = cnt["stg"] % NSTG
            cnt["stg"] += 1
            sv = B.stg[si][:, :nk * ncol].rearrange("p (k c) -> p k c", k=nk)
            P.op("sp", lambda e: e.dma_start(
                out=sv, in_=wap[k0 * 128:(k0 + nk) * 128, c0:c0 + ncol].rearrange("(k p) c -> p k c", p=128)),
                writes=[("stg", si)], dma=True)
            P.op("pool", lambda e: e.tensor_copy(out=bv, in_=sv), reads=[("stg", si)], writes=[("wb", bi)])
            if cfg.get("wcache", True):
                ck = ("wc", len(wcache))
                ct = dr(f"wc_{len(wcache)}", [128, nk * ncol], BF16, kind="Internal")
                wcache[ckey] = (ct, ck)
                P.op("pool", lambda e: e.dma_start(out=ct, in_=B.wbs[bi][:, :nk * ncol]), reads=[("wb", bi)], writes=[ck], dma=True)
            return bv, ("wb", bi)

        def load_x(B, t):
            P.op("sp", lambda e: e.dma_start(out=B.xt[:], in_=xT_tile_ap(t)),
                 reads=[("xT", t, k) for k in range(KC)], writes=[("xt", k) for k in range(KC)], dma=True)

        def rms_rstd(B, src, skey, nk, inv_n):
            for k in range(nk):
                P.op("act", lambda e, k=k: e.activation(out=B.sq[k % 2][:], in_=src(k), func=AF.Square),
                     reads=[skey(k)], writes=[("sq", k % 2)])
                P.op("pe", lambda e, k=k: e.matmul(banks[4][:], lhsT=ones[:], rhs=B.sq[k % 2][:],
                                                   start=(k == 0), stop=(k == nk - 1)),
                     reads=["ones", ("sq", k % 2)], writes=[BK[4]], sig=True)
            P.op("act", lambda e: e.activation(out=B.rstd[:], in_=banks[4][:], func=AF.Sqrt,
                                               scale=inv_n, bias=B.epsc[:]),
                 reads=[BK[4], "epsc"], writes=["rstd"])
            P.op("dve", lambda e: e.reciprocal(out=B.rstd[:], in_=B.rstd[:]), reads=["rstd"], writes=["rstd"])

        def prep_GS(l, j, vsc, vg, gmul):
            for c in range(2):
                i0 = ((l * 2 + c) * 9 + vsc) * KC
                g0 = (l * 3 + j) * KC
                o0 = (c * 3 + 0) * KC
                P.op("dve", lambda e, i0=i0, g0=g0, o0=o0: e.scalar_tensor_tensor(
                    out=GS[:, o0:o0 + KC], in0=modc[:, i0:i0 + KC], scalar=1.0, in1=gcol[:, g0:g0 + KC],
                    op0=ALU.add, op1=ALU.mult), reads=["modc", "gcol"], writes=["GS"])
                i1 = ((l * 2 + c) * 9 + vg) * KC
                o1 = (c * 3 + 1) * KC
                P.op("dve", lambda e, i1=i1, o1=o1: e.tensor_scalar(
                    out=GS[:, o1:o1 + KC], in0=modc[:, i1:i1 + KC], scalar1=gmul, scalar2=None, op0=ALU.mult),
                    reads=["modc"], writes=["GS"])

        def norm_mod(B, l, cnd, vsh):
            rms_rstd(B, lambda k: B.xt[:, k, :], lambda k: ("xt", k), KC, 1.0 / D)
            for k in range(KC):
                P.op("dve", lambda e, k=k: e.tensor_tensor(out=B.tt[k % 2][:], in0=B.xt[:, k, :], in1=B.rstd[:], op=ALU.mult),
                     reads=[("xt", k), "rstd"], writes=[("tt", k % 2)])
                gi = (cnd * 3) * KC + k
                P.op("act", lambda e, k=k, gi=gi: e.activation(out=B.hT[:, k, :], in_=B.tt[k % 2][:], func=AF.Identity,
                                                               scale=GS[:, gi:gi + 1], bias=mcol(l, cnd, vsh, k)),
                     reads=[("tt", k % 2), "GS", "modc"], writes=[("hT", k)])

        def proj(B, wap, nk, src, skey, c0, nchunks, consume, pair=None):
            NG = (nchunks + 1) // 2
            pre = {}

            def issue(g):
                nc_ = min(2, nchunks - g * 2)
                a = wload(B, wap, 0, nk, c0 + g * 256, nc_ * 128)
                b = wload(B, pair[0], 0, nk, c0 + g * 256, nc_ * 128) if pair else None
                pre[g] = (a, b, nc_)
            issue(0)
            for g in range(NG):
                if g + 1 < NG:
                    issue(g + 1)
                (b1, k1), bb, nc_ = pre.pop(g)
                for fc in range(nc_):
                    f = g * 2 + fc
                    it = cnt["it"]
                    cnt["it"] += 1
                    pa, pb = banks[(it % 2) * 2], banks[(it % 2) * 2 + 1]
                    ka, kb = BK[(it % 2) * 2], BK[(it % 2) * 2 + 1]
                    for k in range(nk):
                        P.op("pe", lambda e, pa=pa, b1=b1, fc=fc, k=k: e.matmul(
                            pa[:], lhsT=b1[:, k, fc * 128:(fc + 1) * 128], rhs=src(k),
                            start=(k == 0), stop=(k == nk - 1)),
                            reads=[k1, skey(k)], writes=[ka], sig=(k == nk - 1))
                    if pair:
                        b3, k3 = bb
                        for k in range(nk):
                            P.op("pe", lambda e, pb=pb, b3=b3, fc=fc, k=k: e.matmul(
                                pb[:], lhsT=b3[:, k, fc * 128:(fc + 1) * 128], rhs=src(k),
                                start=(k == 0), stop=(k == nk - 1)),
                                reads=[k3, skey(k)], writes=[kb], sig=(k == nk - 1))
                        pair[1](f, pa, ka, pb, kb)
                    else:
                        consume(f, pa, ka)

        def down_res(B, t, cnd, src, skey, nf, wap):
            pieces = [(f0, min(8, nf - f0)) for f0 in range(0, nf, 8)]
            for dg in range(4):
                pre2 = {}

                def issue2(pi, dg=dg):
                    f0, n_ = pieces[pi]
                    pre2[pi] = wload(B, wap, f0, n_, dg * 512, 512)
                issue2(0)
                for pi, (f0, n_) in enumerate(pieces):
                    if pi + 1 < len(pieces):
                        issue2(pi + 1)
                    bw, kw = pre2.pop(pi)
                    for dc in range(4):
                        for fl in range(n_):
                            f = f0 + fl
                            P.op("pe", lambda e, dc=dc, fl=fl, f=f, bw=bw: e.matmul(
                                banks[4 + dc][:], lhsT=bw[:, fl, dc * 128:(dc + 1) * 128], rhs=src(f),
                                start=(f == 0), stop=(f == nf - 1)),
                                reads=[kw, skey(f)], writes=[BK[4 + dc]], sig=(fl == n_ - 1))
                for dc in range(4):
                    kk = dg * 4 + dc
                    xi = cnt["xc"] % 3
                    cnt["xc"] += 1
                    P.op("sp", lambda e, xi=xi, kk=kk: e.dma_start(
                        out=B.xc[xi][:], in_=xT[kk * 128:(kk + 1) * 128, t * TT:(t + 1) * TT]),
                        reads=[("xT", t, kk)], writes=[("xc", xi)], dma=True)
                    gc = (cnd * 3 + 1) * KC + kk
                    P.op("dve", lambda e, xi=xi, dc=dc, gc=gc: e.scalar_tensor_tensor(
                        out=B.ost[xi][:], in0=banks[4 + dc][:], scalar=GS[:, gc:gc + 1], in1=B.xc[xi][:],
                        op0=ALU.mult, op1=ALU.add),
                        reads=[BK[4 + dc], "GS", ("xc", xi)], writes=[("ost", xi)])
                    P.op("pool", lambda e, xi=xi, kk=kk: e.dma_start(
                        out=xT[kk * 128:(kk + 1) * 128, t * TT:(t + 1) * TT], in_=B.ost[xi][:]),
                        reads=[("ost", xi)], writes=[("xT", t, kk)], dma=True)

        def ffn_half(B, l, j, w1, w3, w2, vsh, vsc, vg):
            if cfg.get("no_ffn"):
                return
            prep_GS(l, j, vsc, vg, 0.5)
            for t in range(NT):
                cnd = 0 if t < NTS else 1
                load_x(B, t)
                norm_mod(B, l, cnd, vsh)

                def gate(f, pa, ka, pb, kb):
                    i = cnt["ev"]
                    cnt["ev"] += 1
                    P.op("act", lambda e: e.activation(out=B.ssb[i % 2][:], in_=pa[:], func=AF.Silu),
                         reads=[ka], writes=[("ssb", i % 2)])
                    P.op("dve", lambda e: e.tensor_tensor(out=B.hh[:, f, :], in0=B.ssb[i % 2][:], in1=pb[:], op=ALU.mult),
                         reads=[("ssb", i % 2), kb], writes=[("hh", f)])
                proj(B, w1, KC, lambda k: B.hT[:, k, :], lambda k: ("hT", k), 0, FC, None, pair=(w3, gate))
                down_res(B, t, cnd, lambda f: B.hh[:, f, :], lambda f: ("hh", f), FC, w2)

        def store_fm(B, scr, f, t, bank, bkey, func, dt, bias=None, ncols=TT):
            i = cnt["ev"]
            cnt["ev"] += 1
            tile_ = (B.ost if dt == F32 else B.obf)[i % 3]
            key = ("ost" if dt == F32 else "obf", i % 3)
            kw = {} if bias is None else {"bias": bias}
            P.op("act", lambda e: e.activation(out=tile_[:, :ncols], in_=bank[:, :ncols], func=func, **kw),
                 reads=[bkey, "cols"], writes=[key])
            P.op("pool", lambda e: e.dma_start(out=scr[f * 128:(f + 1) * 128, t * TT:t * TT + ncols], in_=tile_[:, :ncols]),
                 reads=[key], writes=[(scr.name, f, t)], dma=True)

        if "rg" in MIX:
            rg_w_in = dr("rg_w_in", [D, D]); rg_w_gate = dr("rg_w_gate", [D, D]); rg_w_out = dr("rg_w_out", [D, D])
            rg_conv_w = dr("rg_conv_w", [4, D]); rg_conv_b = dr("rg_conv_b", [D])
            rg_wa = dr("rg_wa", [2, 16, 128, 128]); rg_wx = dr("rg_wx", [2, 16, 128, 128])
            rg_ba = dr("rg_ba", [2, D]); rg_bx = dr("rg_bx", [2, D]); rg_lam = dr("rg_lam", [2, D])
            st_rg = dr("st_rg", [2, D])
            o_rg = dr("o_rg", [NTP * 2, 2, D], kind="ExternalOutput")
            XI = dr("rg_XI", [D, NTOK], kind="Internal")
            GT = dr("rg_GT", [D, NTOK], BF16, kind="Internal")
            YT = dr("rg_YT", [D, NTOK], BF16, kind="Internal")

        def mixer_rg_p1(B, l):
            prep_GS(l, 1, 4, 5, 1.0)
            for t in range(NT):
                cnd = 0 if t < NTS else 1
                load_x(B, t)
                norm_mod(B, l, cnd, 3)
                proj(B, rg_w_in, KC, lambda k: B.hT[:, k, :], lambda k: ("hT", k), 0, KC,
                     lambda f, pa, ka, t=t: store_fm(B, XI, f, t, pa, ka, AF.Identity, F32))
                proj(B, rg_w_gate, KC, lambda k: B.hT[:, k, :], lambda k: ("hT", k), 0, KC,
                     lambda f, pa, ka, t=t: store_fm(B, GT, f, t, pa, ka, AF.Gelu_apprx_tanh, BF16))

        def mixer_rg_p2():
            LM = LS
            with ExitStack() as ph:
                sb_ = lambda name, shape, dt: sbuf(ph, "rg2_" + name, shape, dt)
                xi = sb_("xi", [128, LM + 3], F32)
                xcv = sb_("xcv", [128, LM], F32)
                r_ = sb_("r", [128, LM], F32)
                ig = sb_("ig", [128, LM], F32)
                a_ = sb_("a", [128, LM], F32)
                a2 = sb_("a2", [128, LM], F32)
                hs = [sb_(f"hs{i}", [128, LM], F32) for i in range(2)]
                gt = sb_("gt", [128, LM], BF16)
                yg = sb_("yg", [128, LM], BF16)
                wab = sb_("wab", [128, 2, 2, 128], F32)
                cols = sb_("cols", [128, 16 * 16], F32)
                sp8 = sb_("sp8", [128, 2 * KC], F32)
                zc = sb_("zc", [128, 1], F32)
                CW, CB, BA, BX, LAMC, H0 = 0, 4, 5, 7, 9, 11
                for j in range(4):
                    colload(cols[:, (CW + j) * KC:(CW + j + 1) * KC], rg_conv_w[j], KC)
                colload(cols[:, CB * KC:(CB + 1) * KC], rg_conv_b, KC)
                for d in range(2):
                    colload(cols[:, (BA + d) * KC:(BA + d + 1) * KC], rg_ba[d], KC)
                    colload(cols[:, (BX + d) * KC:(BX + d + 1) * KC], rg_bx[d], KC)
                    colload(cols[:, (LAMC + d) * KC:(LAMC + d + 1) * KC], rg_lam[d], KC)
                    colload(cols[:, (H0 + d) * KC:(H0 + d + 1) * KC], st_rg[d], KC)
                P.op("dve", lambda e: e.memset(zc[:], 0.0), writes=["zc"])
                P.op("act", lambda e: e.activation(out=sp8[:], in_=cols[:, LAMC * KC:(LAMC + 2) * KC], func=AF.Exp, scale=-1.0),
                     reads=["cols"], writes=["sp8"])
                P.op("act", lambda e: e.activation(out=sp8[:], in_=sp8[:], func=AF.Ln, bias=1.0), reads=["sp8"], writes=["sp8"])
                P.op("dve", lambda e: e.tensor_scalar(out=sp8[:], in0=sp8[:], scalar1=-8.0, scalar2=None, op0=ALU.mult),
                     reads=["sp8"], writes=["sp8"])
                col = lambda base, n: cols[:, base * KC + n:base * KC + n + 1]
                bki = 0
                for n in range(KC):
                    for d in range(2):
                        P.op("sp", lambda e, n=n, d=d: e.dma_start(out=wab[:, 0, d, :], in_=rg_wa[d, n]), writes=["wab"], dma=True)
                        P.op("sp", lambda e, n=n, d=d: e.dma_start(out=wab[:, 1, d, :], in_=rg_wx[d, n]), writes=["wab"], dma=True)
                    for (tok0, L, pi) in seqs:
                        P.op("dve", lambda e: e.memset(xi[:, 0:1], 0.0), writes=["xi"])
                        P.op("dve", lambda e, L=L: e.memset(xi[:, L + 1:L + 3], 0.0), writes=["xi"])
                        P.op("sp", lambda e, n=n, tok0=tok0, L=L: e.dma_start(
                            out=xi[:, 1:L + 1], in_=XI[n * 128:(n + 1) * 128, tok0:tok0 + L]), writes=["xi"], dma=True)
                        P.op("sp", lambda e, n=n, tok0=tok0, L=L: e.dma_start(
                            out=gt[:, :L], in_=GT[n * 128:(n + 1) * 128, tok0:tok0 + L]), writes=["gt"], dma=True)
                        P.op("dve", lambda e, n=n, L=L: e.tensor_scalar(
                            out=xcv[:, :L], in0=xi[:, 0:L], scalar1=col(CW, n), scalar2=col(CB, n), op0=ALU.mult, op1=ALU.add),
                            reads=["xi", "cols"], writes=["xcv"])
                        for j in range(1, 4):
                            P.op("dve", lambda e, n=n, L=L, j=j: e.scalar_tensor_tensor(
                                out=xcv[:, :L], in0=xi[:, j:j + L], scalar=col(CW + j, n), in1=xcv[:, :L],
                                op0=ALU.mult, op1=ALU.add), reads=["xi", "cols", "xcv"], writes=["xcv"])
                        for d in range(2):
                            for (gi, dst, bcol, dk) in ((0, r_, BA, "r"), (1, ig, BX, "ig")):
                                for c0 in range(0, L, TT):
                                    w = min(TT, L - c0)
                                    bk, bkk = banks[bki % 4], BK[bki % 4]
                                    bki += 1
                                    P.op("pe", lambda e, bk=bk, gi=gi, d=d, c0=c0, w=w: e.matmul(
                                        bk[:, :w], lhsT=wab[:, gi, d, :], rhs=xcv[:, c0:c0 + w], start=True, stop=True),
                                        reads=["wab", "xcv"], writes=[bkk])
                                    P.op("act", lambda e, bk=bk, dst=dst, c0=c0, w=w, bcol=bcol, d=d, n=n: e.activation(
                                        out=dst[:, c0:c0 + w], in_=bk[:, :w], func=AF.Sigmoid, bias=col(bcol + d, n)),
                                        reads=[bkk, "cols"], writes=[dk])
                            P.op("act", lambda e, L=L, d=d, n=n: e.activation(
                                out=a_[:, :L], in_=r_[:, :L], func=AF.Exp, scale=sp8[:, d * KC + n:d * KC + n + 1]),
                                reads=["r", "sp8"], writes=["a"])
                            P.op("dve", lambda e, L=L: e.tensor_tensor(out=a2[:, :L], in0=a_[:, :L], in1=a_[:, :L], op=ALU.mult),
                                 reads=["a"], writes=["a2"])
                            P.op("act", lambda e, L=L: e.activation(out=a2[:, :L], in_=a2[:, :L], func=AF.Sqrt, scale=-1.0, bias=1.0),
                                 reads=["a2"], writes=["a2"])
                            P.op("dve", lambda e, L=L: e.tensor_tensor(out=ig[:, :L], in0=ig[:, :L], in1=xcv[:, :L], op=ALU.mult),
                                 reads=["ig", "xcv"], writes=["ig"])
                            P.op("dve", lambda e, L=L: e.tensor_tensor(out=a2[:, :L], in0=a2[:, :L], in1=ig[:, :L], op=ALU.mult),
                                 reads=["ig", "a2"], writes=["a2"])
                            h0 = col(H0 + d, n) if pi < 0 else zc[:]
                            if d == 0:
                                o_, x0, x1 = hs[0][:, :L], a_[:, :L], a2[:, :L]
                            else:
                                rv = lambda ap_, L=L: AP(ap_.tensor, ap_.offset + L - 1, [list(ap_.ap[0]), [-1, L]])
                                o_, x0, x1 = rv(hs[1][:, :L]), rv(a_[:, :L]), rv(a2[:, :L])
                            P.op("dve", lambda e, o_=o_, x0=x0, x1=x1, h0=h0: e.tensor_tensor_scan(
                                out=o_, data0=x0, data1=x1, initial=h0, op0=ALU.mult, op1=ALU.add),
                                reads=["a", "a2", "cols", "zc"], writes=[("hs", d)])
                            if pi >= 0:
                                fc_ = L - 1 if d == 0 else 0
                                P.op("pool", lambda e, d=d, n=n, pi=pi, fc_=fc_: e.dma_start(
                                    out=o_rg[pi, d, n * 128:(n + 1) * 128].rearrange("(p o) -> p o", o=1),
                                    in_=hs[d][:, fc_:fc_ + 1]), reads=[("hs", d)], dma=True)
                        P.op("dve", lambda e, L=L: e.tensor_tensor(out=hs[0][:, :L], in0=hs[0][:, :L], in1=hs[1][:, :L], op=ALU.add),
                             reads=[("hs", 0), ("hs", 1)], writes=[("hs", 0)])
                        P.op("dve", lambda e, L=L: e.tensor_tensor(out=yg[:, :L], in0=hs[0][:, :L], in1=gt[:, :L], op=ALU.mult),
                             reads=[("hs", 0), "gt"], writes=["yg"])
                        P.op("pool", lambda e, n=n, tok0=tok0, L=L: e.dma_start(
                            out=YT[n * 128:(n + 1) * 128, tok0:tok0 + L], in_=yg[:, :L]), reads=["yg"], dma=True)
                P.barrier()

        if "s5" in MIX:
            s5_w_in = dr("s5_w_in", [D, 1024]); s5_w_glu = dr("s5_w_glu", [1024, 1024]); s5_w_out = dr("s5_w_out", [1024, D])
            s5_a_re = dr("s5_a_re", [2, 64, 64]); s5_a_im = dr("s5_a_im", [2, 64, 64]); s5_log_dt = dr("s5_log_dt", [2, 64])
            s5_b_re = dr("s5_b_re", [2, 64, 64, 16]); s5_b_im = dr("s5_b_im", [2, 64, 64, 16])
            s5_c_re = dr("s5_c_re", [2, 64, 16, 64]); s5_c_im = dr("s5_c_im", [2, 64, 16, 64])
            s5_d = dr("s5_d", [1024]); s5_b_glu = dr("s5_b_glu", [1024])
            st_s5 = dr("st_s5", [2, 2, 64, 64])
            o_s5 = dr("o_s5", [NTP * 2, 2, 2, 64, 64], kind="ExternalOutput")
            UT = dr("s5_UT", [1024, NTOK], kind="Internal")
            YS = dr("s5_YS", [1024, NTOK], kind="Internal")
            ZT = dr("s5_ZT", [1024, NTOK], BF16, kind="Internal")
            Z2 = dr("s5_Z2", [1024, NTOK], BF16, kind="Internal")

        def mixer_s5_p1(B, l):
            prep_GS(l, 1, 4, 5, 1.0)
            for t in range(NT):
                cnd = 0 if t < NTS else 1
                load_x(B, t)
                norm_mod(B, l, cnd, 3)
                proj(B, s5_w_in, KC, lambda k: B.hT[:, k, :], lambda k: ("hT", k), 0, 8,
                     lambda f, pa, ka, t=t: store_fm(B, UT, f, t, pa, ka, AF.Identity, F32))

        def mixer_s5_p2():
            TWO_PI = 6.283185307179586
            with ExitStack() as ph:
                sb_ = lambda name, shape, dt: sbuf(ph, "s5_" + name, shape, dt)
                W = {n: sb_(n, [128, 32, 128], F32) for n in ("WPr", "WPi", "WMr", "WMi")}
                MB = sb_("MB", [128, 8, 4, 2, 128], BF16)
                MC = sb_("MC", [128, 32, 2, 128], BF16)
                Ct = [sb_(f"Ct{i}", [128, 32, 128], F32) for i in range(2)]
                Bt = Ct
                Ht = [sb_(f"Ht{i}", [128, 32, 128], BF16) for i in range(2)]
                T1, T2 = Ct[0], Ct[1]
                TG = [sb_(f"TG{i}", [128, 8, 128], F32) for i in range(2)]
                onesr = sb_("onesr", [128, 128], F32)
                uf = sb_("uf", [128, 8, 128], F32)
                ub = sb_("ub", [128, 8, 128], BF16)
                yf = sb_("yf", [128, 8, 128], F32)
                yo_ = sb_("yo", [128, 8, 128], F32)
                y2 = sb_("y2", [128, 8, 128], F32)
                zb = sb_("zb", [128, 8, 128], BF16)
                dcol = sb_("dcol", [128, 8], F32)
                sm = {n: sb_("sm_" + n, [128, 32], F32) for n in
                      ("are", "aim", "ldt", "dt", "mag", "ang", "sn", "cs", "lr", "li", "den", "lm1", "cr", "ci",
                       "ir", "ii", "t1", "t2", "t3", "pr", "pi", "qr", "qi", "kr", "ki", "hr", "hi", "h0r", "h0i")}
                smi = sb_("smi", [128, 32], I32)
                BB = {n: sb_(n, [128, 32, 16], F32) for n in ("bre", "bim", "bbr", "bbi")}
                XB = Ct[0]
                CN = sb_("CN", [128, 8, 128], F32)
                P.op("dve", lambda e: e.memset(onesr[:], 1.0), writes=["onesr"])
                colload(dcol[:], s5_d, 8)

                def tt(out, a, b, op, eng="dve", r=("tab",), w=("tab",)):
                    P.op(eng, lambda e: e.tensor_tensor(out=out, in0=a, in1=b, op=op), reads=list(r), writes=list(w))

                def ts(out, a, s1, s2, op0, op1=None, r=("tab",), w=("tab",)):
                    if op1 is None:
                        P.op("dve", lambda e: e.tensor_scalar(out=out, in0=a, scalar1=s1, scalar2=None, op0=op0), reads=list(r), writes=list(w))
                    else:
                        P.op("dve", lambda e: e.tensor_scalar(out=out, in0=a, scalar1=s1, scalar2=s2, op0=op0, op1=op1), reads=list(r), writes=list(w))

                def act(out, a, func, r=("tab",), w=("tab",), **kw):
                    P.op("act", lambda e: e.activation(out=out, in_=a, func=func, **kw), reads=list(r), writes=list(w))

                def cmul(outr, outi, ar, ai, br, bi, t1, t2, eng2="dve", r=("tab",), w=("tab",)):
                    tt(t1, ar, br, ALU.mult, r=r, w=w); tt(t2, ai, bi, ALU.mult, r=r, w=w)
                    tt(outr, t1, t2, ALU.subtract, eng=eng2, r=r, w=w)
                    tt(t1, ar, bi, ALU.mult, r=r, w=w); tt(t2, ai, br, ALU.mult, r=r, w=w)
                    tt(outi, t1, t2, ALU.add, eng=eng2, r=r, w=w)

                def sinturn(out, turns):
                    P.op("dve", lambda e: e.tensor_copy(out=smi[:], in_=turns), reads=["tab"], writes=["tab"])
                    P.op("dve", lambda e: e.tensor_copy(out=sm["t2"][:], in_=smi[:]), reads=["tab"], writes=["tab"])
                    tt(sm["t2"][:], turns, sm["t2"][:], ALU.subtract)
                    ts(sm["t2"][:], sm["t2"][:], 0.4999999, -0.4999999, ALU.min, ALU.max)
                    act(out, sm["t2"][:], AF.Sin, scale=TWO_PI)

                def bc(ap2, n):
                    return AP(ap2.tensor, ap2.offset, [list(ap2.ap[0]), list(ap2.ap[1]), [0, n]])

                def sm_load(dst, src2d):
                    P.op("sp", lambda e: e.dma_start(out=dst, in_=src2d.rearrange("g p -> (g p)").rearrange("(s q) -> q s", q=128),
                                                     allow_slow_non_contiguous=True), writes=["tab"], dma=True)

                for d in range(cfg.get("s5_ndir", 2)):
                    sm_load(sm["are"][:], s5_a_re[d]); sm_load(sm["aim"][:], s5_a_im[d])
                    for h in range(2):
                        src = s5_log_dt[d]
                        bsrc = AP(src.tensor, src.offset + h, [[0, 64], [2, 32]])
                        P.op("sp", lambda e, h=h, bsrc=bsrc: e.dma_start(out=sm["ldt"][h * 64:(h + 1) * 64, :], in_=bsrc,
                                                                         allow_slow_non_contiguous=True), writes=["tab"], dma=True)
                    act(sm["dt"][:], sm["ldt"][:], AF.Exp)
                    tt(sm["t1"][:], sm["are"][:], sm["dt"][:], ALU.mult)
                    act(sm["mag"][:], sm["t1"][:], AF.Exp)
                    tt(sm["ang"][:], sm["aim"][:], sm["dt"][:], ALU.mult)
                    ts(sm["ang"][:], sm["ang"][:], 1.0 / TWO_PI, None, ALU.mult)
                    sinturn(sm["sn"][:], sm["ang"][:])
                    ts(sm["t1"][:], sm["ang"][:], 0.25, None, ALU.add)
                    sinturn(sm["cs"][:], sm["t1"][:])
                    tt(sm["lr"][:], sm["mag"][:], sm["cs"][:], ALU.mult)
                    tt(sm["li"][:], sm["mag"][:], sm["sn"][:], ALU.mult)
                    tt(sm["t1"][:], sm["are"][:], sm["are"][:], ALU.mult); tt(sm["t2"][:], sm["aim"][:], sm["aim"][:], ALU.mult)
                    tt(sm["den"][:], sm["t1"][:], sm["t2"][:], ALU.add)
                    P.op("dve", lambda e: e.reciprocal(out=sm["den"][:], in_=sm["den"][:]), reads=["tab"], writes=["tab"])
                    ts(sm["lm1"][:], sm["lr"][:], -1.0, None, ALU.add)
                    tt(sm["t1"][:], sm["lm1"][:], sm["are"][:], ALU.mult); tt(sm["t2"][:], sm["li"][:], sm["aim"][:], ALU.mult)
                    tt(sm["t1"][:], sm["t1"][:], sm["t2"][:], ALU.add); tt(sm["cr"][:], sm["t1"][:], sm["den"][:], ALU.mult)
                    tt(sm["t1"][:], sm["li"][:], sm["are"][:], ALU.mult); tt(sm["t2"][:], sm["lm1"][:], sm["aim"][:], ALU.mult)
                    tt(sm["t1"][:], sm["t1"][:], sm["t2"][:], ALU.subtract); tt(sm["ci"][:], sm["t1"][:], sm["den"][:], ALU.mult)
                    tt(sm["t1"][:], sm["lr"][:], sm["lr"][:], ALU.mult); tt(sm["t2"][:], sm["li"][:], sm["li"][:], ALU.mult)
                    tt(sm["t1"][:], sm["t1"][:], sm["t2"][:], ALU.add)
                    P.op("dve", lambda e: e.reciprocal(out=sm["t1"][:], in_=sm["t1"][:]), reads=["tab"], writes=["tab"])
                    tt(sm["ir"][:], sm["lr"][:], sm["t1"][:], ALU.mult)
                    tt(sm["ii"][:], sm["li"][:], sm["t1"][:], ALU.mult)
                    ts(sm["ii"][:], sm["ii"][:], -1.0, None, ALU.mult)
                    for (Tr, Ti, mr, mi) in ((W["WPr"], W["WPi"], sm["lr"], sm["li"]), (W["WMr"], W["WMi"], sm["ir"], sm["ii"])):
                        P.op("dve", lambda e, Tr=Tr: e.memset(Tr[:, :, 0:1], 1.0), writes=["tab"])
                        P.op("dve", lambda e, Ti=Ti: e.memset(Ti[:, :, 0:1], 0.0), writes=["tab"])
                        P.op("dve", lambda e, mr=mr: e.tensor_copy(out=sm["pr"][:], in_=mr[:]), reads=["tab"], writes=["tab"])
                        P.op("dve", lambda e, mi=mi: e.tensor_copy(out=sm["pi"][:], in_=mi[:]), reads=["tab"], writes=["tab"])
                        for k in range(7):
                            n = 1 << k
                            cmul(Tr[:, :, n:2 * n], Ti[:, :, n:2 * n], Tr[:, :, 0:n], Ti[:, :, 0:n],
                                 bc(sm["pr"][:], n), bc(sm["pi"][:], n), T1[:, :, 0:n], T2[:, :, 0:n])
                            if k < 6:
                                cmul(sm["qr"][:], sm["qi"][:], sm["pr"][:], sm["pi"][:], sm["pr"][:], sm["pi"][:], sm["t1"][:], sm["t2"][:])
                                P.op("dve", lambda e: e.tensor_copy(out=sm["pr"][:], in_=sm["qr"][:]), reads=["tab"], writes=["tab"])
                                P.op("dve", lambda e: e.tensor_copy(out=sm["pi"][:], in_=sm["qi"][:]), reads=["tab"], writes=["tab"])
                    for (nm, src) in (("bre", s5_b_re), ("bim", s5_b_im)):
                        P.op("sp", lambda e, nm=nm, src=src: e.dma_start(
                            out=BB[nm][:], in_=src[d].rearrange("g p c -> (g p) c").rearrange("(s q) c -> q s c", q=128)),
                            writes=["tab"], dma=True)
                    cmul(BB["bbr"][:], BB["bbi"][:], bc(sm["cr"][:], 16), bc(sm["ci"][:], 16), BB["bre"][:], BB["bim"][:],
                         T1[:, :, 0:16], T2[:, :, 0:16])
                    for ri, nm in enumerate(("bbr", "bbi")):
                        P.op("dve", lambda e: e.memset(XB[:], 0.0), reads=["tab"], writes=["tab"])
                        for r_ in range(4):
                            for h in range(2):
                                off = 16 * ((2 * r_ + h) % 8)
                                P.op("dve", lambda e, r_=r_, h=h, off=off, nm=nm: e.tensor_copy(
                                    out=XB[h * 64:(h + 1) * 64, r_::4, off:off + 16], in_=BB[nm][h * 64:(h + 1) * 64, r_::4, :]),
                                    reads=["tab"], writes=["tab"])
                        for sbi in range(32):
                            bk = banks[sbi % 4]
                            P.op("pe", lambda e, bk=bk, sbi=sbi: e.transpose(bk[:, 0:128], XB[:, sbi, :], ident[:]),
                                 reads=["tab", "ident"], writes=[BK[sbi % 4]])
                            P.op("act", lambda e, bk=bk, sbi=sbi, ri=ri: e.copy(out=MB[:, sbi // 4, sbi % 4, ri, :], in_=bk[:, 0:128]),
                                 reads=[BK[sbi % 4]], writes=["MB"])
                    P.op("dve", lambda e: e.memset(MC[:], 0.0), reads=["MC"], writes=["MC"])
                    for ri, src in enumerate((s5_c_re, s5_c_im)):
                        for hh_ in range(2):
                            P.op("sp", lambda e, src=src, hh_=hh_: e.dma_start(
                                out=CN[:, :, hh_ * 64:(hh_ + 1) * 64],
                                in_=src[d].rearrange("g c p -> (g c) p").rearrange("(k r) p -> r k p", r=128)),
                                writes=["CN"], dma=True)
                        for cc in range(8):
                            bk = banks[cc % 4]
                            P.op("pe", lambda e, bk=bk, cc=cc: e.transpose(bk[:, 0:128], CN[:, cc, :], ident[:]),
                                 reads=["CN", "ident"], writes=[BK[cc % 4]])
                            for sbl in range(4):
                                for h in range(2):
                                    gl = 2 * sbl + h
                                    P.op("act", lambda e, bk=bk, cc=cc, sbl=sbl, h=h, gl=gl, ri=ri: e.activation(
                                        out=MC[h * 64:(h + 1) * 64, cc * 4 + sbl, ri, 16 * gl:16 * gl + 16],
                                        in_=bk[h * 64:(h + 1) * 64, 16 * gl:16 * gl + 16], func=AF.Identity,
                                        scale=(1.0 if ri == 0 else -1.0)),
                                        reads=[BK[cc % 4]], writes=["MC"])
                    P.barrier()
                    for (tok0, L, pi) in (seqs if cfg.get("s5_stop", 9) > 1 else []):
                        nch = L // 128
                        if pi < 0:
                            sm_load(sm["h0r"][:], st_s5[d, 0]); sm_load(sm["h0i"][:], st_s5[d, 1])
                            cmul(sm["kr"][:], sm["ki"][:], sm["lr"][:], sm["li"][:], sm["h0r"][:], sm["h0i"][:], sm["t1"][:], sm["t2"][:],
                                 r=("tab", "K"), w=("tab", "K"))
                        else:
                            P.op("dve", lambda e: e.memset(sm["kr"][:], 0.0), reads=["K"], writes=["K"])
                            P.op("dve", lambda e: e.memset(sm["ki"][:], 0.0), reads=["K"], writes=["K"])
                        order = range(nch) if d == 0 else range(nch - 1, -1, -1)
                        for ci_, ch in enumerate(order):
                            t0 = tok0 + ch * 128
                            P.op("sp", lambda e, t0=t0: e.dma_start(out=uf[:], in_=UT[:, t0:t0 + 128].rearrange("(c p) t -> p c t", p=128)),
                                 writes=["uf"], dma=True)
                            P.op("pool", lambda e: e.tensor_copy(out=ub[:], in_=uf[:]), reads=["uf"], writes=["ub"])
                            if d == 1:
                                P.op("sp", lambda e, t0=t0: e.dma_start(out=yf[:], in_=YS[:, t0:t0 + 128].rearrange("(c p) t -> p c t", p=128)),
                                     writes=["yf"], dma=True)

                            def tv(tile_, g0, ng, rev):
                                a = tile_[:, g0:g0 + ng, :]
                                if not rev:
                                    return a
                                return AP(a.tensor, a.offset + 127, [list(a.ap[0]), list(a.ap[1]), [-1, 128]])
                            rev = (d == 1)
                            for grp in range(8):
                                pr_, pi_ = banks[(grp % 3) * 2], banks[(grp % 3) * 2 + 1]
                                kr_, ki_ = BK[(grp % 3) * 2], BK[(grp % 3) * 2 + 1]
                                for sbl in range(4):
                                    P.op("pe", lambda e, pr_=pr_, grp=grp, sbl=sbl: e.matmul(
                                        pr_[:, sbl * 128:(sbl + 1) * 128], lhsT=MB[:, grp, sbl, 0, :], rhs=ub[:, grp, :], start=True, stop=True),
                                        reads=["MB", "ub"], writes=[kr_], sig=(sbl == 3))
                                for sbl in range(4):
                                    P.op("pe", lambda e, pi_=pi_, grp=grp, sbl=sbl: e.matmul(
                                        pi_[:, sbl * 128:(sbl + 1) * 128], lhsT=MB[:, grp, sbl, 1, :], rhs=ub[:, grp, :], start=True, stop=True),
                                        reads=["MB", "ub"], writes=[ki_], sig=(sbl == 3))
                                g0 = grp * 4
                                tg1 = TG[0][:, (grp % 2) * 4:(grp % 2) * 4 + 4, :]
                                tg2 = TG[1][:, (grp % 2) * 4:(grp % 2) * 4 + 4, :]
                                pr3 = pr_[:].rearrange("p (s t) -> p s t", s=4)
                                pi3 = pi_[:].rearrange("p (s t) -> p s t", s=4)
                                wr, wi = tv(W["WMr"], g0, 4, rev), tv(W["WMi"], g0, 4, rev)
                                kk = ("g", grp % 2)
                                tt(tg1, pr3, wr, ALU.mult, r=(kr_, "tab", kk), w=(kk,))
                                tt(tg2, pi3, wi, ALU.mult, r=(ki_, "tab", kk), w=(kk,))
                                tt(Bt[0][:, g0:g0 + 4, :], tg1, tg2, ALU.subtract, eng="pool", r=(kk, ("C", grp)), w=(kk, ("B", grp)))
                                tt(tg1, pi3, wr, ALU.mult, r=(ki_, "tab", kk), w=(kk,))
                                tt(tg2, pr3, wi, ALU.mult, r=(kr_, "tab", kk), w=(kk,))
                                tt(Bt[1][:, g0:g0 + 4, :], tg1, tg2, ALU.add, eng="pool", r=(kk,), w=(kk, ("B", grp)))
                                for sbi in range(g0, g0 + 4):
                                    for ri in range(2):
                                        o_ = Ct[ri][:, sbi, :]; x_ = Bt[ri][:, sbi, :]; on_ = onesr[:]
                                        if rev:
                                            o_ = AP(o_.tensor, o_.offset + 127, [list(o_.ap[0]), [-1, 128]])
                                            x_ = AP(x_.tensor, x_.offset + 127, [list(x_.ap[0]), [-1, 128]])
                                        kcol = (sm["kr"] if ri == 0 else sm["ki"])[:, sbi:sbi + 1]
                                        P.op("dve", lambda e, o_=o_, x_=x_, on_=on_, kcol=kcol: e.tensor_tensor_scan(
                                            out=o_, data0=on_, data1=x_, initial=kcol, op0=ALU.mult, op1=ALU.add),
                                            reads=[("B", grp), "onesr", "K"], writes=[("C", grp)])
                                wr, wi = tv(W["WPr"], g0, 4, rev), tv(W["WPi"], g0, 4, rev)
                                cr3, ci3 = Ct[0][:, g0:g0 + 4, :], Ct[1][:, g0:g0 + 4, :]
                                tt(tg1, cr3, wr, ALU.mult, r=(("C", grp), "tab", kk), w=(kk,))
                                tt(tg2, ci3, wi, ALU.mult, r=(("C", grp), "tab", kk), w=(kk,))
                                tt(Ht[0][:, g0:g0 + 4, :], tg1, tg2, ALU.subtract, eng="pool", r=(kk, ("H", grp)), w=(kk, ("H", grp)))
                                tt(tg1, ci3, wr, ALU.mult, r=(("C", grp), "tab", kk), w=(kk,))
                                tt(tg2, cr3, wi, ALU.mult, r=(("C", grp), "tab", kk), w=(kk,))
                                tt(Ht[1][:, g0:g0 + 4, :], tg1, tg2, ALU.add, eng="pool", r=(kk,), w=(kk, ("H", grp)))
                            lc = 0 if rev else 127
                            wl = 127
                            Cg = [("C", g) for g in range(8)]
                            cmul(sm["hr"][:], sm["hi"][:], W["WPr"][:, :, wl], W["WPi"][:, :, wl], Ct[0][:, :, lc], Ct[1][:, :, lc],
                                 sm["t1"][:], sm["t2"][:], r=["tab", "K"] + Cg, w=("tab", "K"))
                            cmul(sm["kr"][:], sm["ki"][:], sm["lr"][:], sm["li"][:], sm["hr"][:], sm["hi"][:], sm["t1"][:], sm["t2"][:],
                                 r=("tab", "K"), w=("tab", "K"))
                            if pi >= 0 and ci_ == nch - 1:
                                for ri, nm in enumerate(("hr", "hi")):
                                    P.op("pool", lambda e, ri=ri, nm=nm, pi=pi: e.dma_start(
                                        out=o_s5[pi, d, ri].rearrange("g p -> (g p)").rearrange("(s q) -> q s", q=128), in_=sm[nm][:],
                                        allow_slow_non_contiguous=True), reads=["tab", "K"], dma=True)
                            for cc in range(8):
                                bk, bkk = banks[6 + cc // 4], BK[6 + cc // 4]
                                i_ = 0
                                for sbl in range(4):
                                    for ri in range(2):
                                        sbi = cc * 4 + sbl
                                        P.op("pe", lambda e, bk=bk, cc=cc, sbi=sbi, ri=ri, i_=i_: e.matmul(
                                            bk[:, (cc % 4) * 128:(cc % 4 + 1) * 128], lhsT=MC[:, sbi, ri, :], rhs=Ht[ri][:, sbi, :],
                                            start=(i_ == 0), stop=(i_ == 7)),
                                            reads=["MC", ("H", cc)], writes=[bkk], sig=(i_ == 7))
                                        i_ += 1
                            for hb in range(2):
                                bk, bkk = banks[6 + hb], BK[6 + hb]
                                ysl = (yo_ if d == 0 else y2)[:, hb * 4:(hb + 1) * 4, :]
                                if d == 0:
                                    P.op("act", lambda e, bk=bk, ysl=ysl: e.copy(out=ysl, in_=bk[:].rearrange("p (c t) -> p c t", c=4)),
                                         reads=[bkk], writes=[("yo", hb)])
                                else:
                                    P.op("dve", lambda e, bk=bk, ysl=ysl, hb=hb: e.tensor_tensor(
                                        out=ysl, in0=bk[:].rearrange("p (c t) -> p c t", c=4), in1=yf[:, hb * 4:(hb + 1) * 4, :], op=ALU.add),
                                        reads=[bkk, "yf"], writes=[("y2", hb)])
                            if d == 0:
                                P.op("pool", lambda e, t0=t0: e.dma_start(out=YS[:, t0:t0 + 128].rearrange("(c p) t -> p c t", p=128), in_=yo_[:]),
                                     reads=[("yo", 0), ("yo", 1)], dma=True)
                            else:
                                for cc in range(8):
                                    P.op("dve", lambda e, cc=cc: e.scalar_tensor_tensor(
                                        out=y2[:, cc, :], in0=uf[:, cc, :], scalar=dcol[:, cc:cc + 1], in1=y2[:, cc, :], op0=ALU.mult, op1=ALU.add),
                                        reads=["uf", "cols", ("y2", cc // 4)], writes=[("y2", cc // 4)])
                                P.op("act", lambda e: e.activation(out=zb[:], in_=y2[:], func=AF.Gelu_apprx_tanh),
                                     reads=[("y2", 0), ("y2", 1)], writes=["zb"])
                                P.op("pool", lambda e, t0=t0: e.dma_start(out=ZT[:, t0:t0 + 128].rearrange("(c p) t -> p c t", p=128), in_=zb[:]),
                                     reads=["zb"], dma=True)
                P.barrier()

        def mixer_s5_p3(B, l):
            with ExitStack() as ph2:
                bg = sbuf(ph2, f"bglu_{scopeid[0]}", [128, 8], F32)
                colload(bg[:], s5_b_glu, 8)
                for t in range(NT):
                    P.op("sp", lambda e, t=t: e.dma_start(
                        out=B.hT[:, :8, :], in_=ZT[:, t * TT:(t + 1) * TT].rearrange("(k p) t -> p k t", p=128)),
                        writes=[("hT", k) for k in range(8)], dma=True)

                    def glu(f, pa, ka, t=t):
                        i = cnt["ev"]
                        cnt["ev"] += 1
                        P.op("act", lambda e: e.activation(out=B.ssb[i % 2][:], in_=pa[:], func=AF.Sigmoid, bias=bg[:, f:f + 1]),
                             reads=[ka, "cols"], writes=[("ssb", i % 2)])
                        P.op("dve", lambda e: e.tensor_tensor(out=B.obf[i % 3][:], in0=B.ssb[i % 2][:], in1=B.hT[:, f, :], op=ALU.mult),
                             reads=[("ssb", i % 2), ("hT", f)], writes=[("obf", i % 3)])
                        P.op("pool", lambda e: e.dma_start(out=Z2[f * 128:(f + 1) * 128, t * TT:(t + 1) * TT], in_=B.obf[i % 3][:]),
                             reads=[("obf", i % 3)], dma=True)
                    proj(B, s5_w_glu, 8, lambda k: B.hT[:, k, :], lambda k: ("hT", k), 0, 8, glu)
                P.barrier()

        if "da" in MIX:
            da_wq = dr("da_wq", [D, D]); da_wk = dr("da_wk", [D, D]); da_wv = dr("da_wv", [D, D]); da_wo = dr("da_wo", [D, D])
            da_lam = dr("da_lam", [4, 128]); da_subln_g = dr("da_subln_g", [256])
            kcache = dr("kcache", [256, D]); vcache = dr("vcache", [256, D])
            rope_cos = dr("rope_cos", [128, 4096]); rope_sin = dr("rope_sin", [128, 4096]); rope_perm = dr("rope_perm", [128, 128])
            o_dk = dr("o_dk", [NTP * 512, D], kind="ExternalOutput")
            o_dv = dr("o_dv", [NTP * 512, D], kind="ExternalOutput")
            QT = dr("da_QT", [D, NTOK], BF16, kind="Internal")
            KT = dr("da_KT", [D, 256 + NTOK], BF16, kind="Internal")
            VT = dr("da_VT", [256 + NTOK, D], BF16, kind="Internal")
            AT = dr("da_AT", [D, NTOK], BF16, kind="Internal")
            KO = dr("da_KO", [NTP * 512, D], kind="Internal")
            VO = dr("da_VO", [NTP * 512, D], kind="Internal")
            LAM_INIT = 0.8 - 0.6 * float(np.exp(-0.3 * 2))

        def mixer_da_p1(B, l):
            prep_GS(l, 1, 4, 5, 1.0)
            with ExitStack() as ph2:
                sid = scopeid[0]
                perm = sbuf(ph2, f"perm_{sid}", [128, 128], F32)
                rc = sbuf(ph2, f"rc_{sid}", [128, TT], F32)
                rs = sbuf(ph2, f"rs_{sid}", [128, TT], F32)
                tmk = [sbuf(ph2, f"tmk{i}_{sid}", [128, 256], F32) for i in range(2)]
                tmb = [sbuf(ph2, f"tmb{i}_{sid}", [128, 256], BF16) for i in range(2)]
                P.op("sp", lambda e: e.dma_start(out=perm[:], in_=rope_perm), writes=["perm"], dma=True)
                for tb in range(2):
                    P.op("sp", lambda e, tb=tb: e.dma_start(out=B.xt[:, :4, :].rearrange("p a t -> p (a t)"), in_=kcache[tb * 128:(tb + 1) * 128, :]),
                         writes=[("xt", k) for k in range(KC)], dma=True)
                    for k in range(KC):
                        bk, bkk = banks[k % 4], BK[k % 4]
                        P.op("pe", lambda e, bk=bk, k=k: e.transpose(
                            bk[:, 0:128], B.xt[:, :4, :].rearrange("p a t -> p (a t)")[:, k * 128:(k + 1) * 128], ident[:]),
                            reads=[("xt", 0), "ident"], writes=[bkk])
                        i = cnt["ev"]; cnt["ev"] += 1
                        P.op("act", lambda e, bk=bk, i=i: e.copy(out=B.obf[i % 3][:, 0:128], in_=bk[:, 0:128]), reads=[bkk], writes=[("obf", i % 3)])
                        P.op("pool", lambda e, i=i, k=k, tb=tb: e.dma_start(out=KT[k * 128:(k + 1) * 128, tb * 128:(tb + 1) * 128], in_=B.obf[i % 3][:, 0:128]),
                             reads=[("obf", i % 3)], dma=True)
                    P.op("sp", lambda e, tb=tb: e.dma_start(out=B.xt[:, 4:8, :].rearrange("p a t -> p (a t)"), in_=vcache[tb * 128:(tb + 1) * 128, :]),
                         writes=[("xt", k) for k in range(KC)], dma=True)
                    P.op("dve", lambda e: e.tensor_copy(out=B.hh[:, 0:4, :], in_=B.xt[:, 4:8, :]), reads=[("xt", 0)], writes=[("hh", 0)])
                    P.op("pool", lambda e, tb=tb: e.dma_start(out=VT[tb * 128:(tb + 1) * 128, :], in_=B.hh[:, 0:4, :].rearrange("p a t -> p (a t)")),
                         reads=[("hh", 0)], dma=True)
                for t in (range(NT) if cfg.get("da_p1", 9) > 1 else []):
                    cnd = 0 if t < NTS else 1
                    load_x(B, t)
                    norm_mod(B, l, cnd, 3)
                    if cnd == 0:
                        P.op("sp", lambda e, t=t: e.dma_start(out=rc[:], in_=rope_cos[:, t * TT:(t + 1) * TT]), writes=["rc"], dma=True)
                        P.op("sp", lambda e, t=t: e.dma_start(out=rs[:], in_=rope_sin[:, t * TT:(t + 1) * TT]), writes=["rs"], dma=True)

                    def qk_consume(f, pa, ka, t=t, cnd=cnd, scr=None, coff=0):
                        i = cnt["ev"]; cnt["ev"] += 1
                        ob, obk = B.obf[i % 3], ("obf", i % 3)
                        if cnd == 1:
                            P.op("act", lambda e: e.copy(out=ob[:], in_=pa[:]), reads=[ka], writes=[obk])
                        else:
                            qs, qsk = B.ssb[i % 2], ("ssb", i % 2)
                            t1, t1k = B.tt[i % 2], ("tt", i % 2)
                            pw, pwk = banks[4 + i % 4], BK[4 + i % 4]
                            P.op("act", lambda e: e.copy(out=qs[:], in_=pa[:]), reads=[ka], writes=[qsk])
                            P.op("pe", lambda e: e.matmul(pw[:], lhsT=perm[:], rhs=qs[:], start=True, stop=True), reads=["perm", qsk], writes=[pwk])
                            P.op("dve", lambda e: e.tensor_tensor(out=t1[:], in0=qs[:], in1=rc[:], op=ALU.mult), reads=[qsk, "rc"], writes=[t1k])
                            P.op("dve", lambda e: e.tensor_tensor(out=qs[:], in0=pw[:], in1=rs[:], op=ALU.mult), reads=[pwk, "rs", qsk], writes=[qsk])
                            P.op("dve", lambda e: e.tensor_tensor(out=ob[:], in0=t1[:], in1=qs[:], op=ALU.add), reads=[t1k, qsk], writes=[obk])
                        P.op("pool", lambda e: e.dma_start(out=scr[f * 128:(f + 1) * 128, coff + t * TT:coff + (t + 1) * TT], in_=ob[:]),
                             reads=[obk], dma=True)
                    proj(B, da_wq, KC, lambda k: B.hT[:, k, :], lambda k: ("hT", k), 0, KC,
                         lambda f, pa, ka: qk_consume(f, pa, ka, scr=QT, coff=0))
                    proj(B, da_wk, KC, lambda k: B.hT[:, k, :], lambda k: ("hT", k), 0, KC,
                         lambda f, pa, ka: qk_consume(f, pa, ka, scr=KT, coff=256))
                    for (wap, is_v) in ((da_wv, True), (da_wk, False)):
                        if (not is_v and cnd == 0) or cfg.get("da_p1", 9) < 3:
                            continue
                        for cg in range(8):
                            bw, kw = wload(B, wap, 0, KC, cg * 256, 256)
                            for tb in range(4):
                                it = cnt["it"]; cnt["it"] += 1
                                bk, bkk = banks[it % 4], BK[it % 4]
                                for k in range(KC):
                                    P.op("pe", lambda e, bk=bk, bw=bw, k=k, tb=tb: e.matmul(
                                        bk[:, 0:256], lhsT=B.hT[:, k, tb * 128:(tb + 1) * 128], rhs=bw[:, k, :], start=(k == 0), stop=(k == KC - 1)),
                                        reads=[kw, ("hT", k)], writes=[bkk], sig=(k == KC - 1))
                                i = cnt["ev"]; cnt["ev"] += 1
                                r0 = t * TT + tb * 128
                                P.op("act", lambda e, bk=bk, i=i: e.copy(out=tmk[i % 2][:], in_=bk[:, 0:256]), reads=[bkk], writes=[("tmk", i % 2)])
                                if is_v:
                                    P.op("dve", lambda e, i=i: e.tensor_copy(out=tmb[i % 2][:], in_=tmk[i % 2][:]), reads=[("tmk", i % 2)], writes=[("tmb", i % 2)])
                                    P.op("pool", lambda e, i=i, r0=r0, cg=cg: e.dma_start(out=VT[256 + r0:256 + r0 + 128, cg * 256:(cg + 1) * 256], in_=tmb[i % 2][:]),
                                         reads=[("tmb", i % 2)], dma=True)
                                if cnd == 1:
                                    oo = VO if is_v else KO
                                    p0 = r0 - LS
                                    P.op("pool", lambda e, i=i, p0=p0, cg=cg, oo=oo: e.dma_start(out=oo[p0:p0 + 128, cg * 256:(cg + 1) * 256], in_=tmk[i % 2][:]),
                                         reads=[("tmk", i % 2)], writes=[("oo", is_v)], dma=True)
                if cfg.get("da_p1", 9) >= 3:
                    for r in range(NTP * 4):
                        P.op("sp", lambda e, r=r: e.dma_start(out=o_dk[r * 128:(r + 1) * 128, :], in_=KO[r * 128:(r + 1) * 128, :]), reads=[("oo", False)], dma=True)
                        P.op("sp", lambda e, r=r: e.dma_start(out=o_dv[r * 128:(r + 1) * 128, :], in_=VO[r * 128:(r + 1) * 128, :]), reads=[("oo", True)], dma=True)
                P.barrier()

        def mixer_da_p2():
            SCALE = 1.0 / float(np.sqrt(128.0))
            NKM = (256 + LS) // 128
            with ExitStack() as ph:
                sb_ = lambda name, shape, dt: sbuf(ph, "da_" + name, shape, dt)
                Vh = sb_("Vh", [128, NKM, 256], BF16)
                Kf = [sb_(f"Kf{c}", [128, 256 + LS], BF16) for c in range(2)]
                Qf = [sb_(f"Qf{c}", [128, LS], BF16) for c in range(2)]
                pt = [sb_(f"pt{i}", [128, TT], BF16) for i in range(3)]
                onesb = sb_("onesb", [128, 128], BF16)
                rden = sb_("rden", [128, TT], F32)
                oc0 = [sb_(f"oc0{j}", [128, TT], F32) for j in range(2)]
                dd = [sb_(f"dd{j}", [128, TT], F32) for j in range(2)]
                sqd = [sb_(f"sqd{j}", [128, TT], F32) for j in range(2)]
                rs_ = sb_("rs", [128, TT], F32)
                ob = [sb_(f"ob{j}", [128, TT], BF16) for j in range(2)]
                lc = sb_("lc", [128, 8], F32)
                gc2 = sb_("gc2", [128, 2], F32)
                epsc = sb_("epsc", [128, 1], F32)
                P.op("dve", lambda e: e.memset(epsc[:], EPS), writes=["epsc"])
                P.op("dve", lambda e: e.memset(onesb[:], 1.0), writes=["onesb"])
                for j in range(4):
                    colload(lc[:, j:j + 1], da_lam[j], 1)
                colload(gc2[:], da_subln_g, 2)
                P.op("dve", lambda e: e.tensor_scalar(out=gc2[:], in0=gc2[:], scalar1=1.0 - LAM_INIT, scalar2=None, op0=ALU.mult), reads=["cols"], writes=["cols"])
                P.op("dve", lambda e: e.tensor_tensor(out=lc[:, 4:6], in0=lc[:, 0:4:2], in1=lc[:, 1:4:2], op=ALU.mult), reads=["cols"], writes=["cols"])
                P.op("pe", lambda e: e.matmul(banks[6][:, 0:2], lhsT=ones[:], rhs=lc[:, 4:6], start=True, stop=True), reads=["cols", "ones"], writes=[BK[6]])
                P.op("act", lambda e: e.activation(out=lc[:, 4:6], in_=banks[6][:, 0:2], func=AF.Exp), reads=[BK[6]], writes=["cols"])
                P.op("dve", lambda e: e.tensor_tensor(out=lc[:, 6:7], in0=lc[:, 5:6], in1=lc[:, 4:5], op=ALU.subtract), reads=["cols"], writes=["cols"])
                P.op("dve", lambda e: e.tensor_scalar(out=lc[:, 6:7], in0=lc[:, 6:7], scalar1=-LAM_INIT, scalar2=None, op0=ALU.add), reads=["cols"], writes=["cols"])
                nlam = lc[:, 6:7]
                for (tok0, L, pi) in seqs:
                    k0 = 0 if pi < 0 else 256 + tok0
                    NK = (256 + L) if pi < 0 else L
                    nkc = NK // 128
                    for h in range(8):
                        P.op("sp", lambda e, h=h, k0=k0, NK=NK, nkc=nkc: e.dma_start(
                            out=Vh[:, :nkc, :], in_=VT[k0:k0 + NK, h * 256:(h + 1) * 256].rearrange("(c p) v -> p c v", p=128)),
                            writes=["Vh"], dma=True)
                        for c in range(2):
                            f = h * 2 + c
                            P.op("sp", lambda e, f=f, c=c, k0=k0, NK=NK: e.dma_start(out=Kf[c][:, :NK], in_=KT[f * 128:(f + 1) * 128, k0:k0 + NK]),
                                 writes=[("Kf", c)], dma=True)
                            P.op("sp", lambda e, f=f, c=c, tok0=tok0, L=L: e.dma_start(out=Qf[c][:, :L], in_=QT[f * 128:(f + 1) * 128, tok0:tok0 + L]),
                                 writes=[("Qf", c)], dma=True)
                        for q0 in range(0, L, TT):
                            qw = min(TT, L - q0)
                            for c in range(2):
                                for kc in range(nkc):
                                    bs, bsk = banks[kc % 2], BK[kc % 2]
                                    p_, pk = pt[kc % 3], ("pt", kc % 3)
                                    P.op("pe", lambda e, bs=bs, c=c, kc=kc, q0=q0, qw=qw: e.matmul(
                                        bs[:, :qw], lhsT=Kf[c][:, kc * 128:(kc + 1) * 128], rhs=Qf[c][:, q0:q0 + qw], start=True, stop=True),
                                        reads=[("Kf", c), ("Qf", c)], writes=[bsk])
                                    P.op("act", lambda e, bs=bs, p_=p_, qw=qw: e.activation(out=p_[:, :qw], in_=bs[:, :qw], func=AF.Exp, scale=SCALE),
                                         reads=[bsk], writes=[pk])
                                    last = (kc == nkc - 1)
                                    P.op("pe", lambda e, p_=p_, kc=kc, qw=qw, last=last: e.matmul(
                                        banks[2][:, :qw], lhsT=onesb[:], rhs=p_[:, :qw], start=(kc == 0), stop=last),
                                        reads=["onesb", pk], writes=[BK[2]], sig=False)
                                    for j in range(2):
                                        P.op("pe", lambda e, p_=p_, kc=kc, qw=qw, j=j, last=last: e.matmul(
                                            banks[3 + j][:, :qw], lhsT=Vh[:, kc, j * 128:(j + 1) * 128], rhs=p_[:, :qw], start=(kc == 0), stop=last),
                                            reads=["Vh", pk], writes=[BK[3 + j]], sig=(j == 1))
                                P.op("dve", lambda e, qw=qw: e.reciprocal(out=rden[:, :qw], in_=banks[2][:, :qw]), reads=[BK[2]], writes=["rden"])
                                for j in range(2):
                                    if c == 0:
                                        P.op("dve", lambda e, j=j, qw=qw: e.tensor_tensor(out=oc0[j][:, :qw], in0=banks[3 + j][:, :qw], in1=rden[:, :qw], op=ALU.mult),
                                             reads=[BK[3 + j], "rden"], writes=[("oc0", j)])
                                    else:
                                        P.op("dve", lambda e, j=j, qw=qw: e.tensor_tensor(out=dd[j][:, :qw], in0=banks[3 + j][:, :qw], in1=rden[:, :qw], op=ALU.mult),
                                             reads=[BK[3 + j], "rden"], writes=[("dd", j)])
                                        P.op("dve", lambda e, j=j, qw=qw: e.scalar_tensor_tensor(
                                            out=dd[j][:, :qw], in0=dd[j][:, :qw], scalar=nlam, in1=oc0[j][:, :qw], op0=ALU.mult, op1=ALU.add),
                                            reads=[("dd", j), ("oc0", j), "cols"], writes=[("dd", j)])
                            for j in range(2):
                                P.op("act", lambda e, j=j, qw=qw: e.activation(out=sqd[j][:, :qw], in_=dd[j][:, :qw], func=AF.Square),
                                     reads=[("dd", j)], writes=[("sqd", j)])
                                P.op("pe", lambda e, j=j, qw=qw: e.matmul(banks[5][:, :qw], lhsT=ones[:], rhs=sqd[j][:, :qw], start=(j == 0), stop=(j == 1)),
                                     reads=["ones", ("sqd", j)], writes=[BK[5]], sig=True)
                            P.op("act", lambda e, qw=qw: e.activation(out=rs_[:, :qw], in_=banks[5][:, :qw], func=AF.Sqrt, scale=1.0 / 256, bias=epsc[:]),
                                 reads=[BK[5], "epsc"], writes=["rs"])
                            P.op("dve", lambda e, qw=qw: e.reciprocal(out=rs_[:, :qw], in_=rs_[:, :qw]), reads=["rs"], writes=["rs"])
                            for j in range(2):
                                P.op("dve", lambda e, j=j, qw=qw: e.tensor_tensor(out=dd[j][:, :qw], in0=dd[j][:, :qw], in1=rs_[:, :qw], op=ALU.mult),
                                     reads=[("dd", j), "rs"], writes=[("dd", j)])
                                P.op("act", lambda e, j=j, qw=qw: e.activation(out=ob[j][:, :qw], in_=dd[j][:, :qw], func=AF.Identity, scale=gc2[:, j:j + 1]),
                                     reads=[("dd", j), "cols"], writes=[("ob", j)])
                                P.op("pool", lambda e, j=j, qw=qw, h=h, tok0=tok0, q0=q0: e.dma_start(
                                    out=AT[h * 256 + j * 128:h * 256 + (j + 1) * 128, tok0 + q0:tok0 + q0 + qw], in_=ob[j][:, :qw]),
                                    reads=[("ob", j)], dma=True)
                P.barrier()

        if "ml" in MIX:
            MW = 4096
            ml_w_up = dr("ml_w_up", [D, MW]); ml_w_o = dr("ml_w_o", [D, MW]); ml_w_down = dr("ml_w_down", [MW, D])
            ml_conv_w = dr("ml_conv_w", [4, MW]); ml_conv_b = dr("ml_conv_b", [MW])
            ml_wq = dr("ml_wq", [8, 512, 512]); ml_wk = dr("ml_wk", [8, 512, 512]); ml_wv = dr("ml_wv", [8, 512, 512])
            ml_w_gate = dr("ml_w_gate", [3, MW, 32]); ml_b_gate = dr("ml_b_gate", [32])
            ml_b_o = dr("ml_b_o", [MW]); ml_gn = dr("ml_gn", [MW]); ml_skip = dr("ml_skip", [MW])
            st_C = dr("st_C", [2, 8, 512, 512]); st_n = dr("st_n", [2, 8, 512]); st_m = dr("st_m", [2, 8])
            ml_masks = dr("ml_masks", [2, 64, 64])
            o_C = dr("o_C", [NTP * 2, 2, 8, 512, 512], kind="ExternalOutput")
            o_n = dr("o_n", [NTP * 2, 2, 8, 512], kind="ExternalOutput")
            o_m = dr("o_m", [NTP * 2, 2, 8], kind="ExternalOutput")
            MXI = dr("ml_XI", [MW, NTOK], kind="Internal")
            MXC = dr("ml_XC", [MW, NTOK], BF16, kind="Internal"); MXB = dr("ml_XB", [MW, NTOK], BF16, kind="Internal")
            MOG = dr("ml_OG", [MW, NTOK], BF16, kind="Internal")
            MQ = dr("ml_Q", [MW, NTOK], BF16, kind="Internal"); MK = dr("ml_K", [MW, NTOK], BF16, kind="Internal")
            MKT = dr("ml_KT", [NTOK, MW], BF16, kind="Internal"); MVT = dr("ml_VT", [NTOK, MW], BF16, kind="Internal")
            MG = dr("ml_G", [NTOK, 32], kind=("ExternalOutput" if cfg.get("dbg") else "Internal"))
            CELL = dr("ml_CELL", [MW, NTOK], kind="Internal")
            MYT = dr("ml_YT", [MW, NTOK], BF16, kind="Internal")

        def mixer_ml_p1(B, l):
            prep_GS(l, 1, 4, 5, 1.0)
            with ExitStack() as ph2:
                bo = sbuf(ph2, f"mlbo_{scopeid[0]}", [128, 32], F32)
                colload(bo[:], ml_b_o, 32)
                for t in range(NT):
                    cnd = 0 if t < NTS else 1
                    load_x(B, t)
                    norm_mod(B, l, cnd, 3)
                    proj(B, ml_w_up, KC, lambda k: B.hT[:, k, :], lambda k: ("hT", k), 0, 32,
                         lambda f, pa, ka, t=t: store_fm(B, MXI, f, t, pa, ka, AF.Identity, F32))
                    proj(B, ml_w_o, KC, lambda k: B.hT[:, k, :], lambda k: ("hT", k), 0, 32,
                         lambda f, pa, ka, t=t: store_fm(B, MOG, f, t, pa, ka, AF.Sigmoid, BF16, bias=bo[:, f:f + 1]))
                P.barrier()

        def mixer_ml_p1b():
            with ExitStack() as ph:
                sb_ = lambda name, shape, dt: sbuf(ph, "mlb_" + name, shape, dt)
                xi = sb_("xi", [128, LS + 3], F32)
                xcv = sb_("xcv", [128, LS], F32)
                sg = sb_("sg", [128, LS], F32)
                xcb = sb_("xcb", [128, LS], BF16)
                xib = sb_("xib", [128, LS], BF16)
                cols = sb_("cols", [128, 5 * 32], F32)
                for j in range(4):
                    colload(cols[:, j * 32:(j + 1) * 32], ml_conv_w[j], 32)
                colload(cols[:, 128:160], ml_conv_b, 32)
                col = lambda j, n: cols[:, j * 32 + n:j * 32 + n + 1]
                for n in range(32):
                    for (tok0, L, pi) in seqs:
                        P.op("dve", lambda e: e.memset(xi[:, 0:1], 0.0), writes=["xi"])
                        P.op("dve", lambda e, L=L: e.memset(xi[:, L + 1:L + 3], 0.0), writes=["xi"])
                        P.op("sp", lambda e, n=n, tok0=tok0, L=L: e.dma_start(out=xi[:, 1:L + 1], in_=MXI[n * 128:(n + 1) * 128, tok0:tok0 + L]),
                             writes=["xi"], dma=True)
                        P.op("dve", lambda e, n=n, L=L: e.tensor_scalar(out=xcv[:, :L], in0=xi[:, 0:L], scalar1=col(0, n), scalar2=col(4, n),
                                                                        op0=ALU.mult, op1=ALU.add), reads=["xi", "cols"], writes=["xcv"])
                        for j in range(1, 4):
                            P.op("dve", lambda e, n=n, L=L, j=j: e.scalar_tensor_tensor(
                                out=xcv[:, :L], in0=xi[:, j:j + L], scalar=col(j, n), in1=xcv[:, :L], op0=ALU.mult, op1=ALU.add),
                                reads=["xi", "cols", "xcv"], writes=["xcv"])
                        P.op("act", lambda e, L=L: e.activation(out=xcb[:, :L], in_=xcv[:, :L], func=AF.Silu), reads=["xcv"], writes=["xcb"])
                        P.op("pool", lambda e, L=L: e.tensor_copy(out=xib[:, :L], in_=xi[:, 1:L + 1]), reads=["xi"], writes=["xib"])
                        P.op("pool", lambda e, n=n, tok0=tok0, L=L: e.dma_start(out=MXC[n * 128:(n + 1) * 128, tok0:tok0 + L], in_=xcb[:, :L]),
                             reads=["xcb"], dma=True)
                        P.op("pool", lambda e, n=n, tok0=tok0, L=L: e.dma_start(out=MXB[n * 128:(n + 1) * 128, tok0:tok0 + L], in_=xib[:, :L]),
                             reads=["xib"], dma=True)
                P.barrier()

        def mixer_ml_p2a(B):
            KS = 1.0 / float(np.sqrt(512.0))
            with ExitStack() as ph2:
                sid = scopeid[0]
                wg = sbuf(ph2, f"mlwg_{sid}", [128, 3, 32, 32], BF16)
                bg = sbuf(ph2, f"mlbg_{sid}", [128, 32], F32)
                gsb = sbuf(ph2, f"mlgsb_{sid}", [128, 4, 32], F32)
                tmt = [sbuf(ph2, f"mltmt{i}_{sid}", [128, 512], BF16) for i in range(2)]
                for m_ in range(3):
                    P.op("pool", lambda e, m_=m_: e.dma_start(out=wg[:, m_, :, :], in_=ml_w_gate[m_].rearrange("(c p) j -> p c j", p=128)),
                         writes=["wg"], dma=True)
                bsrc = AP(ml_b_gate.tensor, ml_b_gate.offset, [[0, 128], [1, 32]])
                P.op("sp", lambda e: e.dma_start(out=bg[:], in_=bsrc, allow_slow_non_contiguous=True), writes=["bg"], dma=True)
                for t in range(NT):
                    gcount = [0]
                    for h in range(8):
                        P.op("sp", lambda e, t=t, h=h: e.dma_start(
                            out=B.hT[:, 0:4, :], in_=MXC[h * 512:(h + 1) * 512, t * TT:(t + 1) * TT].rearrange("(k p) t -> p k t", p=128)),
                            writes=[("hT", k) for k in range(4)], dma=True)
                        P.op("sp", lambda e, t=t, h=h: e.dma_start(
                            out=B.hT[:, 4:8, :], in_=MXB[h * 512:(h + 1) * 512, t * TT:(t + 1) * TT].rearrange("(k p) t -> p k t", p=128)),
                            writes=[("hT", k) for k in range(4, 8)], dma=True)

                        def fm_consume(f, pa, ka, m_, scr, scale, t=t, h=h):
                            i = cnt["ev"]; cnt["ev"] += 1
                            ob, obk = B.obf[i % 3], ("obf", i % 3)
                            P.op("act", lambda e: e.activation(out=ob[:], in_=pa[:], func=AF.Identity, scale=scale), reads=[ka], writes=[obk])
                            if scr is not None:
                                P.op("pool", lambda e: e.dma_start(out=scr[(h * 4 + f) * 128:(h * 4 + f + 1) * 128, t * TT:(t + 1) * TT], in_=ob[:]),
                                     reads=[obk], dma=True)
                            for tb in range(4):
                                g = gcount[0]
                                P.op("pe", lambda e, tb=tb, g=g: e.matmul(
                                    banks[4 + tb][:, 0:32], lhsT=ob[:, tb * 128:(tb + 1) * 128], rhs=wg[:, m_, h * 4 + f, :],
                                    start=(g == 0), stop=(g == 95)), reads=[obk, "wg"], writes=[BK[4 + tb]], sig=True)
                            gcount[0] += 1
                        src_c = lambda k: B.hT[:, k, :]
                        src_i = lambda k: B.hT[:, 4 + k, :]
                        proj(B, ml_wq[h], 4, src_c, lambda k: ("hT", k), 0, 4, lambda f, pa, ka: fm_consume(f, pa, ka, 0, MQ, 1.0))
                        proj(B, ml_wk[h], 4, src_c, lambda k: ("hT", k), 0, 4, lambda f, pa, ka: fm_consume(f, pa, ka, 1, MK, KS))
                        proj(B, ml_wv[h], 4, src_i, lambda k: ("hT", 4 + k), 0, 4, lambda f, pa, ka: fm_consume(f, pa, ka, 2, None, 1.0))
                        for (wap, off, scr, scale) in ((ml_wk, 0, MKT, KS), (ml_wv, 4, MVT, 1.0)):
                            bw, kw = wload(B, wap[h], 0, 4, 0, 512)
                            for tb in range(4):
                                it = cnt["it"]; cnt["it"] += 1
                                bk, bkk = banks[it % 4], BK[it % 4]
                                for k in range(4):
                                    P.op("pe", lambda e, bk=bk, bw=bw, k=k, tb=tb, off=off: e.matmul(
                                        bk[:], lhsT=B.hT[:, off + k, tb * 128:(tb + 1) * 128], rhs=bw[:, k, :], start=(k == 0), stop=(k == 3)),
                                        reads=[kw, ("hT", off + k)], writes=[bkk], sig=(k == 3))
                                i = cnt["ev"]; cnt["ev"] += 1
                                r0 = t * TT + tb * 128
                                P.op("act", lambda e, bk=bk, i=i, scale=scale: e.activation(out=tmt[i % 2][:], in_=bk[:], func=AF.Identity, scale=scale),
                                     reads=[bkk], writes=[("tmt", i % 2)])
                                P.op("pool", lambda e, i=i, r0=r0, h=h, scr=scr: e.dma_start(out=scr[r0:r0 + 128, h * 512:(h + 1) * 512], in_=tmt[i % 2][:]),
                                     reads=[("tmt", i % 2)], dma=True)
                    for tb in range(4):
                        P.op("dve", lambda e, tb=tb: e.tensor_tensor(out=gsb[:, tb, :], in0=banks[4 + tb][:, 0:32], in1=bg[:], op=ALU.add),
                             reads=[BK[4 + tb], "bg"], writes=["gsb"])
                    P.op("pool", lambda e, t=t: e.dma_start(out=MG[t * TT:(t + 1) * TT, :].rearrange("(b p) j -> p b j", p=128), in_=gsb[:]),
                         reads=["gsb"], dma=True)
                P.barrier()

        def mixer_ml_p2b():
            with ExitStack() as ph:
                sb_ = lambda name, shape, dt: sbuf(ph, "ml2_" + name, shape, dt)
                GGs = sb_("GGs", [128, LS // 128, 32], F32)
                igr = sb_("igr", [128, LS], F32)
                lfr = sb_("lfr", [128, LS], F32)
                grep = [sb_(f"grep{i}", [128, 128], F32) for i in range(2)]
                S = sb_("S", [128, 4, 512], F32)
                Sb = sb_("Sb", [128, 4, 512], BF16)
                nst = sb_("nst", [128, 4], F32)
                nrep = sb_("nrep", [128, 4, 128], BF16)
                mcol = sb_("mcol", [128, 1], F32)
                qT = [sb_(f"qT{i}", [128, 4, TT], BF16) for i in range(2)]
                kT = [sb_(f"kT{i}", [128, 4, TT], BF16) for i in range(2)]
                ktk = [sb_(f"ktk{i}", [64, 8, 512], BF16) for i in range(2)]
                vtk = [sb_(f"vtk{i}", [64, 8, 512], BF16) for i in range(2)]
                hst = [sb_(f"hst{i}", [128, 4, TT], F32) for i in range(2)]
                msk = sb_("msk", [64, 2, 64], F32)
                onesr = sb_("onesr", [128, 64], F32)
                onesb = sb_("onesb", [64, 128], BF16)
                sm = {n: sb_("r_" + n, [128, 64], F32) for n in ("b", "a", "M", "mt", "wi", "em", "dn")}
                c1 = {n: sb_("c_" + n, [128, 1], F32) for n in ("bl", "mn", "wo", "cs", "t")}
                acol = sb_("acol", [64, 1], F32)
                win = sb_("win", [64, 1], F32)
                winb = sb_("winb", [64, 1], BF16)
                wT = sb_("wT", [64, 64], F32)
                sw = sb_("sw", [64, 64], BF16)
                qs = sb_("qs", [128, 4, 64], BF16)
                vs = sb_("vs", [64, 512], BF16)
                cst = sb_("cst", [128, 512], F32)
                P.op("dve", lambda e: e.memset(onesr[:], 1.0), writes=["onesr"])
                P.op("dve", lambda e: e.memset(onesb[:], 1.0), writes=["onesb"])
                P.op("sp", lambda e: e.dma_start(out=msk[:], in_=ml_masks.rearrange("d s t -> s d t")), writes=["msk"], dma=True)

                def rv(ap_, n):
                    return AP(ap_.tensor, ap_.offset + n - 1, [list(ap_.ap[0]), [-1, n]])
                gi = 0
                for (tok0, L, pi) in seqs:
                    nblk = L // 128
                    P.op("sp", lambda e, tok0=tok0, L=L, nblk=nblk: e.dma_start(
                        out=GGs[:, :nblk, :], in_=MG[tok0:tok0 + L, :].rearrange("(b p) j -> p b j", p=128)), writes=["GGs"], dma=True)
                    for d in range(2):
                        rev = (d == 1)
                        for h in range(8):
                            for (jj, dst, dk) in ((d * 16 + h, igr, "igr"), (d * 16 + 8 + h, lfr, "lfr")):
                                for blk in range(nblk):
                                    gp, gpk = grep[blk % 2], ("grep", blk % 2)
                                    src = GGs[:, blk, jj:jj + 1]
                                    bsrc_ = AP(src.tensor, src.offset, [list(src.ap[0]), [0, 128]])
                                    P.op("dve", lambda e, gp=gp, bsrc_=bsrc_: e.tensor_copy(out=gp[:], in_=bsrc_), reads=["GGs"], writes=[gpk])
                                    bk, bkk = banks[1 + (blk // 4) % 2], BK[1 + (blk // 4) % 2]
                                    P.op("pe", lambda e, gp=gp, bk=bk, blk=blk: e.matmul(
                                        bk[:, (blk % 4) * 128:(blk % 4 + 1) * 128], lhsT=gp[:], rhs=ident[:], start=True, stop=True),
                                        reads=[gpk, "ident"], writes=[bkk])
                                    if blk % 4 == 3 or blk == nblk - 1:
                                        b0 = (blk // 4) * 4
                                        w_ = (blk - b0 + 1) * 128
                                        P.op("act", lambda e, bk=bk, dst=dst, b0=b0, w_=w_: e.copy(out=dst[:, b0 * 128:b0 * 128 + w_], in_=bk[:, :w_]),
                                             reads=[bkk], writes=[dk])
                            P.op("act", lambda e, L=L: e.activation(out=lfr[:, :L], in_=lfr[:, :L], func=AF.Exp, scale=-1.0), reads=["lfr"], writes=["lfr"])
                            P.op("act", lambda e, L=L: e.activation(out=lfr[:, :L], in_=lfr[:, :L], func=AF.Ln, bias=1.0), reads=["lfr"], writes=["lfr"])
                            P.op("dve", lambda e, L=L: e.tensor_scalar(out=lfr[:, :L], in0=lfr[:, :L], scalar1=-1.0, scalar2=None, op0=ALU.mult),
                                 reads=["lfr"], writes=["lfr"])
                            if pi < 0:
                                for vc in range(4):
                                    P.op("sp", lambda e, vc=vc, d=d, h=h: e.dma_start(out=cst[:], in_=st_C[d, h, vc * 128:(vc + 1) * 128, :]), writes=["cst"], dma=True)
                                    for kc in range(4):
                                        P.op("pe", lambda e, kc=kc: e.transpose(banks[3][:, kc * 128:(kc + 1) * 128], cst[:, kc * 128:(kc + 1) * 128], ident[:]),
                                             reads=["cst", "ident"], writes=[BK[3]], sig=(kc == 3))
                                    P.op("act", lambda e, vc=vc: e.copy(out=S[:, :, vc * 128:(vc + 1) * 128], in_=banks[3][:].rearrange("p (k v) -> p k v", k=4)),
                                         reads=[BK[3]], writes=["S"])
                                P.op("sp", lambda e, d=d, h=h: e.dma_start(out=nst[:], in_=st_n[d, h].rearrange("(k p) -> p k", p=128),
                                                                           allow_slow_non_contiguous=True), writes=["nst"], dma=True)
                                msrc = AP(st_m.tensor, st_m.offset + d * 8 + h, [[0, 128], [1, 1]])
                                P.op("sp", lambda e, msrc=msrc: e.dma_start(out=mcol[:], in_=msrc, allow_slow_non_contiguous=True), writes=["mcol"], dma=True)
                            else:
                                P.op("dve", lambda e: e.memset(S[:], 0.0), reads=["S"], writes=["S"])
                                P.op("dve", lambda e: e.memset(nst[:], 0.0), reads=["nst"], writes=["nst"])
                                P.op("dve", lambda e: e.memset(mcol[:], 0.0), reads=["mcol"], writes=["mcol"])
                            P.op("act", lambda e: e.copy(out=Sb[:], in_=S[:]), reads=["S"], writes=["Sb"])
                            nsrc = AP(nst[:].tensor, nst[:].offset, [list(nst[:].ap[0]), [1, 4], [0, 128]])
                            P.op("dve", lambda e, nsrc=nsrc: e.tensor_copy(out=nrep[:], in_=nsrc), reads=["nst"], writes=["nrep"])
                            gw = min(TT, L)
                            ng = L // gw
                            for gi_ in (range(ng) if not rev else range(ng - 1, -1, -1)):
                                g0 = tok0 + gi_ * gw
                                bi = gi % 2
                                gi += 1
                                r0, r1 = h * 512, (h + 1) * 512
                                P.op("sp", lambda e, bi=bi, g0=g0, gw=gw, r0=r0, r1=r1: e.dma_start(
                                    out=qT[bi][:, :, :gw], in_=MQ[r0:r1, g0:g0 + gw].rearrange("(k p) t -> p k t", p=128)), writes=[("qT", bi)], dma=True)
                                P.op("sp", lambda e, bi=bi, g0=g0, gw=gw, r0=r0, r1=r1: e.dma_start(
                                    out=kT[bi][:, :, :gw], in_=MK[r0:r1, g0:g0 + gw].rearrange("(k p) t -> p k t", p=128)), writes=[("kT", bi)], dma=True)
                                P.op("sp", lambda e, bi=bi, g0=g0, gw=gw, r0=r0, r1=r1: e.dma_start(
                                    out=ktk[bi][:, :gw // 64, :], in_=MKT[g0:g0 + gw, r0:r1].rearrange("(c p) f -> p c f", p=64)), writes=[("ktk", bi)], dma=True)
                                P.op("sp", lambda e, bi=bi, g0=g0, gw=gw, r0=r0, r1=r1: e.dma_start(
                                    out=vtk[bi][:, :gw // 64, :], in_=MVT[g0:g0 + gw, r0:r1].rearrange("(c p) f -> p c f", p=64)), writes=[("vtk", bi)], dma=True)
                                ncg = gw // 64
                                for c_ in (range(ncg) if not rev else range(ncg - 1, -1, -1)):
                                    o0 = gi_ * gw + c_ * 64
                                    q0 = c_ * 64
                                    V = (lambda ap_: rv(ap_, 64)) if rev else (lambda ap_: ap_)
                                    last = 0 if rev else 63
                                    rk = ["sm"]
                                    def dv(fn, r=(), w=()):
                                        P.op("dve", fn, reads=list(r) + rk, writes=list(w) + rk)
                                    def ac(fn, r=(), w=()):
                                        P.op("act", fn, reads=list(r) + rk, writes=list(w) + rk)
                                    lf_c, ig_c = lfr[:, o0:o0 + 64], igr[:, o0:o0 + 64]
                                    dv(lambda e, lf_c=lf_c, V=V: e.tensor_tensor_scan(out=V(sm["b"][:]), data0=onesr[:], data1=V(lf_c), initial=0.0,
                                                                                      op0=ALU.mult, op1=ALU.add), r=["lfr", "onesr"])
                                    dv(lambda e, ig_c=ig_c: e.tensor_tensor(out=sm["a"][:], in0=ig_c, in1=sm["b"][:], op=ALU.subtract), r=["igr"])
                                    dv(lambda e, V=V: e.tensor_tensor_scan(out=V(sm["M"][:]), data0=V(sm["a"][:]), data1=V(sm["a"][:]), initial=mcol[:],
                                                                           op0=ALU.max, op1=ALU.max), r=["mcol"])
                                    dv(lambda e: e.tensor_tensor(out=sm["mt"][:], in0=sm["b"][:], in1=sm["M"][:], op=ALU.add))
                                    ac(lambda e: e.activation(out=sm["wi"][:], in_=sm["M"][:], func=AF.Exp, scale=-1.0, bias=mcol[:]), r=["mcol"])
                                    ac(lambda e: e.activation(out=sm["em"][:], in_=sm["mt"][:], func=AF.Exp, scale=-1.0))
                                    P.op("pe", lambda e: e.transpose(banks[0][0:64, 128:256], sm["a"][:], ident[:]), reads=rk + ["ident"], writes=[BK[0]])
                                    dv(lambda e: e.tensor_copy(out=acol[:], in_=banks[0][0:64, 128:129]), r=[BK[0]], w=["acol"])
                                    ac(lambda e: e.activation(out=wT[:], in_=sm["M"][0:64, :], func=AF.Exp, scale=-1.0, bias=acol[:]), r=["acol"], w=["wT"])
                                    dv(lambda e, d=d: e.tensor_tensor(out=wT[:], in0=wT[:], in1=msk[:, d, :], op=ALU.mult), r=["msk", "wT"], w=["wT"])
                                    dv(lambda e, last=last: e.tensor_tensor(out=c1["cs"][:], in0=sm["b"][:, last:last + 1], in1=sm["mt"][:, last:last + 1], op=ALU.subtract))
                                    ac(lambda e: e.activation(out=c1["wo"][:], in_=c1["cs"][:], func=AF.Exp, bias=mcol[:]), r=["mcol"])
                                    ac(lambda e: e.activation(out=win[:], in_=acol[:], func=AF.Exp, bias=c1["cs"][0:64, :]), r=["acol"], w=["win"])
                                    dv(lambda e: e.tensor_copy(out=winb[:], in_=win[:]), r=["win"], w=["winb"])
                                    dv(lambda e, last=last: e.tensor_copy(out=mcol[:], in_=sm["mt"][:, last:last + 1]), r=["mcol"], w=["mcol"])
                                    for kc in range(4):
                                        P.op("pe", lambda e, kc=kc, bi=bi, q0=q0: e.matmul(
                                            banks[0][0:64, 0:64], lhsT=kT[bi][:, kc, q0:q0 + 64], rhs=qT[bi][:, kc, q0:q0 + 64], start=(kc == 0), stop=(kc == 3)),
                                            reads=[("kT", bi), ("qT", bi)], writes=[BK[0]], sig=(kc == 3))
                                    dv(lambda e: e.tensor_tensor(out=sw[:], in0=banks[0][0:64, 0:64], in1=wT[:], op=ALU.mult), r=[BK[0], "wT"], w=["sw"])
                                    wib = AP(sm["wi"][:].tensor, sm["wi"][:].offset, [list(sm["wi"][:].ap[0]), [0, 4], [1, 64]])
                                    dv(lambda e, bi=bi, q0=q0, wib=wib: e.tensor_tensor(out=qs[:], in0=qT[bi][:, :, q0:q0 + 64], in1=wib, op=ALU.mult),
                                       r=[("qT", bi)], w=["qs"])
                                    for vc in range(4):
                                        P.op("pe", lambda e, vc=vc, bi=bi, c_=c_: e.matmul(
                                            banks[1][:, vc * 64:(vc + 1) * 64], lhsT=vtk[bi][:, c_, vc * 128:(vc + 1) * 128], rhs=sw[:], start=True, stop=False),
                                            reads=[("vtk", bi), "sw"], writes=[BK[1]], sig=False)
                                        for kc in range(4):
                                            P.op("pe", lambda e, vc=vc, kc=kc: e.matmul(
                                                banks[1][:, vc * 64:(vc + 1) * 64], lhsT=Sb[:, kc, vc * 128:(vc + 1) * 128], rhs=qs[:, kc, :], start=False, stop=(kc == 3)),
                                                reads=["Sb", "qs"], writes=[BK[1]], sig=(kc == 3 and vc == 3))
                                    P.op("pe", lambda e: e.matmul(banks[2][:, 0:64], lhsT=onesb[:], rhs=sw[:], start=True, stop=False),
                                         reads=["onesb", "sw"], writes=[BK[2]], sig=False)
                                    for kc in range(4):
                                        P.op("pe", lambda e, kc=kc: e.matmul(banks[2][:, 0:64], lhsT=nrep[:, kc, :], rhs=qs[:, kc, :], start=False, stop=(kc == 3)),
                                             reads=["nrep", "qs"], writes=[BK[2]], sig=(kc == 3))
                                    ac(lambda e: e.activation(out=sm["dn"][:], in_=banks[2][:, 0:64], func=AF.Abs), r=[BK[2]])
                                    dv(lambda e: e.tensor_tensor(out=sm["dn"][:], in0=sm["dn"][:], in1=sm["em"][:], op=ALU.max))
                                    dv(lambda e: e.reciprocal(out=sm["dn"][:], in_=sm["dn"][:]))
                                    hb = gi_ % 2
                                    dnb = AP(sm["dn"][:].tensor, sm["dn"][:].offset, [list(sm["dn"][:].ap[0]), [0, 4], [1, 64]])
                                    dv(lambda e, hb=hb, q0=q0, dnb=dnb: e.tensor_tensor(
                                        out=hst[hb][:, :, q0:q0 + 64], in0=banks[1][:, 0:256].rearrange("p (v t) -> p v t", v=4), in1=dnb, op=ALU.mult),
                                       r=[BK[1]], w=[("hst", hb)])
                                    ac(lambda e, bi=bi, c_=c_: e.activation(out=vs[:], in_=vtk[bi][:, c_, :], func=AF.Identity, scale=win[:]),
                                       r=[("vtk", bi), "win"], w=["vs"])
                                    for kc in range(4):
                                        P.op("pe", lambda e, kc=kc, bi=bi, c_=c_: e.matmul(
                                            banks[3 + kc][:], lhsT=ktk[bi][:, c_, kc * 128:(kc + 1) * 128], rhs=vs[:], start=True, stop=True),
                                            reads=[("ktk", bi), "vs"], writes=[BK[3 + kc]])
                                        P.op("pe", lambda e, kc=kc, bi=bi, c_=c_: e.matmul(
                                            banks[7][:, kc:kc + 1], lhsT=ktk[bi][:, c_, kc * 128:(kc + 1) * 128], rhs=winb[:], start=True, stop=True),
                                            reads=[("ktk", bi), "winb"], writes=[BK[7]])
                                    for kc in range(4):
                                        dv(lambda e, kc=kc: e.scalar_tensor_tensor(out=S[:, kc, :], in0=S[:, kc, :], scalar=c1["wo"][:], in1=banks[3 + kc][:],
                                                                                   op0=ALU.mult, op1=ALU.add), r=[BK[3 + kc], "S"], w=["S"])
                                    ac(lambda e: e.copy(out=Sb[:], in_=S[:]), r=["S"], w=["Sb"])
                                    dv(lambda e: e.scalar_tensor_tensor(out=nst[:], in0=nst[:], scalar=c1["wo"][:], in1=banks[7][:, 0:4],
                                                                        op0=ALU.mult, op1=ALU.add), r=[BK[7], "nst"], w=["nst"])
                                    dv(lambda e, nsrc=nsrc: e.tensor_copy(out=nrep[:], in_=nsrc), r=["nst"], w=["nrep"])
                                hb = gi_ % 2
                                if d == 0:
                                    P.op("pool", lambda e, hb=hb, g0=g0, gw=gw, r0=r0, r1=r1: e.dma_start(
                                        out=CELL[r0:r1, g0:g0 + gw].rearrange("(v p) t -> p v t", p=128), in_=hst[hb][:, :, :gw]),
                                        reads=[("hst", hb)], writes=[("CELL", h, g0)], dma=True)
                                else:
                                    P.op("pool", lambda e, hb=hb, g0=g0, gw=gw, r0=r0, r1=r1: e.dma_start(
                                        out=CELL[r0:r1, g0:g0 + gw].rearrange("(v p) t -> p v t", p=128), in_=hst[hb][:, :, :gw], accum_op=ALU.add),
                                        reads=[("hst", hb), ("CELL", h, g0)], writes=[("CELL", h, g0)], dma=True)
                            if pi >= 0:
                                for vc in range(4):
                                    for kc in range(4):
                                        P.op("pe", lambda e, vc=vc, kc=kc: e.transpose(banks[3][:, kc * 128:(kc + 1) * 128], S[:, kc, vc * 128:(vc + 1) * 128], ident[:]),
                                             reads=["S", "ident"], writes=[BK[3]], sig=(kc == 3))
                                    P.op("act", lambda e: e.copy(out=cst[:], in_=banks[3][:]), reads=[BK[3]], writes=["cst"])
                                    P.op("pool", lambda e, vc=vc, d=d, h=h, pi=pi: e.dma_start(out=o_C[pi, d, h, vc * 128:(vc + 1) * 128, :], in_=cst[:]),
                                         reads=["cst"], dma=True)
                                P.op("pool", lambda e, d=d, h=h, pi=pi: e.dma_start(out=o_n[pi, d, h].rearrange("(k p) -> p k", p=128), in_=nst[:],
                                                                                    allow_slow_non_contiguous=True), reads=["nst"], dma=True)
                                mdst = AP(o_m.tensor, o_m.offset + (pi * 2 + d) * 8 + h, [[1, 1], [1, 1]])
                                P.op("pool", lambda e, mdst=mdst: e.dma_start(out=mdst, in_=mcol[0:1, :]), reads=["mcol", "sm"], dma=True)
                P.barrier()

        def mixer_ml_p3(B):
            with ExitStack() as ph2:
                sid = scopeid[0]
                gnc = sbuf(ph2, f"mlgn_{sid}", [128, 32], F32)
                skc = sbuf(ph2, f"mlsk_{sid}", [128, 32], F32)
                og = sbuf(ph2, f"mlog_{sid}", [128, 4, TT], BF16)
                xcb = sbuf(ph2, f"mlxc_{sid}", [128, 4, TT], BF16)
                colload(gnc[:], ml_gn, 32)
                colload(skc[:], ml_skip, 32)
                for t in range(NT):
                    for h in range(8):
                        sl = lambda scr: scr[h * 512:(h + 1) * 512, t * TT:(t + 1) * TT].rearrange("(k p) t -> p k t", p=128)
                        P.op("sp", lambda e, sl=sl: e.dma_start(out=B.xt[:, 0:4, :], in_=sl(CELL)), writes=[("xt", k) for k in range(4)], dma=True)
                        P.op("sp", lambda e, sl=sl: e.dma_start(out=og[:], in_=sl(MOG)), writes=["og"], dma=True)
                        P.op("sp", lambda e, sl=sl: e.dma_start(out=xcb[:], in_=sl(MXC)), writes=["xcb"], dma=True)
                        for k in range(4):
                            P.op("dve", lambda e, k=k: e.tensor_tensor(out=B.xt[:, k, :], in0=B.xt[:, k, :], in1=og[:, k, :], op=ALU.mult),
                                 reads=[("xt", k), "og"], writes=[("xt", k)])
                        rms_rstd(B, lambda k: B.xt[:, k, :], lambda k: ("xt", k), 4, 1.0 / 512)
                        for k in range(4):
                            c = h * 4 + k
                            P.op("dve", lambda e, k=k: e.tensor_tensor(out=B.tt[k % 2][:], in0=B.xt[:, k, :], in1=B.rstd[:], op=ALU.mult),
                                 reads=[("xt", k), "rstd"], writes=[("tt", k % 2)])
                            P.op("act", lambda e, k=k, c=c: e.activation(out=B.tt[k % 2][:], in_=B.tt[k % 2][:], func=AF.Identity, scale=gnc[:, c:c + 1]),
                                 reads=[("tt", k % 2), "cols"], writes=[("tt", k % 2)])
                            i = cnt["ev"]; cnt["ev"] += 1
                            P.op("dve", lambda e, k=k, c=c, i=i: e.scalar_tensor_tensor(
                                out=B.obf[i % 3][:], in0=xcb[:, k, :], scalar=skc[:, c:c + 1], in1=B.tt[k % 2][:], op0=ALU.mult, op1=ALU.add),
                                reads=["xcb", "cols", ("tt", k % 2)], writes=[("obf", i % 3)])
                            P.op("pool", lambda e, c=c, i=i, t=t: e.dma_start(out=MYT[c * 128:(c + 1) * 128, t * TT:(t + 1) * TT], in_=B.obf[i % 3][:]),
                                 reads=[("obf", i % 3)], dma=True)
                P.barrier()

        def mixer_out(B, l, scr, nf, wap):
            prep_GS(l, 1, 4, 5, 1.0)
            for t in range(NT):
                cnd = 0 if t < NTS else 1
                P.op("sp", lambda e, t=t: e.dma_start(
                    out=B.hh[:, :nf, :], in_=scr[:, t * TT:(t + 1) * TT].rearrange("(k p) t -> p k t", p=128)),
                    writes=[("hh", f) for f in range(nf)], dma=True)
                down_res(B, t, cnd, lambda f: B.hh[:, f, :], lambda f: ("hh", f), nf, wap)

        PW = {}
        if cfg.get("precast", True):
            with ExitStack() as ph:
                pst = [sbuf(ph, f"pst{i}", [128, 4096], F32) for i in range(3)]
                pwb = [sbuf(ph, f"pwb{i}", [128, 4096], BF16) for i in range(3)]
                pi_ = 0
                for l in range(LAYERS):
                    for f in ("ffn1", "ffn2"):
                        for mi in range(3):
                            pieces = UP_PIECES if mi < 2 else DN_PIECES
                            pw = PreW(f"pw_{f}_{l}_{mi}", pieces)
                            PW[(f, l, mi)] = pw
                            wsrc = fw[f][mi][l]
                            for (k0, nk, c0, ncol) in pieces:
                                i = pi_ % 3
                                pi_ += 1
                                n_ = nk * ncol
                                sv = pst[i][:, :n_].rearrange("p (k c) -> p k c", k=nk)
                                P.op("sp", lambda e, sv=sv, wsrc=wsrc, k0=k0, nk=nk, c0=c0, ncol=ncol: e.dma_start(
                                    out=sv, in_=wsrc[k0 * 128:(k0 + nk) * 128, c0:c0 + ncol].rearrange("(k p) c -> p k c", p=128)),
                                    writes=[("pst", i)], dma=True)
                                if pi_ % 3 == 0:
                                    P.op("pool", lambda e, i=i, n_=n_: e.tensor_copy(out=pwb[i][:, :n_], in_=pst[i][:, :n_]), reads=[("pst", i)], writes=[("pwb", i)])
                                elif pi_ % 3 == 1:
                                    P.op("act", lambda e, i=i, n_=n_: e.copy(out=pwb[i][:, :n_], in_=pst[i][:, :n_]), reads=[("pst", i)], writes=[("pwb", i)])
                                else:
                                    P.op("dve", lambda e, i=i, n_=n_: e.tensor_copy(out=pwb[i][:, :n_], in_=pst[i][:, :n_]), reads=[("pst", i)], writes=[("pwb", i)])
                                P.op("pool", lambda e, i=i, n_=n_, pw=pw, k0=k0, nk=nk, c0=c0, ncol=ncol: e.dma_start(
                                    out=pw.get(k0, nk, c0, ncol), in_=pwb[i][:, :n_]), reads=[("pwb", i)], writes=[pw.key], dma=True)
                P.barrier()

        def FW(f, l, mi):
            return PW[(f, l, mi)] if (f, l, mi) in PW else fw[f][mi][l]

        for l in range(LAYERS):
            kind = MIX[l]
            with ExitStack() as ph:
                B = tok_scope(ph)
                ffn_half(B, l, 0, FW("ffn1", l, 0), FW("ffn1", l, 1), FW("ffn1", l, 2), 0, 1, 2)
                if kind == "rg":
                    mixer_rg_p1(B, l)
                if kind == "s5":
                    mixer_s5_p1(B, l)
                if kind == "da":
                    mixer_da_p1(B, l)
                if kind == "ml":
                    mixer_ml_p1(B, l)
                P.barrier()
            if kind == "rg":
                mixer_rg_p2()
            if kind == "s5":
                mixer_s5_p2()
            if kind == "da" and cfg.get("da_stop", 9) > 1:
                mixer_da_p2()
            if kind == "ml":
                mixer_ml_p1b()
                with ExitStack() as ph:
                    B = tok_scope(ph)
                    mixer_ml_p2a(B)
                    P.barrier()
                if cfg.get("ml_stop", 9) > 1:
                    mixer_ml_p2b()
            with ExitStack() as ph:
                B = tok_scope(ph)
                if kind == "rg":
                    mixer_out(B, l, YT, KC, rg_w_out)
                if kind == "da" and cfg.get("da_stop", 9) > 2:
                    mixer_out(B, l, AT, KC, da_wo)
                if kind == "ml" and cfg.get("ml_stop", 9) > 2:
                    mixer_ml_p3(B)
                    mixer_out(B, l, MYT, 32, ml_w_down)
                if kind == "s5":
                    mixer_s5_p3(B, l)
                    mixer_out(B, l, Z2, 8, s5_w_out)
                ffn_half(B, l, 2, FW("ffn2", l, 0), FW("ffn2", l, 1), FW("ffn2", l, 2), 6, 7, 8)
                P.barrier()

        with ExitStack() as ph:
            B = NS()
            B.xt = sbuf(ph, "xt2", [128, KC, TT], F32)
            B.sq = [sbuf(ph, f"sq2{i}", [128, TT], F32) for i in range(2)]
            B.rstd = sbuf(ph, "rstd2", [128, TT], F32)
            B.tt = [sbuf(ph, f"tt2{i}", [128, TT], F32) for i in range(2)]
            B.epsc = sbuf(ph, "epsc2", [128, 1], F32)
            P.op("dve", lambda e: e.memset(B.epsc[:], EPS), writes=["epsc"])
            yo = [sbuf(ph, f"yo{i}", [128, D], F32) for i in range(2)]
            xf = sbuf(ph, "xf", [128, KC, TT], F32)
            for t in range(NT):
                load_x(B, t)
                rms_rstd(B, lambda k: B.xt[:, k, :], lambda k: ("xt", k), KC, 1.0 / D)
                for k in range(KC):
                    P.op("dve", lambda e, k=k: e.tensor_tensor(out=B.tt[k % 2][:], in0=B.xt[:, k, :], in1=B.rstd[:], op=ALU.mult),
                         reads=[("xt", k), "rstd"], writes=[("tt", k % 2)])
                    P.op("act", lambda e, k=k: e.activation(out=xf[:, k, :], in_=B.tt[k % 2][:], func=AF.Identity,
                                                            scale=fgcol[:, k:k + 1]),
                         reads=[("tt", k % 2), "fgcol"], writes=[("xf", k)])
                for tb in range(4):
                    yb = yo[tb % 2]
                    for q in range(4):
                        bk = banks[q]
                        for kk in range(4):
                            k = q * 4 + kk
                            P.op("pe", lambda e, bk=bk, kk=kk, k=k, tb=tb: e.transpose(
                                bk[:, kk * 128:(kk + 1) * 128], xf[:, k, tb * 128:(tb + 1) * 128], ident[:]),
                                reads=[("xf", k), "ident"], writes=[BK[q]], sig=(kk == 3))
                        if q % 2 == 0:
                            P.op("act", lambda e, bk=bk, q=q, yb=yb: e.copy(out=yb[:, q * 512:(q + 1) * 512], in_=bk[:]),
                                 reads=[BK[q]], writes=[("yo", tb % 2, q)])
                        else:
                            P.op("dve", lambda e, bk=bk, q=q, yb=yb: e.tensor_copy(out=yb[:, q * 512:(q + 1) * 512], in_=bk[:]),
                                 reads=[BK[q]], writes=[("yo", tb % 2, q)])
                    r0 = t * TT + tb * 128
                    P.op("pool", lambda e, yb=yb, r0=r0: e.dma_start(out=y_out[r0:r0 + 128, :], in_=yb[:]),
                         reads=[("yo", tb % 2, q) for q in range(4)], dma=True)
            P.barrier()
        P.finish()
        print("ops", P.nops, "waits", P.nwaits, flush=True)
    return nc


def _f32(a):
    return np.ascontiguousarray(np.asarray(a, dtype=np.float32))


SHARED = ["ada_w", "ada_b", "norm_g", "ffn1_w1", "ffn1_w3", "ffn1_w2", "ffn2_w1", "ffn2_w3", "ffn2_w2", "final_norm_g",
          "s5_w_in", "s5_w_glu", "s5_w_out", "s5_a_re", "s5_a_im", "s5_log_dt", "s5_b_re", "s5_b_im", "s5_c_re", "s5_c_im", "s5_d", "s5_b_glu",
          "da_wq", "da_wk", "da_wv", "da_wo", "da_lam", "da_subln_g",
          "ml_w_up", "ml_w_o", "ml_w_down", "ml_conv_w", "ml_conv_b", "ml_wq", "ml_wk", "ml_wv", "ml_w_gate", "ml_b_gate",
          "ml_b_o", "ml_gn", "ml_skip", "rg_w_in", "rg_w_gate", "rg_w_out", "rg_conv_w", "rg_conv_b", "rg_wa", "rg_wx", "rg_ba", "rg_bx", "rg_lam"]


_ROPE = {}


def _rope_consts():
    if not _ROPE:
        pos = np.arange(4096)
        row, col = (pos // 64).astype(np.float32), (pos % 64).astype(np.float32)
        inv = (10000.0 ** (-np.arange(0, 64, 2, dtype=np.float32) / 64)).astype(np.float32)
        cos = np.zeros((128, 4096), np.float32)
        sin = np.zeros((128, 4096), np.float32)
        perm = np.zeros((128, 128), np.float32)
        for d in range(128):
            p = row if d < 64 else col
            ang = (p * inv[d % 32]).astype(np.float32)
            cos[d] = np.cos(ang)
            first = (d % 64) < 32
            sin[d] = -np.sin(ang) if first else np.sin(ang)
            perm[d + 32 if first else d - 32, d] = 1.0
        _ROPE.update(rope_cos=cos, rope_sin=sin, rope_perm=perm)
    return _ROPE


def make_in_map(inp, shared, c, ns_tok=4096):
    b = c // 2
    m = dict(shared)
    xs, xp = inp["x_sample"], inp["x_prompt"]
    m["x_in"] = np.concatenate([_f32(xs[b][:ns_tok]), _f32(xp[2 * c]), _f32(xp[2 * c + 1])], axis=0)
    m["cvec"] = np.stack([_f32(inp["c"][b]), _f32(inp["c_ctx"])], axis=0)
    m["st_rg"] = _f32(inp["state_rglru"][b])
    m["kcache"] = _f32(inp["cache_dattn_k"][b]).reshape(256, D)
    m["vcache"] = _f32(inp["cache_dattn_v"][b]).reshape(256, D)
    m.update(_rope_consts())
    m["st_C"] = _f32(inp["state_mlstm_C"][b]); m["st_n"] = _f32(inp["state_mlstm_n"][b]); m["st_m"] = _f32(inp["state_mlstm_m"][b])
    tri = np.tril(np.ones((64, 64), np.float32))
    m["ml_masks"] = np.stack([tri.T.copy(), tri.copy()], axis=0)
    m["st_s5"] = _f32(inp["state_s5"][b])
    return m


def kernel(**inp):
    n = 8
    nc = build_nc({})
    shared = {k: _f32(inp[k]) for k in SHARED}
    in_maps = [make_in_map(inp, shared, c) for c in range(n)]
    res = run_bass_kernel_spmd(nc, in_maps, core_ids=list(range(n)))
    R = res.results
    y_sample = np.stack([R[2 * b]["y_out"][:4096] for b in range(4)], axis=0)
    y_prompt = np.concatenate([R[c]["y_out"][4096:].reshape(2, 256, D) for c in range(n)], axis=0)
    z = lambda *s: np.zeros(s, np.float32)
    o_rg = np.concatenate([R[c]["o_rg"] for c in range(n)], axis=0) if "o_rg" in R[0] else z(16, 2, 2048)
    o_s5 = np.concatenate([R[c]["o_s5"] for c in range(n)], axis=0) if "o_s5" in R[0] else z(16, 2, 2, 64, 64)
    if "o_C" in R[0]:
        o_dk = np.concatenate([R[c]["o_dk"].reshape(2, 256, 8, 2, 128) for c in range(n)], axis=0)
        o_dv = np.concatenate([R[c]["o_dv"].reshape(2, 256, 8, 256) for c in range(n)], axis=0)
        o_C = np.concatenate([R[c]["o_C"] for c in range(n)], axis=0)
        o_n = np.concatenate([R[c]["o_n"] for c in range(n)], axis=0)
        o_m = np.concatenate([R[c]["o_m"] for c in range(n)], axis=0)
        return (y_prompt, y_sample, o_s5, o_rg, o_dk, o_dv, o_C, o_n, o_m)
    if "o_dk" in R[0]:
        o_dk = np.concatenate([R[c]["o_dk"].reshape(2, 256, 8, 2, 128) for c in range(n)], axis=0)
        o_dv = np.concatenate([R[c]["o_dv"].reshape(2, 256, 8, 256) for c in range(n)], axis=0)
        return (y_prompt, y_sample, o_s5, o_rg, o_dk, o_dv, z(16, 2, 8, 512, 512), z(16, 2, 8, 512), z(16, 2, 8))
    return (y_prompt, y_sample, o_s5, o_rg, z(16, 256, 8, 2, 128), z(16, 256, 8, 256),
            z(16, 2, 8, 512, 512), z(16, 2, 8, 512), z(16, 2, 8))
```
